# Optimizing a Trainium2 kernel written in Bass

```python
import math
import jax, jax.numpy as jnp
from jax import lax
import numpy as np

D_MODEL = 2048
BATCH = 1
SEQ = 16384
DEPTH = 1

ATTN_WIDTH = D_MODEL // 2
SSM_WIDTH = D_MODEL - ATTN_WIDTH
HEAD_DIM = 128
N_HEADS = ATTN_WIDTH // HEAD_DIM
ROT_DIM = HEAD_DIM // 4
ROPE_THETA = 500000.0
DILATED_PATTERNS = ((128, 1), (512, 4), (2048, 16))
ATT_BLOCK = 128
SSM_GROUP = 16
N_SSM_GROUPS = SSM_WIDTH // SSM_GROUP
SSM_STATE = 64
DT_MIN = 1e-3
DT_MAX = 1e-1
D_FF = -(-8 * D_MODEL // (3 * 256)) * 256
IN_WIDTH = 3 * ATTN_WIDTH + SSM_WIDTH
RMS_EPS = 1e-6

kernel_name = "hymba_dilated_attn_s5_hybrid"


def _rmsnorm(t, g):
    tf = t.astype(jnp.float32)
    tf = tf * lax.rsqrt(jnp.mean(tf * tf, axis=-1, keepdims=True) + RMS_EPS)
    return (tf * g.astype(jnp.float32)).astype(t.dtype)


def _rotary_tables(seq):
    pos = jnp.arange(seq, dtype=jnp.float32)
    inv_freq = ROPE_THETA ** (-jnp.arange(0, ROT_DIM, 2, dtype=jnp.float32) / ROT_DIM)
    ang = pos[:, None] * inv_freq[None, :]
    return jnp.cos(ang)[None, :, None, :], jnp.sin(ang)[None, :, None, :]


def _apply_partial_rope(t, cos, sin):
    rot, rest = t[..., :ROT_DIM], t[..., ROT_DIM:]
    x1, x2 = rot[..., :ROT_DIM // 2], rot[..., ROT_DIM // 2:]
    return jnp.concatenate([x1 * cos - x2 * sin, x2 * cos + x1 * sin, rest], axis=-1)


def _dilated_band_attention(q, k, v, dilation, band):
    b, s, h, hd = q.shape
    L = s // dilation
    nb = -(-L // ATT_BLOCK)
    pad = nb * ATT_BLOCK - L

    def to_lattice(t):
        return t.reshape(b, L, dilation, h, hd)

    ql = jnp.pad(to_lattice(q), ((0, 0), (0, pad), (0, 0), (0, 0), (0, 0)))
    kl = jnp.pad(to_lattice(k), ((0, 0), (ATT_BLOCK, pad), (0, 0), (0, 0), (0, 0)))
    vl = jnp.pad(to_lattice(v), ((0, 0), (ATT_BLOCK, pad), (0, 0), (0, 0), (0, 0)))
    qb = ql.reshape(b, nb, ATT_BLOCK, dilation, h, hd)
    kb = kl.reshape(b, nb + 1, ATT_BLOCK, dilation, h, hd)
    vb = vl.reshape(b, nb + 1, ATT_BLOCK, dilation, h, hd)
    kw = jnp.concatenate([kb[:, :-1], kb[:, 1:]], axis=2)
    vw = jnp.concatenate([vb[:, :-1], vb[:, 1:]], axis=2)

    scores = jnp.einsum('bnqrhd,bnkrhd->bnrhqk', qb, kw)
    qi = jnp.arange(ATT_BLOCK)[:, None]
    kj = jnp.arange(2 * ATT_BLOCK)[None, :]
    dist = qi - kj + ATT_BLOCK
    key_idx = jnp.arange(nb)[:, None, None] * ATT_BLOCK + kj[None] - ATT_BLOCK
    valid = (dist >= 0) & (dist <= band) & (key_idx >= 0)
    scores = jnp.where(valid[None, :, None, None], scores, -jnp.inf)
    m = jnp.max(scores, axis=-1)
    p = jnp.exp(scores - m[..., None])
    l = jnp.sum(p, axis=-1)
    o = jnp.einsum('bnrhqk,bnkrhd->bnqrhd', p, vw)

    def from_lattice(t):
        tail = t.shape[5:]
        t = t.reshape((b, nb * ATT_BLOCK, dilation, h) + tail)[:, :L]
        return t.reshape((b, s, h) + tail)

    m = jnp.moveaxis(m, -1, 2)
    l = jnp.moveaxis(l, -1, 2)
    return from_lattice(o), from_lattice(m), from_lattice(l)


def _dilated_attention_mixer(q, k, v, cos, sin):
    b, s, _ = q.shape
    split = lambda t: t.astype(jnp.float32).reshape(b, s, N_HEADS, HEAD_DIM)
    qh = _apply_partial_rope(split(q), cos, sin) * (HEAD_DIM ** -0.5)
    kh = _apply_partial_rope(split(k), cos, sin)
    vh = split(v)
    parts = [_dilated_band_attention(qh, kh, vh, dil, win // dil) for win, dil in DILATED_PATTERNS]
    m_all = jnp.max(jnp.stack([m for _, m, _ in parts], axis=0), axis=0)
    num = 0.0
    den = 0.0
    for o_i, m_i, l_i in parts:
        w_i = jnp.exp(m_i - m_all)
        num = num + w_i[..., None] * o_i
        den = den + w_i * l_i
    out = num / den[..., None]
    return out.reshape(b, s, ATTN_WIDTH).astype(q.dtype)


def _s5_mixer(u, a_re, a_im, log_dt, b_re, b_im, c_re, c_im, d_skip, w_glu, b_glu):
    bsz, s, _ = u.shape
    f32 = jnp.float32
    uf = u.astype(f32).reshape(bsz, s, N_SSM_GROUPS, SSM_GROUP)
    lam = lax.complex(a_re.astype(f32), a_im.astype(f32))
    dt = jnp.exp(log_dt.astype(f32))[:, None]
    a_bar = jnp.exp(lam * dt)
    b_mat = lax.complex(b_re.astype(f32), b_im.astype(f32))
    b_bar = ((a_bar - 1.0) / lam)[..., None] * b_mat
    bu = jnp.einsum('bsgp,gnp->bsgn', uf.astype(jnp.complex64), b_bar)
    a_seq = jnp.broadcast_to(a_bar, bu.shape)

    def combine(e1, e2):
        a1, x1 = e1
        a2, x2 = e2
        return a1 * a2, a2 * x1 + x2

    _, states = lax.associative_scan(combine, (a_seq, bu), axis=1)
    c_mat = lax.complex(c_re.astype(f32), c_im.astype(f32))
    y = jnp.einsum('bsgn,gpn->bsgp', states, c_mat).real + d_skip.astype(f32) * uf
    y = jax.nn.gelu(y.reshape(bsz, s, SSM_WIDTH))
    gate = jax.nn.sigmoid(y @ w_glu.astype(f32) + b_glu.astype(f32))
    return (y * gate).astype(u.dtype)


def setup_inputs(seed: int = 0) -> dict:
    key = jax.random.key(seed)
    ks = jax.random.split(key, 20)
    f32 = jnp.float32
    nrm = lambda k, shape, scale: jax.random.normal(k, shape, f32) * scale
    x = jax.random.normal(ks[0], (BATCH, SEQ, D_MODEL), f32)
    norm1_g = 1.0 + nrm(ks[1], (DEPTH, D_MODEL), 0.02)
    w_in = nrm(ks[2], (DEPTH, D_MODEL, IN_WIDTH), D_MODEL ** -0.5)
    a_re = -0.5 + nrm(ks[3], (DEPTH, N_SSM_GROUPS, SSM_STATE), 0.01)
    a_im = (math.pi * jnp.arange(SSM_STATE, dtype=f32))[None, None, :] + nrm(ks[4], (DEPTH, N_SSM_GROUPS, SSM_STATE), 0.01)
    log_dt = jax.random.uniform(ks[5], (DEPTH, N_SSM_GROUPS), f32, math.log(DT_MIN), math.log(DT_MAX))
    b_re = nrm(ks[6], (DEPTH, N_SSM_GROUPS, SSM_STATE, SSM_GROUP), (2 * SSM_GROUP) ** -0.5)
    b_im = nrm(ks[7], (DEPTH, N_SSM_GROUPS, SSM_STATE, SSM_GROUP), (2 * SSM_GROUP) ** -0.5)
    c_re = nrm(ks[8], (DEPTH, N_SSM_GROUPS, SSM_GROUP, SSM_STATE), (2 * SSM_STATE) ** -0.5)
    c_im = nrm(ks[9], (DEPTH, N_SSM_GROUPS, SSM_GROUP, SSM_STATE), (2 * SSM_STATE) ** -0.5)
    d_skip = nrm(ks[10], (DEPTH, N_SSM_GROUPS, SSM_GROUP), 1.0)
    w_glu = nrm(ks[11], (DEPTH, SSM_WIDTH, SSM_WIDTH), SSM_WIDTH ** -0.5)
    b_glu = nrm(ks[12], (DEPTH, SSM_WIDTH), 0.02)
    w_out = nrm(ks[13], (DEPTH, ATTN_WIDTH + SSM_WIDTH, D_MODEL), (ATTN_WIDTH + SSM_WIDTH) ** -0.5)
    norm2_g = 1.0 + nrm(ks[14], (DEPTH, D_MODEL), 0.02)
    w_gate = nrm(ks[15], (DEPTH, D_MODEL, D_FF), D_MODEL ** -0.5)
    w_up = nrm(ks[16], (DEPTH, D_MODEL, D_FF), D_MODEL ** -0.5)
    w_down = nrm(ks[17], (DEPTH, D_FF, D_MODEL), D_FF ** -0.5)
    final_g = 1.0 + nrm(ks[18], (D_MODEL,), 0.02)
    return {"x": x, "norm1_g": norm1_g, "w_in": w_in, "a_re": a_re, "a_im": a_im,
            "log_dt": log_dt, "b_re": b_re, "b_im": b_im, "c_re": c_re, "c_im": c_im,
            "d_skip": d_skip, "w_glu": w_glu, "b_glu": b_glu, "w_out": w_out,
            "norm2_g": norm2_g, "w_gate": w_gate, "w_up": w_up, "w_down": w_down,
            "final_g": final_g}


def reference(x, norm1_g, w_in, a_re, a_im, log_dt, b_re, b_im, c_re, c_im, d_skip,
              w_glu, b_glu, w_out, norm2_g, w_gate, w_up, w_down, final_g):
    _, s, _ = x.shape
    cos, sin = _rotary_tables(s)
    h = x
    for layer in range(DEPTH):
        hn = _rmsnorm(h, norm1_g[layer])
        proj = hn @ w_in[layer]
        q, k, v, u = jnp.split(proj, [ATTN_WIDTH, 2 * ATTN_WIDTH, 3 * ATTN_WIDTH], axis=-1)
        attn_out = _dilated_attention_mixer(q, k, v, cos, sin)
        ssm_out = _s5_mixer(u, a_re[layer], a_im[layer], log_dt[layer], b_re[layer],
                            b_im[layer], c_re[layer], c_im[layer], d_skip[layer],
                            w_glu[layer], b_glu[layer])
        h = h + jnp.concatenate([attn_out, ssm_out], axis=-1) @ w_out[layer]
        hn = _rmsnorm(h, norm2_g[layer])
        h = h + (jax.nn.silu(hn @ w_gate[layer]) * (hn @ w_up[layer])) @ w_down[layer]
    return _rmsnorm(h, final_g)
```

```python
import math
import numpy as np
import ml_dtypes
import concourse.bass as bass
import concourse.mybir as mybir
from concourse.bass_utils import run_bass_kernel_spmd

F32 = mybir.dt.float32
BF16 = mybir.dt.bfloat16
ALU = mybir.AluOpType
ACT = mybir.ActivationFunctionType
AX = mybir.AxisListType

NCORES = 8
D = 2048
SEQ = 16384
NTOK = SEQ // NCORES
BLK = 512
NBLK_ALL = SEQ // BLK
KT = D // 128

ENGS = ("sync", "act", "pool", "dve", "pe")


TWO_PI = 2.0 * math.pi
_UID = [0]
SKIP = set()


def _uid(n):
    _UID[0] += 1
    return "%s_%d" % (n, _UID[0])


class Buf:
    __slots__ = ("name", "w", "r", "dsem", "sb", "keep")

    def __init__(self, name, sb=True, keep=False):
        self.name = name
        self.w = None
        self.r = {}
        self.dsem = None
        self.sb = sb
        self.keep = keep


class Prog:
    def __init__(self, nc, stack, ndsem=80):
        self.nc = nc
        self.csem = {k: stack.enter_context(nc.semaphore("s_" + k)) for k in ("c_act", "c_pool", "c_dve", "c_pe")}
        self.dsem = [stack.enter_context(nc.semaphore("sd%d" % i)) for i in range(ndsem)]
        self.dval = [0] * ndsem
        self.dfree = list(range(ndsem))
        self.dkeep = set()
        self.ops = {e: [] for e in ENGS}
        self.cnt = {e: 0 for e in ENGS}
        self.known = {e: {} for e in ENGS}
        self.bufs = []
        self.nblocks = 0

    def _reg(self, b):
        if b not in self.bufs:
            self.bufs.append(b)

    def _deps(self, eng, reads, writes, skip_same_pe=False):
        need = {}

        def add(tok):
            if tok is None:
                return
            k, v = tok
            if skip_same_pe and k == "c_pe":
                return
            if need.get(k, 0) < v:
                need[k] = v
        for b in reads:
            add(b.w)
        for b in writes:
            add(b.w)
            for k, v in b.r.items():
                add((k, v))
        waits = []
        kn = self.known[eng]
        for k, v in need.items():
            if kn.get(k, 0) < v:
                kn[k] = v
                waits.append((k, v))
        return waits

    def _mark(self, tok, reads, writes):
        for b in reads:
            b.r[tok[0]] = tok[1]
            self._reg(b)
        for b in writes:
            b.w = tok
            b.r = {}
            self._reg(b)

    def op(self, eng, fn, reads=(), writes=()):
        waits = self._deps(eng, reads, writes)
        self.cnt[eng] += 1
        tok = ("c_" + eng, self.cnt[eng])
        self.ops[eng].append((waits, fn, (tok[0], 1)))
        self._mark(tok, reads, writes)

    def pe_group(self, fns, reads=(), writes=()):
        waits = self._deps("pe", reads, writes, skip_same_pe=True)
        self.cnt["pe"] += 1
        tok = ("c_pe", self.cnt["pe"])
        n = len(fns)
        for i, fn in enumerate(fns):
            self.ops["pe"].append((waits if i == 0 else [], fn, (tok[0], 1) if i == n - 1 else None))
        self._mark(tok, reads, writes)

    def dma(self, q, fn, dst, src):
        waits = self._deps(q, [src], [dst])
        key = dst if dst.sb else src
        if key.dsem is None:
            key.dsem = self.dfree.pop(0)
            if key.keep:
                self.dkeep.add(key.dsem)
        i = key.dsem
        self.dval[i] += 16
        tok = (i, self.dval[i])
        self.ops[q].append((waits, fn, (i, 16)))
        src.r[tok[0]] = tok[1]
        dst.w = tok
        dst.r = {}
        self._reg(src)
        self._reg(dst)
        self._reg(key)

    def wait_all(self, eng, bufs):
        waits = self._deps(eng, bufs, [])
        self.ops[eng].append((waits, None, None))

    def _sem(self, k):
        return self.csem[k] if isinstance(k, str) else self.dsem[k]

    def flush(self, final_bufs=()):
        kn = self.known["sync"]
        waits = []
        for i, v in enumerate(self.dval):
            if i in self.dkeep or i in self.dfree:
                continue
            if kn.get(i, 0) < v:
                kn[i] = v
                waits.append((i, v))
        for b in final_bufs:
            if b.w is not None and kn.get(b.w[0], 0) < b.w[1]:
                kn[b.w[0]] = b.w[1]
                waits.append(b.w)
        self.ops["sync"].append((waits, None, None))
        nc = self.nc
        self.nblocks += 1
        with nc.Block() as block:
            def run(name):
                def body(e):
                    for waits, fn, inc in self.ops[name]:
                        for k, v in waits:
                            e.wait_ge(self._sem(k), v)
                        if fn is not None:
                            ins = fn(e)
                            if inc is not None:
                                ins.then_inc(self._sem(inc[0]), inc[1])
                return body
            block.sync(run("sync"))
            block.scalar(run("act"))
            block.gpsimd(run("pool"))
            block.vector(run("dve"))
            block.tensor(run("pe"))
        self.ops = {e: [] for e in ENGS}
        for e in ENGS:
            kn = self.known[e]
            for k in ("act", "pool", "dve", "pe"):
                kn["c_" + k] = self.cnt[k]
            for i, v in enumerate(self.dval):
                if i not in self.dkeep:
                    kn[i] = v
        for b in self.bufs:
            if b.w is not None and b.w[0] in self.dkeep:
                pass
            else:
                b.w = None
            b.r = {k: v for k, v in b.r.items() if k in self.dkeep}
            if b.dsem is not None and b.dsem not in self.dkeep:
                self.dfree.append(b.dsem)
                b.dsem = None
        self.bufs = [b for b in self.bufs if b.w is not None or b.r or b.dsem is not None]


def TT(P, eng, out, in0, in1, op, reads, writes):
    P.op(eng, lambda e: e.tensor_tensor(out=out, in0=in0, in1=in1, op=op), reads, writes)


def TS(P, eng, out, in0, s1, s2, op0, op1, reads, writes):
    if s2 is None:
        P.op(eng, lambda e: e.tensor_scalar(out=out, in0=in0, scalar1=s1, scalar2=None, op0=op0), reads, writes)
    else:
        P.op(eng, lambda e: e.tensor_scalar(out=out, in0=in0, scalar1=s1, scalar2=s2, op0=op0, op1=op1), reads, writes)


def STT(P, eng, out, in0, scalar, in1, op0, op1, reads, writes):
    P.op(eng, lambda e: e.scalar_tensor_tensor(out=out, in0=in0, scalar=scalar, in1=in1, op0=op0, op1=op1), reads, writes)


def ACTF(P, out, in_, func, reads, writes, scale=1.0, bias=None):
    if bias is None:
        P.op("act", lambda e: e.activation(out=out, in_=in_, func=func, scale=scale), reads, writes)
    else:
        P.op("act", lambda e: e.activation(out=out, in_=in_, func=func, scale=scale, bias=bias), reads, writes)


def CP(P, eng, out, in_, reads, writes):
    if eng == "act":
        P.op("act", lambda e: e.activation(out=out, in_=in_, func=ACT.Copy), reads, writes)
    else:
        P.op(eng, lambda e: e.tensor_copy(out=out, in_=in_), reads, writes)


def DMA(P, q, out, in_, dst, src, slow=False):
    if slow:
        P.dma(q, lambda e: e.dma_start(out=out, in_=in_, allow_slow_non_contiguous=True), dst, src)
    else:
        P.dma(q, lambda e: e.dma_start(out=out, in_=in_), dst, src)


def cmul(P, eng, o_re, o_im, a_re, a_im, b_re, b_im, t1, t2, reads, writes, tb):
    TT(P, eng, t1, a_re, b_re, ALU.mult, reads, [tb])
    TT(P, eng, t2, a_im, b_im, ALU.mult, reads, [tb])
    TT(P, eng, o_re, t1, t2, ALU.subtract, [tb], writes)
    TT(P, eng, t1, a_re, b_im, ALU.mult, reads, [tb])
    TT(P, eng, t2, a_im, b_re, ALU.mult, reads, [tb])
    TT(P, eng, o_im, t1, t2, ALU.add, [tb], writes)


T0 = 8
UNIT = 1024
NSC = UNIT // T0
I32 = mybir.dt.int32


def s5_stage(nc, P, dr, n_pre_units, n_own_units):
    from contextlib import ExitStack
    B = Buf
    n_units = n_pre_units + n_own_units
    ext = dr["a_re"][1]
    with ExitStack() as st0:
        sbp = lambda n, s, d: st0.enter_context(nc.sbuf_tensor(_uid(n), s, d))
        sc = sbp("sc", [128, 24, 32], F32); b_sc = B("sc")
        pw = sbp("pw", [128, 3, T0 + 1, 32], F32); b_pw = B("pw")
        WBT = sbp("WBT", [128, 8, T0, 2, 128], BF16); b_WBT = B("WBT")
        WCT = sbp("WCT", [128, 8, 2, 128], BF16); b_WCT = B("WCT")
        Rc = sbp("Rc", [128, 32, NSC], F32); Rs = sbp("Rs", [128, 32, NSC], F32); b_R = B("R")
        Dcol = sbp("Dcol", [128, 8], F32); b_D = B("Dcol")
        wglu = sbp("wglu", [128, 8, 1024], BF16); b_wglu = B("wglu")
        bglu = sbp("bglu", [128, 8], F32); b_bglu = B("bglu")
        S = lambda j: sc[:, j, :]
        DT, MAG, PHI, SINP, COSP, ABR, ABI, CR, CI, T1, T2, T3, RHO, C8, S8, NUMR, NUMI, DEN = range(18)
        DMA(P, "sync", Dcol[:], dr["d_skip"][0].rearrange("(k gl) p -> (gl p) k", gl=8), b_D, ext, slow=True)
        DMA(P, "sync", wglu[:], dr["wglu_bf"][0], b_wglu, dr["wglu_bf"][1])
        DMA(P, "sync", bglu[:], dr["b_glu"][0].rearrange("(k p) -> p k", p=128), b_bglu, ext, slow=True)

        with ExitStack() as st:
            sb = lambda n, s, d: st.enter_context(nc.sbuf_tensor(_uid(n), s, d))
            ps = lambda n, s, d: st.enter_context(nc.psum_tensor(_uid(n), s, d))
            are = sb("are", [128, 32], F32); aim = sb("aim", [128, 32], F32); ldt = sb("ldt", [128, 32], F32)
            b_par = B("par")
            DMA(P, "sync", are[:], dr["a_re"][0].rearrange("(pi g) n -> (g n) pi", g=2), b_par, ext, slow=True)
            DMA(P, "sync", aim[:], dr["a_im"][0].rearrange("(pi g) n -> (g n) pi", g=2), b_par, ext, slow=True)
            for g2 in range(2):
                src = dr["log_dt"][0].rearrange("(pi g) -> g pi", g=2)[g2:g2 + 1, :].to_broadcast([64, 32])
                DMA(P, "sync", ldt[64 * g2:64 * g2 + 64, :], src, b_par, ext, slow=True)
            Bre = sb("Bre", [128, 32, 16], F32); Bim = sb("Bim", [128, 32, 16], F32); b_B = B("B")
            DMA(P, "sync", Bre[:], dr["b_re"][0].rearrange("(pi g) n q -> (g n) pi q", g=2), b_B, ext, slow=True)
            DMA(P, "sync", Bim[:], dr["b_im"][0].rearrange("(pi g) n q -> (g n) pi q", g=2), b_B, ext, slow=True)
            id32 = sb("id32", [128, 128], F32); b_id = B("id32")
            DMA(P, "sync", id32[:], dr["ident32"][0], b_id, ext)
            Cx = [sb("Cx%d" % i, [128, 8, 128], F32) for i in range(2)]; b_Cx = B("Cx")
            for i in range(2):
                P.op("pool", lambda e, i=i: e.memset(Cx[i][:], 0.0), [], [b_Cx])
            for i, nm in enumerate(("c_re", "c_im")):
                cv = dr[nm][0].rearrange("(pi8 pi4 g) p n -> pi4 g p pi8 n", pi4=4, g=2)
                for pi4 in range(4):
                    for g2 in range(2):
                        p0 = pi4 * 32 + g2 * 16
                        DMA(P, "sync", Cx[i][p0:p0 + 16, :, 64 * g2:64 * g2 + 64], cv[pi4, g2], b_Cx, ext, slow=True)
            ki = sb("ki", [128, 32], I32); b_ki = B("ki")

            def sin_of(out, ang):
                TS(P, "dve", S(T1), ang, 1.0 / TWO_PI, None, ALU.mult, None, [b_sc], [b_sc])
                CP(P, "dve", ki[:], S(T1), [b_sc], [b_ki])
                CP(P, "dve", S(T1), ki[:], [b_ki], [b_sc])
                STT(P, "dve", S(T2), S(T1), -TWO_PI, ang, ALU.mult, ALU.add, [b_sc], [b_sc])
                TS(P, "dve", S(T3), S(T2), math.pi, -TWO_PI, ALU.is_gt, ALU.mult, [b_sc], [b_sc])
                TT(P, "dve", S(T2), S(T2), S(T3), ALU.add, [b_sc], [b_sc])
                TS(P, "dve", S(T3), S(T2), -math.pi, TWO_PI, ALU.is_lt, ALU.mult, [b_sc], [b_sc])
                TT(P, "dve", S(T2), S(T2), S(T3), ALU.add, [b_sc], [b_sc])
                ACTF(P, out, S(T2), ACT.Sin, [b_sc], [b_sc])

            ACTF(P, S(DT), ldt[:], ACT.Exp, [b_par], [b_sc])
            TT(P, "dve", S(MAG), are[:], S(DT), ALU.mult, [b_par, b_sc], [b_sc])
            ACTF(P, S(RHO), S(MAG), ACT.Exp, [b_sc], [b_sc], scale=float(T0))
            ACTF(P, S(MAG), S(MAG), ACT.Exp, [b_sc], [b_sc])
            TT(P, "dve", S(PHI), aim[:], S(DT), ALU.mult, [b_par, b_sc], [b_sc])
            sin_of(S(SINP), S(PHI))
            TS(P, "dve", S(NUMR), S(PHI), math.pi / 2, None, ALU.add, None, [b_sc], [b_sc])
            sin_of(S(COSP), S(NUMR))
            TT(P, "dve", S(ABR), S(MAG), S(COSP), ALU.mult, [b_sc], [b_sc])
            TT(P, "dve", S(ABI), S(MAG), S(SINP), ALU.mult, [b_sc], [b_sc])
            TS(P, "dve", S(T1), S(ABR), -1.0, None, ALU.add, None, [b_sc], [b_sc])
            TT(P, "dve", S(NUMR), S(T1), are[:], ALU.mult, [b_sc, b_par], [b_sc])
            TT(P, "dve", S(T2), S(ABI), aim[:], ALU.mult, [b_sc, b_par], [b_sc])
            TT(P, "dve", S(NUMR), S(NUMR), S(T2), ALU.add, [b_sc], [b_sc])
            TT(P, "dve", S(NUMI), S(ABI), are[:], ALU.mult, [b_sc, b_par], [b_sc])
            TT(P, "dve", S(T2), S(T1), aim[:], ALU.mult, [b_sc, b_par], [b_sc])
            TT(P, "dve", S(NUMI), S(NUMI), S(T2), ALU.subtract, [b_sc], [b_sc])
            TT(P, "dve", S(DEN), are[:], are[:], ALU.mult, [b_par], [b_sc])
            TT(P, "dve", S(T2), aim[:], aim[:], ALU.mult, [b_par], [b_sc])
            TT(P, "dve", S(DEN), S(DEN), S(T2), ALU.add, [b_sc], [b_sc])
            P.op("dve", lambda e: e.reciprocal(out=S(DEN), in_=S(DEN)), [b_sc], [b_sc])
            TT(P, "dve", S(CR), S(NUMR), S(DEN), ALU.mult, [b_sc], [b_sc])
            TT(P, "dve", S(CI), S(NUMI), S(DEN), ALU.mult, [b_sc], [b_sc])
            CP(P, "dve", S(C8), S(COSP), [b_sc], [b_sc])
            CP(P, "dve", S(S8), S(SINP), [b_sc], [b_sc])
            for _ in range(3):
                TT(P, "dve", S(T1), S(C8), S(C8), ALU.mult, [b_sc], [b_sc])
                TT(P, "dve", S(T2), S(S8), S(S8), ALU.mult, [b_sc], [b_sc])
                TT(P, "dve", S(T3), S(C8), S(S8), ALU.mult, [b_sc], [b_sc])
                TT(P, "dve", S(C8), S(T1), S(T2), ALU.subtract, [b_sc], [b_sc])
                TS(P, "dve", S(S8), S(T3), 2.0, None, ALU.mult, None, [b_sc], [b_sc])
            P.op("dve", lambda e: e.memset(pw[:, 0, 0, :], 1.0), [], [b_pw])
            P.op("dve", lambda e: e.memset(pw[:, 1, 0, :], 0.0), [], [b_pw])
            for k in range(1, T0 + 1):
                pr, pi_ = pw[:, 0, k - 1, :], pw[:, 1, k - 1, :]
                TT(P, "dve", S(T1), pr, S(ABR), ALU.mult, [b_pw, b_sc], [b_sc])
                TT(P, "dve", S(T2), pi_, S(ABI), ALU.mult, [b_pw, b_sc], [b_sc])
                TT(P, "dve", pw[:, 0, k, :], S(T1), S(T2), ALU.subtract, [b_sc], [b_pw])
                TT(P, "dve", S(T1), pr, S(ABI), ALU.mult, [b_pw, b_sc], [b_sc])
                TT(P, "dve", S(T2), pi_, S(ABR), ALU.mult, [b_pw, b_sc], [b_sc])
                TT(P, "dve", pw[:, 1, k, :], S(T1), S(T2), ALU.add, [b_sc], [b_pw])
            TS(P, "dve", pw[:, 2, :, :], pw[:, 1, :, :], -1.0, None, ALU.mult, None, [b_pw], [b_pw])
            Bb = sb("Bb", [128, 2, 32, 16], F32); b_Bb = B("Bb")
            tA = sb("tA", [128, T0, 32, 16], F32); tB = sb("tB", [128, T0, 32, 16], F32); b_t = B("tAB")
            bc16 = lambda j: S(j).unsqueeze(2).to_broadcast([128, 32, 16])
            cmul(P, "dve", Bb[:, 0], Bb[:, 1], bc16(CR), bc16(CI), Bre[:], Bim[:], tA[:, 0], tB[:, 0],
                 [b_sc, b_B], [b_Bb], b_t)
            WBx = sb("WBx", [128, T0, 2, 32, 32], F32); b_WBx = B("WBx")
            P.op("pool", lambda e: e.memset(WBx[:], 0.0), [], [b_WBx])
            for g2 in range(2):
                ps_ = slice(64 * g2, 64 * g2 + 64)
                cs_ = slice(16 * g2, 16 * g2 + 16)
                pwb = lambda ri: pw[ps_, ri, 0:T0, :].unsqueeze(3).to_broadcast([64, T0, 32, 16])
                bbb = lambda ri: Bb[ps_, ri].unsqueeze(1).to_broadcast([64, T0, 32, 16])
                cmul(P, "dve", WBx[ps_, :, 0, :, cs_], WBx[ps_, :, 1, :, cs_], pwb(0), pwb(1), bbb(0), bbb(1),
                     tA[ps_], tB[ps_], [b_pw, b_Bb], [b_WBx], b_t)
            pTr = [ps("pTr%d" % i, [128, 4, 128], F32) for i in range(2)]
            b_pTr = [B("pTr0"), B("pTr1")]
            nt = 0
            for pi8 in range(8):
                for tau in range(T0):
                    s = nt % 2; nt += 1
                    fns = []
                    for ri in range(2):
                        fns.append(lambda e, s=s, ri=ri, pi8=pi8, tau=tau: e.transpose(
                            out=pTr[s][:, ri, :],
                            in_=WBx[:, tau, ri, 4 * pi8:4 * pi8 + 4, :].rearrange("p a b -> p (a b)"),
                            identity=id32[:]))
                    P.pe_group(fns, [b_WBx, b_id], [b_pTr[s]])
                    CP(P, "act" if nt % 2 else "dve", WBT[:, pi8, tau, :, :], pTr[s][:, 0:2, :], [b_pTr[s]], [b_WBT])
            for pi8 in range(0, 8, 2):
                s = nt % 2; nt += 1
                fns = []
                for j in range(2):
                    for ri in range(2):
                        fns.append(lambda e, s=s, ri=ri, j=j, pi8=pi8: e.transpose(
                            out=pTr[s][:, 2 * j + ri, :], in_=Cx[ri][:, pi8 + j, :], identity=id32[:]))
                P.pe_group(fns, [b_Cx, b_id], [b_pTr[s]])
                for j in range(2):
                    CP(P, "dve", WCT[:, pi8 + j, 0, :], pTr[s][:, 2 * j, :], [b_pTr[s]], [b_WCT])
                    TS(P, "dve", WCT[:, pi8 + j, 1, :], pTr[s][:, 2 * j + 1, :], -1.0, None, ALU.mult, None,
                       [b_pTr[s]], [b_WCT])
            CP(P, "dve", Rc[:, :, 0], S(C8), [b_sc], [b_R])
            CP(P, "dve", Rs[:, :, 0], S(S8), [b_sc], [b_R])
            m = 1
            tAv = tA[:].rearrange("p a b c -> p (a b c)")
            tBv = tB[:].rearrange("p a b c -> p (a b c)")
            while m < NSC:
                bc = lambda t: t[:, :, m - 1:m].to_broadcast([128, 32, m])
                t1v = tAv[:, 0:32 * m].rearrange("p (a b) -> p a b", a=32)
                t2v = tBv[:, 0:32 * m].rearrange("p (a b) -> p a b", a=32)
                cmul(P, "dve", Rc[:, :, m:2 * m], Rs[:, :, m:2 * m], Rc[:, :, 0:m], Rs[:, :, 0:m], bc(Rc), bc(Rs),
                     t1v, t2v, [b_R], [b_R], b_t)
                m *= 2
            P.flush()

        with ExitStack() as st:
            sb = lambda n, s, d: st.enter_context(nc.sbuf_tensor(_uid(n), s, d))
            ps = lambda n, s, d: st.enter_context(nc.psum_tensor(_uid(n), s, d))
            NQ = 8
            uT = sb("uT", [128, 8, UNIT], BF16); b_uT = B("uT")
            bA = sb("bA", [128, NQ, NSC], F32); bB = sb("bB", [128, NQ, NSC], F32)
            bC = sb("bC", [128, NQ, NSC], F32); bD = sb("bD", [128, NQ, NSC], F32)
            bE = sb("bE", [128, NQ, NSC + 1], F32); bF = sb("bF", [128, NQ, NSC + 1], F32)
            b_A, b_Bq, b_C, b_Dq, b_E, b_F = B("bA"), B("bB"), B("bC"), B("bD"), B("bE"), B("bF")
            cr = sb("cr", [128, 32, 1], F32); ci = sb("ci", [128, 32, 1], F32); b_c = B("carry")
            P.op("dve", lambda e: e.memset(cr[:], 0.0), [], [b_c])
            P.op("dve", lambda e: e.memset(ci[:], 0.0), [], [b_c])
            psS = [ps("psS%d" % i, [128, 4, NSC], F32) for i in range(2)]; b_psS = B("psS")
            psH = [ps("psH%d" % i, [128, T0, NSC], F32) for i in range(2)]; b_psH = B("psH")
            psY = ps("psY", [128, UNIT], F32); b_psY = B("psY")
            hA = sb("hA", [128, T0, NSC], F32); hB = sb("hB", [128, T0, NSC], F32)
            hC = sb("hC", [128, T0, NSC], F32); b_h = B("hABC")
            Hre = [sb("Hre%d" % i, [128, UNIT], BF16) for i in range(2)]
            Him = [sb("Him%d" % i, [128, UNIT], BF16) for i in range(2)]
            b_H = [B("H0"), B("H1")]
            ysb = sb("ysb", [128, UNIT], F32); b_y = B("ysb")
            g1 = sb("g1", [128, UNIT], F32); g2t = sb("g2t", [128, UNIT], F32); b_g = B("g12")
            ygT = sb("ygT", [128, 8, UNIT], BF16); b_yg = B("ygT")
            gate = sb("gate", [128, 512], F32); b_gate = B("gate")
            sso = [sb("sso%d" % i, [128, UNIT], BF16) for i in range(2)]; b_sso = [B("sso0"), B("sso1")]
            uT_d, b_uTd = dr["uT_d"]
            ssm_d, b_ssmd = dr["ssm_d"]

            for u in range(n_units):
                own = u >= n_pre_units
                ou = u - n_pre_units
                DMA(P, "sync", uT[:], uT_d[:, :, u * UNIT:(u + 1) * UNIT], b_uT, b_uTd)
                u3 = uT[:].rearrange("p k (c j) -> p k j c", j=T0)
                for q4 in range(32 // NQ):
                    psl = slice(NQ * q4, NQ * q4 + NQ)
                    for h8 in range(NQ // 4):
                        pi8 = (NQ // 4) * q4 + h8
                        fns = []
                        for pi4 in range(4):
                            r0 = 32 * pi4
                            for ri in range(2):
                                for j in range(T0):
                                    fns.append(lambda e, r0=r0, ri=ri, j=j, pi8=pi8, pi4=pi4, u3=u3: e.matmul(
                                        psS[ri][:, pi4, :], lhsT=WBT[r0:r0 + 32, pi8, T0 - 1 - j, ri, :],
                                        rhs=u3[r0:r0 + 32, pi8, j, :], start=(j == 0), stop=(j == T0 - 1),
                                        tile_position=(r0, 0)))
                        P.pe_group(fns, [b_WBT, b_uT], [b_psS])
                        CP(P, "act", bA[:, 4 * h8:4 * h8 + 4, :], psS[0][:], [b_psS], [b_A])
                        CP(P, "act", bB[:, 4 * h8:4 * h8 + 4, :], psS[1][:], [b_psS], [b_Bq])
                    rc, rs = Rc[:, psl, :], Rs[:, psl, :]
                    E1, F1 = bE[:, :, 1:], bF[:, :, 1:]
                    TT(P, "dve", bC[:], bA[:], rc, ALU.mult, [b_A, b_R], [b_C])
                    TT(P, "pool", E1, bB[:], rs, ALU.mult, [b_Bq, b_R], [b_E])
                    TT(P, "dve", bC[:], bC[:], E1, ALU.add, [b_C, b_E], [b_C])
                    TT(P, "dve", bD[:], bB[:], rc, ALU.mult, [b_Bq, b_R], [b_Dq])
                    TT(P, "pool", F1, bA[:], rs, ALU.mult, [b_A, b_R], [b_F])
                    TT(P, "dve", bD[:], bD[:], F1, ALU.subtract, [b_Dq, b_F], [b_Dq])
                    for p_ in range(NQ):
                        pi = NQ * q4 + p_
                        P.op("dve", lambda e, pi=pi, p_=p_: e.tensor_tensor_scan(
                            out=bA[:, p_, :], data0=sc[:, RHO, pi:pi + 1].to_broadcast([128, NSC]), data1=bC[:, p_, :],
                            initial=cr[:, pi, :], op0=ALU.mult, op1=ALU.add), [b_C, b_sc, b_c], [b_A])
                        P.op("dve", lambda e, pi=pi, p_=p_: e.tensor_tensor_scan(
                            out=bB[:, p_, :], data0=sc[:, RHO, pi:pi + 1].to_broadcast([128, NSC]), data1=bD[:, p_, :],
                            initial=ci[:, pi, :], op0=ALU.mult, op1=ALU.add), [b_Dq, b_sc, b_c], [b_Bq])
                    co = slice(0, NSC) if own else slice(NSC - 1, NSC)
                    rco, rso = rc[:, :, co], rs[:, :, co]
                    TT(P, "dve", E1[:, :, co], bA[:, :, co], rco, ALU.mult, [b_A, b_R], [b_E])
                    TT(P, "pool", bC[:, :, co], bB[:, :, co], rso, ALU.mult, [b_Bq, b_R], [b_C])
                    TT(P, "dve", E1[:, :, co], E1[:, :, co], bC[:, :, co], ALU.subtract, [b_E, b_C], [b_E])
                    TT(P, "dve", F1[:, :, co], bB[:, :, co], rco, ALU.mult, [b_Bq, b_R], [b_F])
                    TT(P, "pool", bD[:, :, co], bA[:, :, co], rso, ALU.mult, [b_A, b_R], [b_Dq])
                    TT(P, "dve", F1[:, :, co], F1[:, :, co], bD[:, :, co], ALU.add, [b_F, b_Dq], [b_F])
                    if own:
                        CP(P, "dve", bE[:, :, 0:1], cr[:, psl, :], [b_c], [b_E])
                        CP(P, "dve", bF[:, :, 0:1], ci[:, psl, :], [b_c], [b_F])
                    CP(P, "dve", cr[:, psl, :], bE[:, :, NSC:NSC + 1], [b_E], [b_c])
                    CP(P, "dve", ci[:, psl, :], bF[:, :, NSC:NSC + 1], [b_F], [b_c])
                    if not own:
                        continue
                    for h8 in range(NQ // 4):
                        pi8 = (NQ // 4) * q4 + h8
                        for pi4 in range(4):
                            pi = 4 * pi8 + pi4
                            p_ = 4 * h8 + pi4
                            r0 = 32 * pi4
                            hs = pi % 2
                            fns = []
                            for ri in range(2):
                                for i in range(T0):
                                    for tau in range(i + 1):
                                        fns.append(lambda e, r0=r0, ri=ri, i=i, tau=tau, pi8=pi8, u3=u3: e.matmul(
                                            psH[ri][:, i, :], lhsT=WBT[r0:r0 + 32, pi8, tau, ri, :],
                                            rhs=u3[r0:r0 + 32, pi8, i - tau, :], start=(tau == 0), stop=(tau == i),
                                            tile_position=(r0, 0)))
                            P.pe_group(fns, [b_WBT, b_uT], [b_psH])
                            xr_b = bE[:, p_, 0:NSC].unsqueeze(1).to_broadcast([128, T0, NSC])
                            xi_b = bF[:, p_, 0:NSC].unsqueeze(1).to_broadcast([128, T0, NSC])
                            pwr_b = pw[:, 0, 1:T0 + 1, pi].unsqueeze(2).to_broadcast([128, T0, NSC])
                            pwi_b = pw[:, 1, 1:T0 + 1, pi].unsqueeze(2).to_broadcast([128, T0, NSC])
                            hre_v = Hre[hs][:].rearrange("p (c i) -> p i c", i=T0)
                            him_v = Him[hs][:].rearrange("p (c i) -> p i c", i=T0)
                            TT(P, "pool", hA[:], xr_b, pwr_b, ALU.mult, [b_E, b_pw], [b_h])
                            TT(P, "pool", hB[:], xi_b, pwi_b, ALU.mult, [b_F, b_pw], [b_h])
                            TT(P, "pool", hA[:], hA[:], hB[:], ALU.subtract, [b_h], [b_h])
                            TT(P, "dve", hre_v, psH[0][:], hA[:], ALU.add, [b_psH, b_h], [b_H[hs]])
                            TT(P, "pool", hC[:], xi_b, pwr_b, ALU.mult, [b_F, b_pw], [b_h])
                            TT(P, "pool", hB[:], xr_b, pwi_b, ALU.mult, [b_E, b_pw], [b_h])
                            TT(P, "pool", hC[:], hC[:], hB[:], ALU.add, [b_h], [b_h])
                            TT(P, "dve", him_v, psH[1][:], hC[:], ALU.add, [b_psH, b_h], [b_H[hs]])
                            fns = []
                            for half in range(2):
                                hsl = slice(512 * half, 512 * half + 512)
                                fns.append(lambda e, hs=hs, hsl=hsl, r0=r0, pi8=pi8: e.matmul(
                                    psY[r0:r0 + 32, hsl], lhsT=WCT[:, pi8, 0, r0:r0 + 32], rhs=Hre[hs][:, hsl],
                                    start=True, stop=False, tile_position=(0, r0)))
                                fns.append(lambda e, hs=hs, hsl=hsl, r0=r0, pi8=pi8: e.matmul(
                                    psY[r0:r0 + 32, hsl], lhsT=WCT[:, pi8, 1, r0:r0 + 32], rhs=Him[hs][:, hsl],
                                    start=False, stop=True, tile_position=(0, r0)))
                            P.pe_group(fns, [b_WCT, b_H[hs]], [b_psY])
                        STT(P, "dve", ysb[:], uT[:, pi8, :], Dcol[:, pi8:pi8 + 1], psY[:], ALU.mult, ALU.add,
                            [b_uT, b_D, b_psY], [b_y])
                        ACTF(P, g1[:], ysb[:], ACT.Square, [b_y], [b_g])
                        TS(P, "pool", g1[:], g1[:], 0.044715, 1.0, ALU.mult, ALU.add, [b_g], [b_g])
                        TT(P, "pool", g1[:], g1[:], ysb[:], ALU.mult, [b_g, b_y], [b_g])
                        ACTF(P, g2t[:], g1[:], ACT.Sigmoid, [b_g], [b_g], scale=1.5957691216057308)
                        TT(P, "pool", ygT[:, pi8, :], g2t[:], ysb[:], ALU.mult, [b_g, b_y], [b_yg])
                if not own:
                    continue
                for mt in range(8):
                    so = mt % 2
                    for half in range(2):
                        hsl = slice(512 * half, 512 * half + 512)
                        fns = [lambda e, kt=kt, mt=mt, hsl=hsl: e.matmul(
                            psY[:, hsl], lhsT=wglu[:, kt, 128 * mt:128 * mt + 128], rhs=ygT[:, kt, hsl],
                            start=(kt == 0), stop=(kt == 7)) for kt in range(8)]
                        P.pe_group(fns, [b_wglu, b_yg], [b_psY])
                        ACTF(P, gate[:], psY[:, hsl], ACT.Sigmoid, [b_psY, b_bglu], [b_gate], bias=bglu[:, mt:mt + 1])
                        TT(P, "dve", sso[so][:, hsl], gate[:], ygT[:, mt, hsl], ALU.mult, [b_gate, b_yg], [b_sso[so]])
                    DMA(P, "act", ssm_d[:, mt, ou * UNIT:(ou + 1) * UNIT], sso[so][:], b_ssmd, b_sso[so])
            P.flush()


def cast_weights(nc, P, pairs):
    for dst, db, src, sbuf_ in pairs:
        rows, cols = dst.shape
        step = max(1, (1 << 20) // cols)
        for r in range(0, rows, step):
            r1 = min(rows, r + step)
            DMA(P, "pool", dst[r:r1, :], src[r:r1, :], db, sbuf_)


NCT = 30
CT_Q, CT_QSW, CT_K, CT_KSW, CT_U = 0, 8, 11, 19, 22


def norm_transpose(nc, P, T, x_rows_ap, b_src, hn_out3, b_hn, gain, b_gain, s):
    xt, junk, ss, rstd, xn, pT, idt, eps_t = T["xt"], T["junk"], T["ss"], T["rstd"], T["xn"], T["pT"], T["idt"], T["eps"]
    s2 = s % 2
    bx, bj, bs, br, bxn, bpT = T["b_xt"][s], T["b_junk"], T["b_ss"][s], T["b_rstd"][s], T["b_xn"][s2], T["b_pT"][s2]
    if x_rows_ap is not None:
        DMA(P, "sync", xt[s][:], x_rows_ap, bx, b_src)
    P.op("act", lambda e: e.activation(out=junk[:], in_=xt[s][:], func=ACT.Square, accum_out=ss[:, s:s + 1]),
         [bx], [bj, bs])
    P.op("act", lambda e: e.activation(out=rstd[:, s:s + 1], in_=ss[:, s:s + 1], func=ACT.Sqrt, scale=1.0 / D,
                                       bias=eps_t[:]), [bs, T["b_eps"]], [br])
    P.op("dve", lambda e: e.reciprocal(out=rstd[:, s:s + 1], in_=rstd[:, s:s + 1]), [br], [br])
    P.op("act", lambda e: e.activation(out=xn[s2][:], in_=xt[s][:], func=ACT.Copy, scale=rstd[:, s:s + 1]),
         [bx, br], [bxn])
    P.pe_group([(lambda e, k=k: e.transpose(out=pT[s2][:, k * 128:(k + 1) * 128], in_=xn[s2][:, k * 128:(k + 1) * 128],
                                            identity=idt[:])) for k in range(KT)], [bxn, T["b_idt"]], [bpT])
    TT(P, "dve", hn_out3, pT[s2][:].rearrange("p (k c) -> p k c", k=KT),
       gain[:].unsqueeze(2).to_broadcast([128, KT, 128]), ALU.mult, [bpT, b_gain], [b_hn])


def norm_tiles(nc, sb, ps, B, nx=2):
    T = {}
    T["xt"] = [sb("xt%d" % i, [128, D], F32) for i in range(nx)]
    T["junk"] = sb("junk", [128, D], BF16)
    T["ss"] = sb("ss", [128, nx], F32); T["rstd"] = sb("rstd", [128, nx], F32)
    T["xn"] = [sb("xn%d" % i, [128, D], BF16) for i in range(2)]
    T["pT"] = [ps("pT%d" % i, [128, D], BF16) for i in range(2)]
    T["idt"] = sb("idt", [128, 128], BF16); T["eps"] = sb("eps_t", [128, 1], F32)
    T["b_xt"] = [B("xt%d" % i) for i in range(nx)]; T["b_junk"] = B("junk"); T["b_ss"] = [B("ss%d" % i) for i in range(nx)]
    T["b_rstd"] = [B("r%d" % i) for i in range(nx)]; T["b_xn"] = [B("xn0"), B("xn1")]; T["b_pT"] = [B("pT0"), B("pT1")]
    T["b_idt"] = B("idt"); T["b_eps"] = B("eps")
    return T


def front_stage(nc, P, dr, n_blocks, first_kv, first_q):
    from contextlib import ExitStack
    B = Buf
    ext = dr["ext"]
    with ExitStack() as st:
        sb = lambda n, s, d: st.enter_context(nc.sbuf_tensor(_uid(n), s, d))
        ps = lambda n, s, d: st.enter_context(nc.psum_tensor(_uid(n), s, d))
        T = norm_tiles(nc, sb, ps, B, nx=4)
        DMA(P, "sync", T["idt"][:], dr["ident_bf"][0], T["b_idt"], ext)
        P.op("dve", lambda e: e.memset(T["eps"][:], 1e-6), [], [T["b_eps"]])
        g1t = sb("g1t", [128, KT], F32); b_g1 = B("g1t")
        DMA(P, "sync", g1t[:], dr["g1"][0], b_g1, ext)
        hnT = [sb("hnT%d" % i, [128, KT, BLK], BF16) for i in range(2)]; b_hn = [B("hn0"), B("hn1")]
        wu = sb("wu", [128, 8, KT, 128], BF16); b_wu = B("wu")
        wfm_bf, b_wfm = dr["wfm_bf"]
        DMA(P, "sync", wu[:], wfm_bf[CT_U:CT_U + 8].rearrange("c p k m -> p c k m"), b_wu, b_wfm)
        ublk = [sb("ublk%d" % i, [128, 8, BLK], BF16) for i in range(2)]; b_ub = [B("ub0"), B("ub1")]
        NWS = 2
        wst = [sb("wst%d" % i, [128, 4 * KT * 128], BF16) for i in range(NWS)]; b_wst = [B("wst%d" % i) for i in range(NWS)]
        pP = [ps("pP%d" % i, [128, BLK], F32) for i in range(3)]; b_pP = [B("pP%d" % i) for i in range(3)]
        cosb = sb("cosb", [128, 3, BLK], F32); sinb = sb("sinb", [128, 3, BLK], F32); b_cs = B("cossin")
        swsin = sb("swsin", [128, 3, BLK], F32); b_sw = B("swsin")
        rtmp = sb("rtmp", [128, BLK], F32); b_rt = B("rtmp")
        rtmp2 = sb("rtmp2", [128, BLK], F32); b_rt2 = B("rtmp2")
        qkblk = [sb("qkblk%d" % i, [128, 8, BLK], BF16) for i in range(2)]; b_qk = [B("qk0"), B("qk1")]
        vrow = sb("vrow", [128, 4, 1024], BF16); b_vr = B("vrow")
        xpad, b_x = dr["xpad"]
        uT_d, b_uTd = dr["uT_d"]
        wv_bf, b_wv = dr["wv_bf"]
        st_ = {"np": 0, "nw": 0, "nqk": 0}

        def proj_tile(lhs_w, hn, b_w, b_h):
            s = st_["np"] % 3; st_["np"] += 1
            P.pe_group([(lambda e, k=k, s=s: e.matmul(pP[s][:], lhsT=lhs_w[:, k, :], rhs=hn[:, k, :],
                                                      start=(k == 0), stop=(k == KT - 1))) for k in range(KT)],
                       [b_w, b_h], [b_pP[s]])
            return s

        def load_chunk(ct0, n):
            s = st_["nw"] % NWS; st_["nw"] += 1
            v4 = wst[s][:, 0:n * KT * 128].rearrange("p (c k m) -> p c k m", c=n, k=KT)
            DMA(P, "sync", v4, wfm_bf[ct0:ct0 + n].rearrange("c p k m -> p c k m"), b_wst[s], b_wfm)
            return s, v4

        def qk_proj(hs, ct_main, ct_sw, out_d, b_outd, col0):
            hn = hnT[hs]
            s, v4 = load_chunk(ct_sw, 3)
            for j in range(3):
                sp = proj_tile(v4[:, j], hn[:], b_wst[s], b_hn[hs])
                CP(P, "act", swsin[:, j, :], pP[sp][:], [b_pP[sp]], [b_sw])
            qs = st_["nqk"] % 2; st_["nqk"] += 1
            for c4 in range(2):
                s, v4 = load_chunk(ct_main + 4 * c4, 4)
                for j in range(4):
                    h = 4 * c4 + j
                    jb = h % 3
                    sp = proj_tile(v4[:, j], hn[:], b_wst[s], b_hn[hs])
                    TT(P, "dve", rtmp[:], pP[sp][:], cosb[:, jb, :], ALU.mult, [b_pP[sp], b_cs], [b_rt])
                    TT(P, "pool", rtmp2[:], swsin[:, h // 3, :], sinb[:, jb, :], ALU.mult, [b_sw, b_cs], [b_rt2])
                    TT(P, "pool", qkblk[qs][:, h, :], rtmp[:], rtmp2[:], ALU.add, [b_rt, b_rt2], [b_qk[qs]])
            DMA(P, "pool", out_d[:, :, col0:col0 + BLK], qkblk[qs][:], b_outd, b_qk[qs])

        nt = 0
        for b in range(n_blocks):
            hs = b % 2
            for t in range(4):
                s = nt % 4; nt += 1
                row = b * BLK + t * 128
                norm_transpose(nc, P, T, xpad[row:row + 128, :], b_x, hnT[hs][:, :, t * 128:(t + 1) * 128], b_hn[hs],
                               g1t, b_g1, s)
            us = b % 2
            for ct in range(8):
                sp = proj_tile(wu[:, ct], hnT[hs][:], b_wu, b_hn[hs])
                CP(P, "act", ublk[us][:, ct, :], pP[sp][:], [b_pP[sp]], [b_ub[us]])
            DMA(P, "pool", uT_d[:, :, b * BLK:(b + 1) * BLK], ublk[us][:], b_uTd, b_ub[us])
            if b < first_kv:
                continue
            wb = b - first_kv
            DMA(P, "sync", cosb[:], dr["cos_d"][0][:, :, wb * BLK:(wb + 1) * BLK], b_cs, ext)
            DMA(P, "sync", sinb[:], dr["sin_d"][0][:, :, wb * BLK:(wb + 1) * BLK], b_cs, ext)
            if b >= first_q:
                qk_proj(hs, CT_Q, CT_QSW, dr["qT_d"][0], dr["qT_d"][1], (b - first_q) * BLK)
            qk_proj(hs, CT_K, CT_KSW, dr["kT_d"][0], dr["kT_d"][1], wb * BLK)
            for half in range(2):
                if "v" in SKIP:
                    break
                s = st_["nw"] % NWS; st_["nw"] += 1
                vv = wst[s][:, 0:KT * 512].rearrange("p (k n) -> p k n", k=KT)
                DMA(P, "sync", vv, wv_bf[:, :, half * 512:(half + 1) * 512], b_wst[s], b_wv)
                for t in range(4):
                    sp = st_["np"] % 3; st_["np"] += 1
                    P.pe_group([(lambda e, k=k, sp=sp, t=t, vv=vv, hs=hs: e.matmul(
                        pP[sp][:], lhsT=hnT[hs][:, k, t * 128:(t + 1) * 128], rhs=vv[:, k, :],
                        start=(k == 0), stop=(k == KT - 1))) for k in range(KT)], [b_wst[s], b_hn[hs]], [b_pP[sp]])
                    CP(P, "act", vrow[:, t, half * 512:(half + 1) * 512], pP[sp][:], [b_pP[sp]], [b_vr])
            if "v" not in SKIP:
                DMA(P, "pool", dr["V_d"][0][wb * BLK:(wb + 1) * BLK, :].rearrange("(t p) c -> p t c", p=128), vrow[:],
                    dr["V_d"][1], b_vr)
        P.flush()


def attn_stage(nc, P, dr):
    from contextlib import ExitStack
    B = Buf
    ext = dr["ext"]
    W = 2 * NTOK
    SC = 128 ** -0.5
    with ExitStack() as st:
        sb = lambda n, s, d: st.enter_context(nc.sbuf_tensor(_uid(n), s, d))
        ps = lambda n, s, d: st.enter_context(nc.psum_tensor(_uid(n), s, d))
        cs = sb("cs", [128, 4, 128], BF16); b_c = B("cs")
        DMA(P, "sync", cs[:], dr["aconsts"][0], b_c, ext)
        hm = sb("hm", [128, 1], F32); b_hm = B("hm")
        DMA(P, "sync", hm[:], dr["hmask"][0], b_hm, ext)
        qT = [sb("qTh%d" % i, [128, NTOK], BF16) for i in range(2)]; b_q = [B("q0"), B("q1")]
        kT = [sb("kTh%d" % i, [128, W], BF16) for i in range(2)]; b_k = [B("k0"), B("k1")]
        num = sb("num", [128, NTOK], F32); den = sb("den", [128, NTOK], F32); b_num, b_den = B("num"), B("den")
        outb = [sb("outb%d" % i, [128, NTOK], BF16) for i in range(2)]; b_ob = [B("ob0"), B("ob1")]
        NV = 6
        vt = [sb("vt%d" % i, [128, 128], BF16) for i in range(NV)]; b_vt = [B("vt%d" % i) for i in range(NV)]
        pt = [sb("pt%d" % i, [128, 128], BF16) for i in range(4)]; b_pt = [B("pt%d" % i) for i in range(4)]
        pS = [ps("pS%d" % i, [128, 128], F32) for i in range(4)]; b_pS = [B("pS%d" % i) for i in range(4)]
        pO = [ps("pO%d" % i, [128, 128], F32) for i in range(2)]; b_pO = [B("pO0"), B("pO1")]
        pL = [ps("pL%d" % i, [128, 128], F32) for i in range(2)]; b_pL = [B("pL0"), B("pL1")]
        qT_d, b_qd = dr["qT_d"]; kT_d, b_kd = dr["kT_d"]; V_d, b_vd = dr["V_d"]; mix_d, b_mix = dr["mix_d"]
        iv = 0; ip = 0; blk = 0
        for h in range(8):
            hs = h % 2
            DMA(P, "sync", qT[hs][:], qT_d[:, h, :], b_q[hs], b_qd)
            DMA(P, "sync", kT[hs][:], kT_d[:, h, :], b_k[hs], b_kd)
            for pi_, dil in enumerate((1, 4, 16)):
                nb_all = W // dil // 128
                nb0 = nb_all // 2
                for n in range(nb0, nb_all):
                    for r in range(dil):
                        def wpos(nn):
                            s0 = nn * 128 * dil + r
                            return slice(s0, s0 + 127 * dil + 1, dil)
                        pq_w = wpos(n)
                        pq = slice(pq_w.start - NTOK, pq_w.stop - NTOK, dil)
                        so = blk % 2; blk += 1
                        slots = []
                        for (kn, mi) in ((n, 1), (n - 1, 2)):
                            pk = wpos(kn)
                            halo = kn < nb0
                            sv = iv % NV; iv += 1
                            sp = ip % 4; ip += 1
                            slots.append((sv, sp))
                            DMA(P, "sync", vt[sv][:], V_d[pk, 128 * h:128 * h + 128], b_vt[sv], b_vd)
                            P.pe_group([lambda e, sp=sp, pk=pk, pq=pq, hs=hs: e.matmul(
                                            pS[sp][:], lhsT=kT[hs][:, pk], rhs=qT[hs][:, pq], start=True, stop=False),
                                        lambda e, sp=sp, mi=mi: e.matmul(
                                            pS[sp][:], lhsT=cs[:, 0, :], rhs=cs[:, mi, :], start=False, stop=True)],
                                       [b_k[hs], b_q[hs], b_c], [b_pS[sp]])
                            if halo:
                                ACTF(P, pt[sp][:], pS[sp][:], ACT.Exp, [b_pS[sp], b_hm], [b_pt[sp]], scale=SC, bias=hm[:])
                            else:
                                ACTF(P, pt[sp][:], pS[sp][:], ACT.Exp, [b_pS[sp]], [b_pt[sp]], scale=SC)
                        P.pe_group([lambda e, sv=sv, sp=sp, i=i, so=so: e.matmul(
                            pO[so][:], lhsT=vt[sv][:], rhs=pt[sp][:], start=(i == 0), stop=(i == 1))
                            for i, (sv, sp) in enumerate(slots)],
                            [b_vt[sv] for sv, _ in slots] + [b_pt[sp] for _, sp in slots], [b_pO[so]])
                        P.pe_group([lambda e, sp=sp, i=i, so=so: e.matmul(
                            pL[so][:], lhsT=cs[:, 3, :], rhs=pt[sp][:], start=(i == 0), stop=(i == 1))
                            for i, (_, sp) in enumerate(slots)], [b_c] + [b_pt[sp] for _, sp in slots], [b_pL[so]])
                        if pi_ == 0:
                            CP(P, "dve", num[:, pq], pO[so][:], [b_pO[so]], [b_num])
                            CP(P, "act", den[:, pq], pL[so][:], [b_pL[so]], [b_den])
                        else:
                            TT(P, "dve", num[:, pq], pO[so][:], num[:, pq], ALU.add, [b_pO[so], b_num], [b_num])
                            TT(P, "dve", den[:, pq], pL[so][:], den[:, pq], ALU.add, [b_pL[so], b_den], [b_den])
            P.op("dve", lambda e: e.reciprocal(out=den[:], in_=den[:]), [b_den], [b_den])
            TT(P, "dve", outb[hs][:], num[:], den[:], ALU.mult, [b_num, b_den], [b_ob[hs]])
            DMA(P, "pool", mix_d[:, h, :], outb[hs][:], b_mix, b_ob[hs])
        P.flush()


def tail1_stage(nc, P, dr, x_row0):
    from contextlib import ExitStack
    B = Buf
    ext = dr["ext"]
    with ExitStack() as st:
        sb = lambda n, s, d: st.enter_context(nc.sbuf_tensor(_uid(n), s, d))
        ps = lambda n, s, d: st.enter_context(nc.psum_tensor(_uid(n), s, d))
        T = norm_tiles(nc, sb, ps, B)
        DMA(P, "sync", T["idt"][:], dr["ident_bf"][0], T["b_idt"], ext)
        P.op("dve", lambda e: e.memset(T["eps"][:], 1e-6), [], [T["b_eps"]])
        g2t = sb("g2t_", [128, KT], F32); b_g2 = B("g2t")
        DMA(P, "sync", g2t[:], dr["g2"][0], b_g2, ext)
        wout = sb("wout", [128, KT, D], BF16); b_wo = B("wout")
        DMA(P, "sync", wout[:], dr["wout_bf"][0], b_wo, dr["wout_bf"][1])
        mixt = [sb("mixt%d" % i, [128, KT, 128], BF16) for i in range(2)]; b_mt = [B("mt0"), B("mt1")]
        xin = [sb("xin%d" % i, [128, D], F32) for i in range(2)]; b_xin = [B("xin0"), B("xin1")]
        hn2b = [sb("hn2b%d" % i, [128, KT, 128], BF16) for i in range(2)]; b_h2 = [B("h2b0"), B("h2b1")]
        pH = ps("pH", [128, D], F32); b_pH = B("pH")
        xpad, b_x = dr["xpad"]; mix_d, b_mix = dr["mix_d"]; h_d, b_hd = dr["h_d"]; hn2_d, b_hn2d = dr["hn2T_d"]
        for i in range(NTOK // 128):
            s = i % 2
            DMA(P, "sync", mixt[s][:], mix_d[:, :, i * 128:(i + 1) * 128], b_mt[s], b_mix)
            DMA(P, "sync", xin[s][:], xpad[x_row0 + i * 128:x_row0 + (i + 1) * 128, :], b_xin[s], b_x)
            fns = []
            for fb in range(4):
                for k in range(KT):
                    fns.append(lambda e, s=s, fb=fb, k=k: e.matmul(
                        pH[:, fb * 512:(fb + 1) * 512], lhsT=mixt[s][:, k, :], rhs=wout[:, k, fb * 512:(fb + 1) * 512],
                        start=(k == 0), stop=(k == KT - 1)))
            P.pe_group(fns, [b_mt[s], b_wo], [b_pH])
            TT(P, "dve", T["xt"][s][:], pH[:], xin[s][:], ALU.add, [b_pH, b_xin[s]], [T["b_xt"][s]])
            DMA(P, "pool", h_d[i * 128:(i + 1) * 128, :], T["xt"][s][:], b_hd, T["b_xt"][s])
            norm_transpose(nc, P, T, None, None, hn2b[s][:], b_h2[s], g2t, b_g2, s)
            DMA(P, "pool", hn2_d[:, :, i * 128:(i + 1) * 128], hn2b[s][:], b_hn2d, b_h2[s])
        P.flush()


DFF = 5632
NF = DFF // 128


def tail2_stage(nc, P, dr):
    from contextlib import ExitStack
    B = Buf
    ext = dr["ext"]
    with ExitStack() as st:
        sb = lambda n, s, d: st.enter_context(nc.sbuf_tensor(_uid(n), s, d))
        ps = lambda n, s, d: st.enter_context(nc.psum_tensor(_uid(n), s, d))
        hn2 = [sb("hn2s%d" % i, [128, KT, BLK], BF16) for i in range(2)]; b_hn2 = [B("hn2s0"), B("hn2s1")]
        HT = sb("HT", [128, NF, BLK], BF16); b_HT = B("HT")
        wst = [sb("wst2_%d" % i, [128, KT * 512], BF16) for i in range(3)]; b_wst = [B("w2_%d" % i) for i in range(3)]
        gsig = sb("gsig", [128, BLK], F32); b_gs = B("gsig")
        hin = [sb("hin%d" % i, [128, D], F32) for i in range(4)]; b_hin = [B("hin%d" % i) for i in range(4)]
        junk = sb("junk2", [128, D], F32); b_junk = B("junk2")
        gF = sb("gF", [128, D], F32); b_gF = B("gF")
        DMA(P, "sync", gF[:], dr["final_g"][0].partition_broadcast(128), b_gF, ext)
        ss = sb("ss2", [128, 4], F32); rstd = sb("rstd2", [128, 4], F32); b_ss = [B("ss2_%d" % i) for i in range(4)]
        eps_t = sb("eps2", [128, 1], F32); b_eps = B("eps2")
        P.op("dve", lambda e: e.memset(eps_t[:], 1e-6), [], [b_eps])
        pG = [ps("pG%d" % i, [128, BLK], F32) for i in range(2)]; b_pG = [B("pG0"), B("pG1")]
        pU = [ps("pU%d" % i, [128, BLK], F32) for i in range(2)]; b_pU = [B("pU0"), B("pU1")]
        pD = [ps("pD%d" % i, [128, 1024], F32) for i in range(2)]; b_pD = [B("pD0"), B("pD1")]
        hn2_d, b_hn2d = dr["hn2T_d"]; h_d, b_hd = dr["h_d"]; y_d, b_yd = dr["y"]
        wg_bf, b_wg = dr["wgate_bf"]; wu_bf, b_wub = dr["wup_bf"]; wd_bf, b_wd = dr["wdown_bf"]
        nw = 0
        for sbk in range(NTOK // BLK):
            hs = sbk % 2
            DMA(P, "sync", hn2[hs][:], hn2_d[:, :, sbk * BLK:(sbk + 1) * BLK], b_hn2[hs], b_hn2d)
            for fc in range(NF // 4):
                sg = nw % 3; nw += 1
                vg = wst[sg][:].rearrange("p (k n) -> p k n", k=KT)
                DMA(P, "sync", vg, wg_bf[:, :, fc * 512:(fc + 1) * 512], b_wst[sg], b_wg)
                su = nw % 3; nw += 1
                vu = wst[su][:].rearrange("p (k n) -> p k n", k=KT)
                DMA(P, "sync", vu, wu_bf[:, :, fc * 512:(fc + 1) * 512], b_wst[su], b_wub)
                for j in range(4):
                    f = 4 * fc + j
                    s = f % 2
                    P.pe_group([(lambda e, k=k, s=s, j=j, vg=vg, hs=hs: e.matmul(
                        pG[s][:], lhsT=vg[:, k, j * 128:(j + 1) * 128], rhs=hn2[hs][:, k, :],
                        start=(k == 0), stop=(k == KT - 1))) for k in range(KT)], [b_wst[sg], b_hn2[hs]], [b_pG[s]])
                    P.pe_group([(lambda e, k=k, s=s, j=j, vu=vu, hs=hs: e.matmul(
                        pU[s][:], lhsT=vu[:, k, j * 128:(j + 1) * 128], rhs=hn2[hs][:, k, :],
                        start=(k == 0), stop=(k == KT - 1))) for k in range(KT)], [b_wst[su], b_hn2[hs]], [b_pU[s]])
                    ACTF(P, gsig[:], pG[s][:], ACT.Silu, [b_pG[s]], [b_gs])
                    TT(P, "dve", HT[:, f, :], pU[s][:], gsig[:], ALU.mult, [b_pU[s], b_gs], [b_HT])
            for t in range(4):
                i = sbk * 4 + t
                DMA(P, "sync", hin[t][:], h_d[i * 128:(i + 1) * 128, :], b_hin[t], b_hd)
            for fc in range(NF // 4):
                sd = nw % 3; nw += 1
                vd = wst[sd][:].rearrange("p (f n) -> p f n", f=4)
                DMA(P, "sync", vd, wd_bf[:, fc * 4:(fc + 1) * 4, :], b_wst[sd], b_wd)
                for t in range(4):
                    for hf in range(2):
                        fns = []
                        for j in range(4):
                            f = 4 * fc + j
                            for fb2 in range(2):
                                fb = 2 * hf + fb2
                                fns.append(lambda e, j=j, f=f, fb=fb, fb2=fb2, hf=hf, vd=vd, t=t: e.matmul(
                                    pD[hf][:, fb2 * 512:(fb2 + 1) * 512], lhsT=HT[:, f, t * 128:(t + 1) * 128],
                                    rhs=vd[:, j, fb * 512:(fb + 1) * 512], start=(j == 0), stop=(j == 3)))
                        P.pe_group(fns, [b_HT, b_wst[sd]], [b_pD[hf]])
                        hsl = slice(1024 * hf, 1024 * hf + 1024)
                        TT(P, "dve", hin[t][:, hsl], pD[hf][:], hin[t][:, hsl], ALU.add, [b_pD[hf], b_hin[t]], [b_hin[t]])
            for t in range(4):
                i = sbk * 4 + t
                s2 = t
                P.op("act", lambda e, s2=s2: e.activation(out=junk[:], in_=hin[s2][:], func=ACT.Square,
                                                          accum_out=ss[:, s2:s2 + 1]), [b_hin[s2]], [b_junk, b_ss[s2]])
                P.op("act", lambda e, s2=s2: e.activation(out=rstd[:, s2:s2 + 1], in_=ss[:, s2:s2 + 1], func=ACT.Sqrt,
                                                          scale=1.0 / D, bias=eps_t[:]), [b_ss[s2], b_eps], [b_ss[s2]])
                P.op("dve", lambda e, s2=s2: e.reciprocal(out=rstd[:, s2:s2 + 1], in_=rstd[:, s2:s2 + 1]),
                     [b_ss[s2]], [b_ss[s2]])
                STT(P, "dve", hin[s2][:], hin[s2][:], rstd[:, s2:s2 + 1], gF[:], ALU.mult, ALU.mult,
                    [b_hin[s2], b_ss[s2], b_gF], [b_hin[s2]])
                DMA(P, "pool", y_d[i * 128:(i + 1) * 128, :], hin[s2][:], b_yd, b_hin[s2])
        P.flush(final_bufs=[b_yd])


N_PRE_UNITS = (SEQ - NTOK) // UNIT
N_OWN_UNITS = NTOK // UNIT
FIRST_KV_BLK = (SEQ - 2 * NTOK) // BLK
FIRST_Q_BLK = (SEQ - NTOK) // BLK

_IN_SPECS = [
    ("xpad", [SEQ, D], F32), ("wfm32", [NCT * 128, KT * 128], F32), ("wv32", [128, KT * 1024], F32),
    ("wglu32", [128, 8 * 1024], F32), ("wout32", [128, KT * D], F32), ("wgate32", [128, KT * DFF], F32),
    ("wup32", [128, KT * DFF], F32), ("wdown32", [128, NF * D], F32), ("g1", [128, KT], F32), ("g2", [128, KT], F32),
    ("final_g", [D], F32), ("a_re", [64, 64], F32), ("a_im", [64, 64], F32), ("log_dt", [64], F32),
    ("b_re", [64, 64, 16], F32), ("b_im", [64, 64, 16], F32), ("c_re", [64, 16, 64], F32), ("c_im", [64, 16, 64], F32),
    ("d_skip", [64, 16], F32), ("b_glu", [1024], F32), ("ident32", [128, 128], F32), ("ident_bf", [128, 128], BF16),
    ("aconsts", [128, 4, 128], BF16), ("hmask", [128, 1], F32), ("cos_d", [128, 3, 2 * NTOK], F32),
    ("sin_d", [128, 3, 2 * NTOK], F32),
]


def build_program():
    from contextlib import ExitStack
    nc = bass.Bass("TRN2", target_bir_lowering=False)
    ext = Buf("ext", False)
    dr = {"ext": ext}
    for name, shape, dt in _IN_SPECS:
        dr[name] = (nc.dram_tensor(name, shape, dt, kind="ExternalInput").ap(), ext)
    dr["y"] = (nc.dram_tensor("y", [NTOK, D], F32, kind="ExternalOutput").ap(), Buf("y", False))

    def scratch(name, shape, dt, keep=False):
        dr[name] = (nc.dram_tensor(name, shape, dt).ap(), Buf(name, False, keep))
    scratch("wfm_bf2", [NCT * 128, KT * 128], BF16); scratch("wv_bf2", [128, KT * 1024], BF16)
    scratch("wglu_bf2", [128, 8 * 1024], BF16); scratch("wout_bf2", [128, KT * D], BF16)
    scratch("wgate_bf2", [128, KT * DFF], BF16); scratch("wup_bf2", [128, KT * DFF], BF16)
    scratch("wdown_bf2", [128, NF * D], BF16)
    scratch("uT_d", [128, 8, SEQ], BF16); scratch("qT_d", [128, 8, NTOK], BF16); scratch("kT_d", [128, 8, 2 * NTOK], BF16)
    scratch("V_d", [2 * NTOK, 1024], BF16); scratch("mix_d", [128, 16, NTOK], BF16)
    scratch("h_d", [NTOK, D], F32); scratch("hn2T_d", [128, KT, NTOK], BF16)
    dr["wfm_bf"] = (dr["wfm_bf2"][0].rearrange("(c p) (k m) -> c p k m", p=128, k=KT), dr["wfm_bf2"][1])
    dr["wv_bf"] = (dr["wv_bf2"][0].rearrange("p (k n) -> p k n", k=KT), dr["wv_bf2"][1])
    dr["wglu_bf"] = (dr["wglu_bf2"][0].rearrange("p (k n) -> p k n", k=8), dr["wglu_bf2"][1])
    dr["wout_bf"] = (dr["wout_bf2"][0].rearrange("p (k n) -> p k n", k=KT), dr["wout_bf2"][1])
    dr["wgate_bf"] = (dr["wgate_bf2"][0].rearrange("p (k n) -> p k n", k=KT), dr["wgate_bf2"][1])
    dr["wup_bf"] = (dr["wup_bf2"][0].rearrange("p (k n) -> p k n", k=KT), dr["wup_bf2"][1])
    dr["wdown_bf"] = (dr["wdown_bf2"][0].rearrange("p (f n) -> p f n", f=NF), dr["wdown_bf2"][1])
    dr["ssm_d"] = (dr["mix_d"][0][:, 8:16, :], dr["mix_d"][1])
    with ExitStack() as st:
        P = Prog(nc, st)
        pairs = []
        for nm in ("wfm", "wv", "wglu", "wout", "wgate", "wup", "wdown"):
            pairs.append((dr[nm + "_bf2"][0], dr[nm + "_bf2"][1], dr[nm + "32"][0], Buf(nm + "32", False, keep=True)))
        cast_weights(nc, P, pairs)
        front_stage(nc, P, dr, NBLK_ALL, FIRST_KV_BLK, FIRST_Q_BLK)
        s5_stage(nc, P, dr, N_PRE_UNITS, N_OWN_UNITS)
        attn_stage(nc, P, dr)
        tail1_stage(nc, P, dr, SEQ - NTOK)
        tail2_stage(nc, P, dr)
    return nc


def _tile_rows(w, kt):
    n = w.shape[1]
    return np.ascontiguousarray(w.reshape(kt, 128, n).transpose(1, 0, 2)).reshape(128, kt * n)


def _head_perm(h):
    j = h % 3
    perm = np.zeros(128, np.int64)
    for m in range(128):
        if 32 * j <= m < 32 * j + 32:
            perm[m] = m - 32 * j
        elif m < 32 * j:
            perm[m] = 32 + m
        else:
            perm[m] = m
    return perm


def _prep_shared(inp):
    f32 = np.float32
    w_in = np.asarray(inp["w_in"], f32)[0]
    cols = np.full((NCT, 128), -1, np.int64)
    for base, ct0, ctsw in ((0, CT_Q, CT_QSW), (1024, CT_K, CT_KSW)):
        for h in range(8):
            cols[ct0 + h] = base + h * 128 + _head_perm(h)
            tt, j = h // 3, h % 3
            for i in range(32):
                cols[ctsw + tt, 32 * j + i] = base + h * 128 + (i + 16) % 32
    for k in range(8):
        cols[CT_U + k] = 3072 + k * 128 + np.arange(128)
    flat = cols.reshape(-1)
    wsel = np.where(flat[None, :] >= 0, w_in[:, np.maximum(flat, 0)], 0.0).astype(f32)
    wfm = wsel.reshape(KT, 128, NCT, 128).transpose(2, 1, 0, 3)
    sh = {}
    sh["wfm32"] = np.ascontiguousarray(wfm).reshape(NCT * 128, KT * 128)
    sh["wv32"] = _tile_rows(np.ascontiguousarray(w_in[:, 2048:3072]), KT)
    sh["wglu32"] = _tile_rows(np.asarray(inp["w_glu"], f32)[0], 8)
    sh["wout32"] = _tile_rows(np.asarray(inp["w_out"], f32)[0], KT)
    sh["wgate32"] = _tile_rows(np.asarray(inp["w_gate"], f32)[0], KT)
    sh["wup32"] = _tile_rows(np.asarray(inp["w_up"], f32)[0], KT)
    sh["wdown32"] = _tile_rows(np.asarray(inp["w_down"], f32)[0], NF)
    sh["g1"] = np.ascontiguousarray(np.asarray(inp["norm1_g"], f32)[0].reshape(KT, 128).T)
    sh["g2"] = np.ascontiguousarray(np.asarray(inp["norm2_g"], f32)[0].reshape(KT, 128).T)
    sh["final_g"] = np.ascontiguousarray(np.asarray(inp["final_g"], f32))
    for nm in ("a_re", "a_im", "log_dt", "b_re", "b_im", "c_re", "c_im", "d_skip", "b_glu"):
        sh[nm] = np.ascontiguousarray(np.asarray(inp[nm], f32)[0])
    sh["ident32"] = np.eye(128, dtype=f32)
    sh["ident_bf"] = np.eye(128, dtype=f32).astype(ml_dtypes.bfloat16)
    kk = np.arange(128)[:, None]; qq = np.arange(128)[None, :]
    mcur = np.where(kk <= qq, 0.0, -30000.0); mprev = np.where(kk >= qq, 0.0, -30000.0)
    sh["aconsts"] = np.ascontiguousarray(
        np.stack([np.eye(128), mcur, mprev, np.ones((128, 128))], 1).astype(f32).astype(ml_dtypes.bfloat16))
    return sh


def _rope_tables(t0):
    f32 = np.float32
    pos = (np.arange(2 * NTOK) + (t0 - NTOK)).astype(f32)
    pos = np.maximum(pos, f32(0))
    inv_freq = (f32(500000.0) ** (-(np.arange(0, 32, 2).astype(f32)) / f32(32))).astype(f32)
    i = np.arange(32)
    ang = (pos[None, :] * inv_freq[i % 16][:, None]).astype(f32)
    c32 = np.cos(ang).astype(f32)
    s32 = np.sin(ang).astype(f32) * np.where(i < 16, -1.0, 1.0).astype(f32)[:, None]
    cosT = np.ones((128, 3, 2 * NTOK), f32)
    sinT = np.zeros((128, 3, 2 * NTOK), f32)
    for j in range(3):
        cosT[32 * j:32 * j + 32, j, :] = c32
        sinT[32 * j:32 * j + 32, j, :] = s32
    return cosT, sinT


def kernel(**inputs):
    x = np.asarray(inputs["x"], np.float32)[0]
    sh = _prep_shared(inputs)
    nc = build_program()
    in_maps = []
    for c in range(NCORES):
        t0 = c * NTOK
        xpad = np.zeros((SEQ, D), np.float32)
        n_real = t0 + NTOK
        xpad[SEQ - n_real:] = x[:n_real]
        cosT, sinT = _rope_tables(t0)
        m = dict(sh)
        m["xpad"] = xpad
        m["cos_d"] = cosT
        m["sin_d"] = sinT
        m["hmask"] = np.full((128, 1), -30000.0 if c == 0 else 0.0, np.float32)
        in_maps.append(m)
    res = run_bass_kernel_spmd(nc, in_maps, core_ids=list(range(NCORES)))
    y = np.concatenate([np.asarray(res.results[c]["y"], np.float32) for c in range(NCORES)], axis=0)
    return y.reshape(1, SEQ, D)
```

```python
import math
import numpy as np
import ml_dtypes
import concourse.bass as bass
import concourse.mybir as mybir
from concourse.bass_utils import run_bass_kernel_spmd

F32 = mybir.dt.float32
BF16 = mybir.dt.bfloat16
ALU = mybir.AluOpType
ACT = mybir.ActivationFunctionType
AX = mybir.AxisListType

NCORES = 8
D = 2048
SEQ = 16384
NTOK = SEQ // NCORES
BLK = 512
NBLK_ALL = SEQ // BLK
KT = D // 128

ENGS = ("sync", "act", "pool", "dve", "pe")


TWO_PI = 2.0 * math.pi
_UID = [0]
SKIP = set()


def _uid(n):
    _UID[0] += 1
    return "%s_%d" % (n, _UID[0])


class Buf:
    __slots__ = ("name", "w", "r", "dsem", "sb", "keep")

    def __init__(self, name, sb=True, keep=False):
        self.name = name
        self.w = None
        self.r = {}
        self.dsem = None
        self.sb = sb
        self.keep = keep


class Prog:
    def __init__(self, nc, stack, ndsem=80):
        self.nc = nc
        self.csem = {k: stack.enter_context(nc.semaphore("s_" + k)) for k in ("c_act", "c_pool", "c_dve", "c_pe")}
        self.dsem = [stack.enter_context(nc.semaphore("sd%d" % i)) for i in range(ndsem)]
        self.dval = [0] * ndsem
        self.dfree = list(range(ndsem))
        self.dkeep = set()
        self.ops = {e: [] for e in ENGS}
        self.cnt = {e: 0 for e in ENGS}
        self.known = {e: {} for e in ENGS}
        self.bufs = []
        self.nblocks = 0

    def _reg(self, b):
        if b not in self.bufs:
            self.bufs.append(b)

    def _deps(self, eng, reads, writes, skip_same_pe=False):
        need = {}

        def add(tok):
            if tok is None:
                return
            k, v = tok
            if skip_same_pe and k == "c_pe":
                return
            if need.get(k, 0) < v:
                need[k] = v
        for b in reads:
            add(b.w)
        for b in writes:
            add(b.w)
            for k, v in b.r.items():
                add((k, v))
        waits = []
        kn = self.known[eng]
        for k, v in need.items():
            if kn.get(k, 0) < v:
                kn[k] = v
                waits.append((k, v))
        return waits

    def _mark(self, tok, reads, writes):
        for b in reads:
            b.r[tok[0]] = tok[1]
            self._reg(b)
        for b in writes:
            b.w = tok
            b.r = {}
            self._reg(b)

    def op(self, eng, fn, reads=(), writes=()):
        waits = self._deps(eng, reads, writes)
        self.cnt[eng] += 1
        tok = ("c_" + eng, self.cnt[eng])
        self.ops[eng].append((waits, fn, (tok[0], 1)))
        self._mark(tok, reads, writes)

    def pe_group(self, fns, reads=(), writes=()):
        waits = self._deps("pe", reads, writes, skip_same_pe=True)
        self.cnt["pe"] += 1
        tok = ("c_pe", self.cnt["pe"])
        n = len(fns)
        for i, fn in enumerate(fns):
            self.ops["pe"].append((waits if i == 0 else [], fn, (tok[0], 1) if i == n - 1 else None))
        self._mark(tok, reads, writes)

    def dma(self, q, fn, dst, src):
        waits = self._deps(q, [src], [dst])
        key = dst if dst.sb else src
        if key.dsem is None:
            key.dsem = self.dfree.pop(0)
            if key.keep:
                self.dkeep.add(key.dsem)
        i = key.dsem
        self.dval[i] += 16
        tok = (i, self.dval[i])
        self.ops[q].append((waits, fn, (i, 16)))
        src.r[tok[0]] = tok[1]
        dst.w = tok
        dst.r = {}
        self._reg(src)
        self._reg(dst)
        self._reg(key)

    def wait_all(self, eng, bufs):
        waits = self._deps(eng, bufs, [])
        self.ops[eng].append((waits, None, None))

    def _sem(self, k):
        return self.csem[k] if isinstance(k, str) else self.dsem[k]

    def flush(self, final_bufs=()):
        kn = self.known["sync"]
        waits = []
        for i, v in enumerate(self.dval):
            if i in self.dkeep or i in self.dfree:
                continue
            if kn.get(i, 0) < v:
                kn[i] = v
                waits.append((i, v))
        for b in final_bufs:
            if b.w is not None and kn.get(b.w[0], 0) < b.w[1]:
                kn[b.w[0]] = b.w[1]
                waits.append(b.w)
        self.ops["sync"].append((waits, None, None))
        nc = self.nc
        self.nblocks += 1
        with nc.Block() as block:
            def run(name):
                def body(e):
                    for waits, fn, inc in self.ops[name]:
                        for k, v in waits:
                            e.wait_ge(self._sem(k), v)
                        if fn is not None:
                            ins = fn(e)
                            if inc is not None:
                                ins.then_inc(self._sem(inc[0]), inc[1])
                return body
            block.sync(run("sync"))
            block.scalar(run("act"))
            block.gpsimd(run("pool"))
            block.vector(run("dve"))
            block.tensor(run("pe"))
        self.ops = {e: [] for e in ENGS}
        for e in ENGS:
            kn = self.known[e]
            for k in ("act", "pool", "dve", "pe"):
                kn["c_" + k] = self.cnt[k]
            for i, v in enumerate(self.dval):
                if i not in self.dkeep:
                    kn[i] = v
        for b in self.bufs:
            if b.w is not None and b.w[0] in self.dkeep:
                pass
            else:
                b.w = None
            b.r = {k: v for k, v in b.r.items() if k in self.dkeep}
            if b.dsem is not None and b.dsem not in self.dkeep:
                self.dfree.append(b.dsem)
                b.dsem = None
        self.bufs = [b for b in self.bufs if b.w is not None or b.r or b.dsem is not None]


def TT(P, eng, out, in0, in1, op, reads, writes):
    P.op(eng, lambda e: e.tensor_tensor(out=out, in0=in0, in1=in1, op=op), reads, writes)


def TS(P, eng, out, in0, s1, s2, op0, op1, reads, writes):
    if s2 is None:
        P.op(eng, lambda e: e.tensor_scalar(out=out, in0=in0, scalar1=s1, scalar2=None, op0=op0), reads, writes)
    else:
        P.op(eng, lambda e: e.tensor_scalar(out=out, in0=in0, scalar1=s1, scalar2=s2, op0=op0, op1=op1), reads, writes)


def STT(P, eng, out, in0, scalar, in1, op0, op1, reads, writes):
    P.op(eng, lambda e: e.scalar_tensor_tensor(out=out, in0=in0, scalar=scalar, in1=in1, op0=op0, op1=op1), reads, writes)


def ACTF(P, out, in_, func, reads, writes, scale=1.0, bias=None):
    if bias is None:
        P.op("act", lambda e: e.activation(out=out, in_=in_, func=func, scale=scale), reads, writes)
    else:
        P.op("act", lambda e: e.activation(out=out, in_=in_, func=func, scale=scale, bias=bias), reads, writes)


def CP(P, eng, out, in_, reads, writes):
    if eng == "act":
        P.op("act", lambda e: e.activation(out=out, in_=in_, func=ACT.Copy), reads, writes)
    else:
        P.op(eng, lambda e: e.tensor_copy(out=out, in_=in_), reads, writes)


def DMA(P, q, out, in_, dst, src, slow=False):
    if slow:
        P.dma(q, lambda e: e.dma_start(out=out, in_=in_, allow_slow_non_contiguous=True), dst, src)
    else:
        P.dma(q, lambda e: e.dma_start(out=out, in_=in_), dst, src)


def cmul(P, eng, o_re, o_im, a_re, a_im, b_re, b_im, t1, t2, reads, writes, tb):
    TT(P, eng, t1, a_re, b_re, ALU.mult, reads, [tb])
    TT(P, eng, t2, a_im, b_im, ALU.mult, reads, [tb])
    TT(P, eng, o_re, t1, t2, ALU.subtract, [tb], writes)
    TT(P, eng, t1, a_re, b_im, ALU.mult, reads, [tb])
    TT(P, eng, t2, a_im, b_re, ALU.mult, reads, [tb])
    TT(P, eng, o_im, t1, t2, ALU.add, [tb], writes)


T0 = 8
UNIT = 1024
NSC = UNIT // T0
I32 = mybir.dt.int32


def s5_stage(nc, P, dr, n_pre_units, n_own_units):
    from contextlib import ExitStack
    B = Buf
    n_units = n_pre_units + n_own_units
    ext = dr["a_re"][1]
    with ExitStack() as st0:
        sbp = lambda n, s, d: st0.enter_context(nc.sbuf_tensor(_uid(n), s, d))
        sc = sbp("sc", [128, 24, 32], F32); b_sc = B("sc")
        pw = sbp("pw", [128, 3, T0 + 1, 32], F32); b_pw = B("pw")
        WBT = sbp("WBT", [128, 8, T0, 2, 128], BF16); b_WBT = B("WBT")
        WCT = sbp("WCT", [128, 8, 2, 128], BF16); b_WCT = B("WCT")
        Rc = sbp("Rc", [128, 32, NSC], F32); Rs = sbp("Rs", [128, 32, NSC], F32); b_R = B("R")
        Dcol = sbp("Dcol", [128, 8], F32); b_D = B("Dcol")
        wglu = sbp("wglu", [128, 8, 1024], BF16); b_wglu = B("wglu")
        bglu = sbp("bglu", [128, 8], F32); b_bglu = B("bglu")
        cr = sbp("cr", [128, 32, 1], F32); ci = sbp("ci", [128, 32, 1], F32); b_c = B("carry")
        P.op("dve", lambda e: e.memset(cr[:], 0.0), [], [b_c])
        P.op("dve", lambda e: e.memset(ci[:], 0.0), [], [b_c])
        S = lambda j: sc[:, j, :]
        DT, MAG, PHI, SINP, COSP, ABR, ABI, CR, CI, T1, T2, T3, RHO, C8, S8, NUMR, NUMI, DEN = range(18)
        DMA(P, "sync", Dcol[:], dr["d_skip"][0].rearrange("(k gl) p -> (gl p) k", gl=8), b_D, ext, slow=True)
        DMA(P, "sync", wglu[:], dr["wglu_bf"][0], b_wglu, dr["wglu_bf"][1])
        DMA(P, "sync", bglu[:], dr["b_glu"][0].rearrange("(k p) -> p k", p=128), b_bglu, ext, slow=True)

        with ExitStack() as st:
            sb = lambda n, s, d: st.enter_context(nc.sbuf_tensor(_uid(n), s, d))
            ps = lambda n, s, d: st.enter_context(nc.psum_tensor(_uid(n), s, d))
            are = sb("are", [128, 32], F32); aim = sb("aim", [128, 32], F32); ldt = sb("ldt", [128, 32], F32)
            b_par = B("par")
            DMA(P, "sync", are[:], dr["a_re"][0].rearrange("(pi g) n -> (g n) pi", g=2), b_par, ext, slow=True)
            DMA(P, "sync", aim[:], dr["a_im"][0].rearrange("(pi g) n -> (g n) pi", g=2), b_par, ext, slow=True)
            for g2 in range(2):
                src = dr["log_dt"][0].rearrange("(pi g) -> g pi", g=2)[g2:g2 + 1, :].to_broadcast([64, 32])
                DMA(P, "sync", ldt[64 * g2:64 * g2 + 64, :], src, b_par, ext, slow=True)
            Bre = sb("Bre", [128, 32, 16], F32); Bim = sb("Bim", [128, 32, 16], F32); b_B = B("B")
            DMA(P, "sync", Bre[:], dr["b_re"][0].rearrange("(pi g) n q -> (g n) pi q", g=2), b_B, ext, slow=True)
            DMA(P, "sync", Bim[:], dr["b_im"][0].rearrange("(pi g) n q -> (g n) pi q", g=2), b_B, ext, slow=True)
            id32 = sb("id32", [128, 128], F32); b_id = B("id32")
            DMA(P, "sync", id32[:], dr["ident32"][0], b_id, ext)
            Cx = [sb("Cx%d" % i, [128, 8, 128], F32) for i in range(2)]; b_Cx = B("Cx")
            for i in range(2):
                P.op("pool", lambda e, i=i: e.memset(Cx[i][:], 0.0), [], [b_Cx])
            for i, nm in enumerate(("c_re", "c_im")):
                cv = dr[nm][0].rearrange("(pi8 pi4 g) p n -> pi4 g p pi8 n", pi4=4, g=2)
                for pi4 in range(4):
                    for g2 in range(2):
                        p0 = pi4 * 32 + g2 * 16
                        DMA(P, "sync", Cx[i][p0:p0 + 16, :, 64 * g2:64 * g2 + 64], cv[pi4, g2], b_Cx, ext, slow=True)
            ki = sb("ki", [128, 32], I32); b_ki = B("ki")

            def sin_of(out, ang):
                TS(P, "dve", S(T1), ang, 1.0 / TWO_PI, None, ALU.mult, None, [b_sc], [b_sc])
                CP(P, "dve", ki[:], S(T1), [b_sc], [b_ki])
                CP(P, "dve", S(T1), ki[:], [b_ki], [b_sc])
                STT(P, "dve", S(T2), S(T1), -TWO_PI, ang, ALU.mult, ALU.add, [b_sc], [b_sc])
                TS(P, "dve", S(T3), S(T2), math.pi, -TWO_PI, ALU.is_gt, ALU.mult, [b_sc], [b_sc])
                TT(P, "dve", S(T2), S(T2), S(T3), ALU.add, [b_sc], [b_sc])
                TS(P, "dve", S(T3), S(T2), -math.pi, TWO_PI, ALU.is_lt, ALU.mult, [b_sc], [b_sc])
                TT(P, "dve", S(T2), S(T2), S(T3), ALU.add, [b_sc], [b_sc])
                ACTF(P, out, S(T2), ACT.Sin, [b_sc], [b_sc])

            ACTF(P, S(DT), ldt[:], ACT.Exp, [b_par], [b_sc])
            TT(P, "dve", S(MAG), are[:], S(DT), ALU.mult, [b_par, b_sc], [b_sc])
            ACTF(P, S(RHO), S(MAG), ACT.Exp, [b_sc], [b_sc], scale=float(T0))
            ACTF(P, S(MAG), S(MAG), ACT.Exp, [b_sc], [b_sc])
            TT(P, "dve", S(PHI), aim[:], S(DT), ALU.mult, [b_par, b_sc], [b_sc])
            sin_of(S(SINP), S(PHI))
            TS(P, "dve", S(NUMR), S(PHI), math.pi / 2, None, ALU.add, None, [b_sc], [b_sc])
            sin_of(S(COSP), S(NUMR))
            TT(P, "dve", S(ABR), S(MAG), S(COSP), ALU.mult, [b_sc], [b_sc])
            TT(P, "dve", S(ABI), S(MAG), S(SINP), ALU.mult, [b_sc], [b_sc])
            TS(P, "dve", S(T1), S(ABR), -1.0, None, ALU.add, None, [b_sc], [b_sc])
            TT(P, "dve", S(NUMR), S(T1), are[:], ALU.mult, [b_sc, b_par], [b_sc])
            TT(P, "dve", S(T2), S(ABI), aim[:], ALU.mult, [b_sc, b_par], [b_sc])
            TT(P, "dve", S(NUMR), S(NUMR), S(T2), ALU.add, [b_sc], [b_sc])
            TT(P, "dve", S(NUMI), S(ABI), are[:], ALU.mult, [b_sc, b_par], [b_sc])
            TT(P, "dve", S(T2), S(T1), aim[:], ALU.mult, [b_sc, b_par], [b_sc])
            TT(P, "dve", S(NUMI), S(NUMI), S(T2), ALU.subtract, [b_sc], [b_sc])
            TT(P, "dve", S(DEN), are[:], are[:], ALU.mult, [b_par], [b_sc])
            TT(P, "dve", S(T2), aim[:], aim[:], ALU.mult, [b_par], [b_sc])
            TT(P, "dve", S(DEN), S(DEN), S(T2), ALU.add, [b_sc], [b_sc])
            P.op("dve", lambda e: e.reciprocal(out=S(DEN), in_=S(DEN)), [b_sc], [b_sc])
            TT(P, "dve", S(CR), S(NUMR), S(DEN), ALU.mult, [b_sc], [b_sc])
            TT(P, "dve", S(CI), S(NUMI), S(DEN), ALU.mult, [b_sc], [b_sc])
            CP(P, "dve", S(C8), S(COSP), [b_sc], [b_sc])
            CP(P, "dve", S(S8), S(SINP), [b_sc], [b_sc])
            for _ in range(3):
                TT(P, "dve", S(T1), S(C8), S(C8), ALU.mult, [b_sc], [b_sc])
                TT(P, "dve", S(T2), S(S8), S(S8), ALU.mult, [b_sc], [b_sc])
                TT(P, "dve", S(T3), S(C8), S(S8), ALU.mult, [b_sc], [b_sc])
                TT(P, "dve", S(C8), S(T1), S(T2), ALU.subtract, [b_sc], [b_sc])
                TS(P, "dve", S(S8), S(T3), 2.0, None, ALU.mult, None, [b_sc], [b_sc])
            P.op("dve", lambda e: e.memset(pw[:, 0, 0, :], 1.0), [], [b_pw])
            P.op("dve", lambda e: e.memset(pw[:, 1, 0, :], 0.0), [], [b_pw])
            for k in range(1, T0 + 1):
                pr, pi_ = pw[:, 0, k - 1, :], pw[:, 1, k - 1, :]
                TT(P, "dve", S(T1), pr, S(ABR), ALU.mult, [b_pw, b_sc], [b_sc])
                TT(P, "dve", S(T2), pi_, S(ABI), ALU.mult, [b_pw, b_sc], [b_sc])
                TT(P, "dve", pw[:, 0, k, :], S(T1), S(T2), ALU.subtract, [b_sc], [b_pw])
                TT(P, "dve", S(T1), pr, S(ABI), ALU.mult, [b_pw, b_sc], [b_sc])
                TT(P, "dve", S(T2), pi_, S(ABR), ALU.mult, [b_pw, b_sc], [b_sc])
                TT(P, "dve", pw[:, 1, k, :], S(T1), S(T2), ALU.add, [b_sc], [b_pw])
            TS(P, "dve", pw[:, 2, :, :], pw[:, 1, :, :], -1.0, None, ALU.mult, None, [b_pw], [b_pw])
            Bb = sb("Bb", [128, 2, 32, 16], F32); b_Bb = B("Bb")
            tA = sb("tA", [128, T0, 32, 16], F32); tB = sb("tB", [128, T0, 32, 16], F32); b_t = B("tAB")
            bc16 = lambda j: S(j).unsqueeze(2).to_broadcast([128, 32, 16])
            cmul(P, "dve", Bb[:, 0], Bb[:, 1], bc16(CR), bc16(CI), Bre[:], Bim[:], tA[:, 0], tB[:, 0],
                 [b_sc, b_B], [b_Bb], b_t)
            WBx = sb("WBx", [128, T0, 2, 32, 32], F32); b_WBx = B("WBx")
            P.op("pool", lambda e: e.memset(WBx[:], 0.0), [], [b_WBx])
            for g2 in range(2):
                ps_ = slice(64 * g2, 64 * g2 + 64)
                cs_ = slice(16 * g2, 16 * g2 + 16)
                pwb = lambda ri: pw[ps_, ri, 0:T0, :].unsqueeze(3).to_broadcast([64, T0, 32, 16])
                bbb = lambda ri: Bb[ps_, ri].unsqueeze(1).to_broadcast([64, T0, 32, 16])
                cmul(P, "dve", WBx[ps_, :, 0, :, cs_], WBx[ps_, :, 1, :, cs_], pwb(0), pwb(1), bbb(0), bbb(1),
                     tA[ps_], tB[ps_], [b_pw, b_Bb], [b_WBx], b_t)
            pTr = [ps("pTr%d" % i, [128, 4, 128], F32) for i in range(2)]
            b_pTr = [B("pTr0"), B("pTr1")]
            nt = 0
            for pi8 in range(8):
                for tau in range(T0):
                    s = nt % 2; nt += 1
                    fns = []
                    for ri in range(2):
                        fns.append(lambda e, s=s, ri=ri, pi8=pi8, tau=tau: e.transpose(
                            out=pTr[s][:, ri, :],
                            in_=WBx[:, tau, ri, 4 * pi8:4 * pi8 + 4, :].rearrange("p a b -> p (a b)"),
                            identity=id32[:]))
                    P.pe_group(fns, [b_WBx, b_id], [b_pTr[s]])
                    CP(P, "act" if nt % 2 else "dve", WBT[:, pi8, tau, :, :], pTr[s][:, 0:2, :], [b_pTr[s]], [b_WBT])
            for pi8 in range(0, 8, 2):
                s = nt % 2; nt += 1
                fns = []
                for j in range(2):
                    for ri in range(2):
                        fns.append(lambda e, s=s, ri=ri, j=j, pi8=pi8: e.transpose(
                            out=pTr[s][:, 2 * j + ri, :], in_=Cx[ri][:, pi8 + j, :], identity=id32[:]))
                P.pe_group(fns, [b_Cx, b_id], [b_pTr[s]])
                for j in range(2):
                    CP(P, "dve", WCT[:, pi8 + j, 0, :], pTr[s][:, 2 * j, :], [b_pTr[s]], [b_WCT])
                    TS(P, "dve", WCT[:, pi8 + j, 1, :], pTr[s][:, 2 * j + 1, :], -1.0, None, ALU.mult, None,
                       [b_pTr[s]], [b_WCT])
            CP(P, "dve", Rc[:, :, 0], S(C8), [b_sc], [b_R])
            CP(P, "dve", Rs[:, :, 0], S(S8), [b_sc], [b_R])
            m = 1
            tAv = tA[:].rearrange("p a b c -> p (a b c)")
            tBv = tB[:].rearrange("p a b c -> p (a b c)")
            while m < NSC:
                bc = lambda t: t[:, :, m - 1:m].to_broadcast([128, 32, m])
                t1v = tAv[:, 0:32 * m].rearrange("p (a b) -> p a b", a=32)
                t2v = tBv[:, 0:32 * m].rearrange("p (a b) -> p a b", a=32)
                cmul(P, "dve", Rc[:, :, m:2 * m], Rs[:, :, m:2 * m], Rc[:, :, 0:m], Rs[:, :, 0:m], bc(Rc), bc(Rs),
                     t1v, t2v, [b_R], [b_R], b_t)
                m *= 2
            P.flush()

        NQ = 8
        uT_d, b_uTd = dr["uT_d"]
        ssm_d, b_ssmd = dr["ssm_d"]

        def alloc_set(sb, ps, tag):
            Sx = {}
            for nm in ("bA", "bB", "bC", "bD"):
                Sx[nm] = sb(nm + tag, [128, NQ, NSC], F32); Sx["b_" + nm] = B(nm + tag)
            for nm in ("bE", "bF"):
                Sx[nm] = sb(nm + tag, [128, NQ, NSC + 1], F32); Sx["b_" + nm] = B(nm + tag)
            Sx["psS"] = [ps("psS%d%s" % (i, tag), [128, 4, NSC], F32) for i in range(2)]; Sx["b_psS"] = B("psS" + tag)
            return Sx

        def quarter(Sx, uTb, b_uTb, q4, own):
            bA, bB, bC, bD, bE, bF = Sx["bA"], Sx["bB"], Sx["bC"], Sx["bD"], Sx["bE"], Sx["bF"]
            b_A, b_Bq, b_C, b_Dq, b_E, b_F = Sx["b_bA"], Sx["b_bB"], Sx["b_bC"], Sx["b_bD"], Sx["b_bE"], Sx["b_bF"]
            psS, b_psS = Sx["psS"], Sx["b_psS"]
            u3 = uTb[:].rearrange("p k (c j) -> p k j c", j=T0)
            psl = slice(NQ * q4, NQ * q4 + NQ)
            for h8 in range(NQ // 4):
                pi8 = (NQ // 4) * q4 + h8
                fns = []
                for pi4 in range(4):
                    r0 = 32 * pi4
                    for ri in range(2):
                        for j in range(T0):
                            fns.append(lambda e, r0=r0, ri=ri, j=j, pi8=pi8, pi4=pi4, u3=u3, psS=psS: e.matmul(
                                psS[ri][:, pi4, :], lhsT=WBT[r0:r0 + 32, pi8, T0 - 1 - j, ri, :],
                                rhs=u3[r0:r0 + 32, pi8, j, :], start=(j == 0), stop=(j == T0 - 1),
                                tile_position=(r0, 0)))
                P.pe_group(fns, [b_WBT, b_uTb], [b_psS])
                CP(P, "act", bA[:, 4 * h8:4 * h8 + 4, :], psS[0][:], [b_psS], [b_A])
                CP(P, "act", bB[:, 4 * h8:4 * h8 + 4, :], psS[1][:], [b_psS], [b_Bq])
            rc, rs = Rc[:, psl, :], Rs[:, psl, :]
            E1, F1 = bE[:, :, 1:], bF[:, :, 1:]
            TT(P, "dve", bC[:], bA[:], rc, ALU.mult, [b_A, b_R], [b_C])
            TT(P, "pool", E1, bB[:], rs, ALU.mult, [b_Bq, b_R], [b_E])
            TT(P, "dve", bC[:], bC[:], E1, ALU.add, [b_C, b_E], [b_C])
            TT(P, "dve", bD[:], bB[:], rc, ALU.mult, [b_Bq, b_R], [b_Dq])
            TT(P, "pool", F1, bA[:], rs, ALU.mult, [b_A, b_R], [b_F])
            TT(P, "dve", bD[:], bD[:], F1, ALU.subtract, [b_Dq, b_F], [b_Dq])
            for p_ in range(NQ):
                pi = NQ * q4 + p_
                P.op("dve", lambda e, pi=pi, p_=p_, bA=bA, bC=bC: e.tensor_tensor_scan(
                    out=bA[:, p_, :], data0=sc[:, RHO, pi:pi + 1].to_broadcast([128, NSC]), data1=bC[:, p_, :],
                    initial=cr[:, pi, :], op0=ALU.mult, op1=ALU.add), [b_C, b_sc, b_c], [b_A])
                P.op("dve", lambda e, pi=pi, p_=p_, bB=bB, bD=bD: e.tensor_tensor_scan(
                    out=bB[:, p_, :], data0=sc[:, RHO, pi:pi + 1].to_broadcast([128, NSC]), data1=bD[:, p_, :],
                    initial=ci[:, pi, :], op0=ALU.mult, op1=ALU.add), [b_Dq, b_sc, b_c], [b_Bq])
            co = slice(0, NSC) if own else slice(NSC - 1, NSC)
            rco, rso = rc[:, :, co], rs[:, :, co]
            TT(P, "dve", E1[:, :, co], bA[:, :, co], rco, ALU.mult, [b_A, b_R], [b_E])
            TT(P, "pool", bC[:, :, co], bB[:, :, co], rso, ALU.mult, [b_Bq, b_R], [b_C])
            TT(P, "dve", E1[:, :, co], E1[:, :, co], bC[:, :, co], ALU.subtract, [b_E, b_C], [b_E])
            TT(P, "dve", F1[:, :, co], bB[:, :, co], rco, ALU.mult, [b_Bq, b_R], [b_F])
            TT(P, "pool", bD[:, :, co], bA[:, :, co], rso, ALU.mult, [b_A, b_R], [b_Dq])
            TT(P, "dve", F1[:, :, co], F1[:, :, co], bD[:, :, co], ALU.add, [b_F, b_Dq], [b_F])
            if own:
                CP(P, "dve", bE[:, :, 0:1], cr[:, psl, :], [b_c], [b_E])
                CP(P, "dve", bF[:, :, 0:1], ci[:, psl, :], [b_c], [b_F])
            CP(P, "dve", cr[:, psl, :], bE[:, :, NSC:NSC + 1], [b_E], [b_c])
            CP(P, "dve", ci[:, psl, :], bF[:, :, NSC:NSC + 1], [b_F], [b_c])

        with ExitStack() as st:
            sb = lambda n, s, d: st.enter_context(nc.sbuf_tensor(_uid(n), s, d))
            ps = lambda n, s, d: st.enter_context(nc.psum_tensor(_uid(n), s, d))
            uTp = [sb("uTp%d" % i, [128, 8, UNIT], BF16) for i in range(2)]; b_uTp = [B("uTp0"), B("uTp1")]
            sets = [alloc_set(sb, ps, "p%d" % i) for i in range(2)]
            nq = 0
            for u in range(n_pre_units):
                su = u % 2
                DMA(P, "sync", uTp[su][:], uT_d[:, :, u * UNIT:(u + 1) * UNIT], b_uTp[su], b_uTd)
                for q4 in range(32 // NQ):
                    quarter(sets[nq % 2], uTp[su], b_uTp[su], q4, False)
                    nq += 1
            P.flush()

        with ExitStack() as st:
            sb = lambda n, s, d: st.enter_context(nc.sbuf_tensor(_uid(n), s, d))
            ps = lambda n, s, d: st.enter_context(nc.psum_tensor(_uid(n), s, d))
            uT = sb("uT", [128, 8, UNIT], BF16); b_uT = B("uT")
            Sx = alloc_set(sb, ps, "o")
            bE, bF, b_E, b_F = Sx["bE"], Sx["bF"], Sx["b_bE"], Sx["b_bF"]
            psH = [ps("psH%d" % i, [128, T0, NSC], F32) for i in range(2)]; b_psH = B("psH")
            psY = ps("psY", [128, UNIT], F32); b_psY = B("psY")
            hA = sb("hA", [128, T0, NSC], F32); hB = sb("hB", [128, T0, NSC], F32)
            hC = sb("hC", [128, T0, NSC], F32); b_h = B("hABC")
            Hre = [sb("Hre%d" % i, [128, UNIT], BF16) for i in range(2)]
            Him = [sb("Him%d" % i, [128, UNIT], BF16) for i in range(2)]
            b_H = [B("H0"), B("H1")]
            ysb = sb("ysb", [128, UNIT], F32); b_y = B("ysb")
            gl1 = sb("gl1", [128, UNIT], F32); gl2 = sb("gl2", [128, UNIT], F32); b_g = B("g12")
            ygT = sb("ygT", [128, 8, UNIT], BF16); b_yg = B("ygT")
            gate = sb("gate", [128, 512], F32); b_gate = B("gate")
            sso = [sb("sso%d" % i, [128, UNIT], BF16) for i in range(2)]; b_sso = [B("sso0"), B("sso1")]
            for ou in range(n_own_units):
                u = n_pre_units + ou
                DMA(P, "sync", uT[:], uT_d[:, :, u * UNIT:(u + 1) * UNIT], b_uT, b_uTd)
                u3 = uT[:].rearrange("p k (c j) -> p k j c", j=T0)
                for q4 in range(32 // NQ):
                    quarter(Sx, uT, b_uT, q4, True)
                    for h8 in range(NQ // 4):
                        pi8 = (NQ // 4) * q4 + h8
                        for pi4 in range(4):
                            pi = 4 * pi8 + pi4
                            p_ = 4 * h8 + pi4
                            r0 = 32 * pi4
                            hs = pi % 2
                            fns = []
                            for ri in range(2):
                                for i in range(T0):
                                    for tau in range(i + 1):
                                        fns.append(lambda e, r0=r0, ri=ri, i=i, tau=tau, pi8=pi8, u3=u3: e.matmul(
                                            psH[ri][:, i, :], lhsT=WBT[r0:r0 + 32, pi8, tau, ri, :],
                                            rhs=u3[r0:r0 + 32, pi8, i - tau, :], start=(tau == 0), stop=(tau == i),
                                            tile_position=(r0, 0)))
                            P.pe_group(fns, [b_WBT, b_uT], [b_psH])
                            xr_b = bE[:, p_, 0:NSC].unsqueeze(1).to_broadcast([128, T0, NSC])
                            xi_b = bF[:, p_, 0:NSC].unsqueeze(1).to_broadcast([128, T0, NSC])
                            pwr_b = pw[:, 0, 1:T0 + 1, pi].unsqueeze(2).to_broadcast([128, T0, NSC])
                            pwi_b = pw[:, 1, 1:T0 + 1, pi].unsqueeze(2).to_broadcast([128, T0, NSC])
                            hre_v = Hre[hs][:].rearrange("p (c i) -> p i c", i=T0)
                            him_v = Him[hs][:].rearrange("p (c i) -> p i c", i=T0)
                            TT(P, "pool", hA[:], xr_b, pwr_b, ALU.mult, [b_E, b_pw], [b_h])
                            TT(P, "pool", hB[:], xi_b, pwi_b, ALU.mult, [b_F, b_pw], [b_h])
                            TT(P, "pool", hA[:], hA[:], hB[:], ALU.subtract, [b_h], [b_h])
                            TT(P, "dve", hre_v, psH[0][:], hA[:], ALU.add, [b_psH, b_h], [b_H[hs]])
                            TT(P, "pool", hC[:], xi_b, pwr_b, ALU.mult, [b_F, b_pw], [b_h])
                            TT(P, "pool", hB[:], xr_b, pwi_b, ALU.mult, [b_E, b_pw], [b_h])
                            TT(P, "pool", hC[:], hC[:], hB[:], ALU.add, [b_h], [b_h])
                            TT(P, "dve", him_v, psH[1][:], hC[:], ALU.add, [b_psH, b_h], [b_H[hs]])
                            fns = []
                            for half in range(2):
                                hsl = slice(512 * half, 512 * half + 512)
                                fns.append(lambda e, hs=hs, hsl=hsl, r0=r0, pi8=pi8: e.matmul(
                                    psY[r0:r0 + 32, hsl], lhsT=WCT[:, pi8, 0, r0:r0 + 32], rhs=Hre[hs][:, hsl],
                                    start=True, stop=False, tile_position=(0, r0)))
                                fns.append(lambda e, hs=hs, hsl=hsl, r0=r0, pi8=pi8: e.matmul(
                                    psY[r0:r0 + 32, hsl], lhsT=WCT[:, pi8, 1, r0:r0 + 32], rhs=Him[hs][:, hsl],
                                    start=False, stop=True, tile_position=(0, r0)))
                            P.pe_group(fns, [b_WCT, b_H[hs]], [b_psY])
                        STT(P, "dve", ysb[:], uT[:, pi8, :], Dcol[:, pi8:pi8 + 1], psY[:], ALU.mult, ALU.add,
                            [b_uT, b_D, b_psY], [b_y])
                        ACTF(P, gl1[:], ysb[:], ACT.Square, [b_y], [b_g])
                        TS(P, "pool", gl1[:], gl1[:], 0.044715, 1.0, ALU.mult, ALU.add, [b_g], [b_g])
                        TT(P, "pool", gl1[:], gl1[:], ysb[:], ALU.mult, [b_g, b_y], [b_g])
                        ACTF(P, gl2[:], gl1[:], ACT.Sigmoid, [b_g], [b_g], scale=1.5957691216057308)
                        TT(P, "pool", ygT[:, pi8, :], gl2[:], ysb[:], ALU.mult, [b_g, b_y], [b_yg])
                for mt in range(8):
                    so = mt % 2
                    for half in range(2):
                        hsl = slice(512 * half, 512 * half + 512)
                        fns = [lambda e, kt=kt, mt=mt, hsl=hsl: e.matmul(
                            psY[:, hsl], lhsT=wglu[:, kt, 128 * mt:128 * mt + 128], rhs=ygT[:, kt, hsl],
                            start=(kt == 0), stop=(kt == 7)) for kt in range(8)]
                        P.pe_group(fns, [b_wglu, b_yg], [b_psY])
                        ACTF(P, gate[:], psY[:, hsl], ACT.Sigmoid, [b_psY, b_bglu], [b_gate], bias=bglu[:, mt:mt + 1])
                        TT(P, "dve", sso[so][:, hsl], gate[:], ygT[:, mt, hsl], ALU.mult, [b_gate, b_yg], [b_sso[so]])
                    DMA(P, "act", ssm_d[:, mt, ou * UNIT:(ou + 1) * UNIT], sso[so][:], b_ssmd, b_sso[so])
            P.flush()


def cast_weights(nc, P, pairs):
    for dst, db, src, sbuf_ in pairs:
        rows, cols = dst.shape
        step = max(1, (1 << 20) // cols)
        for r in range(0, rows, step):
            r1 = min(rows, r + step)
            DMA(P, "pool", dst[r:r1, :], src[r:r1, :], db, sbuf_)


NCT = 30
CT_Q, CT_QSW, CT_K, CT_KSW, CT_U = 0, 8, 11, 19, 22


def norm_transpose(nc, P, T, x_rows_ap, b_src, hn_out3, b_hn, gain, b_gain, s):
    xt, junk, ss, rstd, xn, pT, idt, eps_t = T["xt"], T["junk"], T["ss"], T["rstd"], T["xn"], T["pT"], T["idt"], T["eps"]
    s2 = s % 2
    bx, bj, bs, br, bxn, bpT = T["b_xt"][s], T["b_junk"], T["b_ss"][s], T["b_rstd"][s], T["b_xn"][s2], T["b_pT"][s2]
    if x_rows_ap is not None:
        DMA(P, "sync", xt[s][:], x_rows_ap, bx, b_src)
    P.op("act", lambda e: e.activation(out=junk[:], in_=xt[s][:], func=ACT.Square, accum_out=ss[:, s:s + 1]),
         [bx], [bj, bs])
    P.op("act", lambda e: e.activation(out=rstd[:, s:s + 1], in_=ss[:, s:s + 1], func=ACT.Sqrt, scale=1.0 / D,
                                       bias=eps_t[:]), [bs, T["b_eps"]], [br])
    P.op("dve", lambda e: e.reciprocal(out=rstd[:, s:s + 1], in_=rstd[:, s:s + 1]), [br], [br])
    P.op("act", lambda e: e.activation(out=xn[s2][:], in_=xt[s][:], func=ACT.Copy, scale=rstd[:, s:s + 1]),
         [bx, br], [bxn])
    P.pe_group([(lambda e, k=k: e.transpose(out=pT[s2][:, k * 128:(k + 1) * 128], in_=xn[s2][:, k * 128:(k + 1) * 128],
                                            identity=idt[:])) for k in range(KT)], [bxn, T["b_idt"]], [bpT])
    TT(P, "dve", hn_out3, pT[s2][:].rearrange("p (k c) -> p k c", k=KT),
       gain[:].unsqueeze(2).to_broadcast([128, KT, 128]), ALU.mult, [bpT, b_gain], [b_hn])


def norm_tiles(nc, sb, ps, B, nx=2):
    T = {}
    T["xt"] = [sb("xt%d" % i, [128, D], F32) for i in range(nx)]
    T["junk"] = sb("junk", [128, D], BF16)
    T["ss"] = sb("ss", [128, nx], F32); T["rstd"] = sb("rstd", [128, nx], F32)
    T["xn"] = [sb("xn%d" % i, [128, D], BF16) for i in range(2)]
    T["pT"] = [ps("pT%d" % i, [128, D], BF16) for i in range(2)]
    T["idt"] = sb("idt", [128, 128], BF16); T["eps"] = sb("eps_t", [128, 1], F32)
    T["b_xt"] = [B("xt%d" % i) for i in range(nx)]; T["b_junk"] = B("junk"); T["b_ss"] = [B("ss%d" % i) for i in range(nx)]
    T["b_rstd"] = [B("r%d" % i) for i in range(nx)]; T["b_xn"] = [B("xn0"), B("xn1")]; T["b_pT"] = [B("pT0"), B("pT1")]
    T["b_idt"] = B("idt"); T["b_eps"] = B("eps")
    return T


def front_stage(nc, P, dr, n_blocks, first_kv, first_q):
    from contextlib import ExitStack
    B = Buf
    ext = dr["ext"]
    with ExitStack() as st:
        sb = lambda n, s, d: st.enter_context(nc.sbuf_tensor(_uid(n), s, d))
        ps = lambda n, s, d: st.enter_context(nc.psum_tensor(_uid(n), s, d))
        T = norm_tiles(nc, sb, ps, B, nx=4)
        DMA(P, "sync", T["idt"][:], dr["ident_bf"][0], T["b_idt"], ext)
        P.op("dve", lambda e: e.memset(T["eps"][:], 1e-6), [], [T["b_eps"]])
        g1t = sb("g1t", [128, KT], F32); b_g1 = B("g1t")
        DMA(P, "sync", g1t[:], dr["g1"][0], b_g1, ext)
        hnT = [sb("hnT%d" % i, [128, KT, BLK], BF16) for i in range(2)]; b_hn = [B("hn0"), B("hn1")]
        wu = sb("wu", [128, 8, KT, 128], BF16); b_wu = B("wu")
        wfm_bf, b_wfm = dr["wfm_bf"]
        DMA(P, "sync", wu[:], wfm_bf[CT_U:CT_U + 8].rearrange("c p k m -> p c k m"), b_wu, b_wfm)
        ublk = [sb("ublk%d" % i, [128, 8, BLK], BF16) for i in range(2)]; b_ub = [B("ub0"), B("ub1")]
        NWS = 2
        wst = [sb("wst%d" % i, [128, 4 * KT * 128], BF16) for i in range(NWS)]; b_wst = [B("wst%d" % i) for i in range(NWS)]
        pP = [ps("pP%d" % i, [128, BLK], F32) for i in range(3)]; b_pP = [B("pP%d" % i) for i in range(3)]
        cosb = sb("cosb", [128, 3, BLK], F32); sinb = sb("sinb", [128, 3, BLK], F32); b_cs = B("cossin")
        swsin = sb("swsin", [128, 3, BLK], F32); b_sw = B("swsin")
        rtmp = sb("rtmp", [128, BLK], F32); b_rt = B("rtmp")
        rtmp2 = sb("rtmp2", [128, BLK], F32); b_rt2 = B("rtmp2")
        qkblk = [sb("qkblk%d" % i, [128, 8, BLK], BF16) for i in range(2)]; b_qk = [B("qk0"), B("qk1")]
        vrow = sb("vrow", [128, 4, 1024], BF16); b_vr = B("vrow")
        xpad, b_x = dr["xpad"]
        uT_d, b_uTd = dr["uT_d"]
        wv_bf, b_wv = dr["wv_bf"]
        st_ = {"np": 0, "nw": 0, "nqk": 0}

        def proj_tile(lhs_w, hn, b_w, b_h):
            s = st_["np"] % 3; st_["np"] += 1
            P.pe_group([(lambda e, k=k, s=s: e.matmul(pP[s][:], lhsT=lhs_w[:, k, :], rhs=hn[:, k, :],
                                                      start=(k == 0), stop=(k == KT - 1))) for k in range(KT)],
                       [b_w, b_h], [b_pP[s]])
            return s

        def load_chunk(ct0, n):
            s = st_["nw"] % NWS; st_["nw"] += 1
            v4 = wst[s][:, 0:n * KT * 128].rearrange("p (c k m) -> p c k m", c=n, k=KT)
            DMA(P, "sync", v4, wfm_bf[ct0:ct0 + n].rearrange("c p k m -> p c k m"), b_wst[s], b_wfm)
            return s, v4

        def qk_proj(hs, ct_main, ct_sw, out_d, b_outd, col0):
            hn = hnT[hs]
            s, v4 = load_chunk(ct_sw, 3)
            for j in range(3):
                sp = proj_tile(v4[:, j], hn[:], b_wst[s], b_hn[hs])
                CP(P, "act", swsin[:, j, :], pP[sp][:], [b_pP[sp]], [b_sw])
            qs = st_["nqk"] % 2; st_["nqk"] += 1
            for c4 in range(2):
                s, v4 = load_chunk(ct_main + 4 * c4, 4)
                for j in range(4):
                    h = 4 * c4 + j
                    jb = h % 3
                    sp = proj_tile(v4[:, j], hn[:], b_wst[s], b_hn[hs])
                    TT(P, "dve", rtmp[:], pP[sp][:], cosb[:, jb, :], ALU.mult, [b_pP[sp], b_cs], [b_rt])
                    TT(P, "pool", rtmp2[:], swsin[:, h // 3, :], sinb[:, jb, :], ALU.mult, [b_sw, b_cs], [b_rt2])
                    TT(P, "pool", qkblk[qs][:, h, :], rtmp[:], rtmp2[:], ALU.add, [b_rt, b_rt2], [b_qk[qs]])
            DMA(P, "pool", out_d[:, :, col0:col0 + BLK], qkblk[qs][:], b_outd, b_qk[qs])

        nt = 0
        for b in range(n_blocks):
            hs = b % 2
            for t in range(4):
                s = nt % 4; nt += 1
                row = b * BLK + t * 128
                norm_transpose(nc, P, T, xpad[row:row + 128, :], b_x, hnT[hs][:, :, t * 128:(t + 1) * 128], b_hn[hs],
                               g1t, b_g1, s)
            us = b % 2
            for ct in range(8):
                sp = proj_tile(wu[:, ct], hnT[hs][:], b_wu, b_hn[hs])
                CP(P, "act", ublk[us][:, ct, :], pP[sp][:], [b_pP[sp]], [b_ub[us]])
            DMA(P, "pool", uT_d[:, :, b * BLK:(b + 1) * BLK], ublk[us][:], b_uTd, b_ub[us])
            if b < first_kv:
                continue
            wb = b - first_kv
            DMA(P, "sync", cosb[:], dr["cos_d"][0][:, :, wb * BLK:(wb + 1) * BLK], b_cs, ext)
            DMA(P, "sync", sinb[:], dr["sin_d"][0][:, :, wb * BLK:(wb + 1) * BLK], b_cs, ext)
            if b >= first_q:
                qk_proj(hs, CT_Q, CT_QSW, dr["qT_d"][0], dr["qT_d"][1], (b - first_q) * BLK)
            qk_proj(hs, CT_K, CT_KSW, dr["kT_d"][0], dr["kT_d"][1], wb * BLK)
            for half in range(2):
                if "v" in SKIP:
                    break
                s = st_["nw"] % NWS; st_["nw"] += 1
                vv = wst[s][:, 0:KT * 512].rearrange("p (k n) -> p k n", k=KT)
                DMA(P, "sync", vv, wv_bf[:, :, half * 512:(half + 1) * 512], b_wst[s], b_wv)
                for t in range(4):
                    sp = st_["np"] % 3; st_["np"] += 1
                    P.pe_group([(lambda e, k=k, sp=sp, t=t, vv=vv, hs=hs: e.matmul(
                        pP[sp][:], lhsT=hnT[hs][:, k, t * 128:(t + 1) * 128], rhs=vv[:, k, :],
                        start=(k == 0), stop=(k == KT - 1))) for k in range(KT)], [b_wst[s], b_hn[hs]], [b_pP[sp]])
                    CP(P, "act", vrow[:, t, half * 512:(half + 1) * 512], pP[sp][:], [b_pP[sp]], [b_vr])
            if "v" not in SKIP:
                DMA(P, "pool", dr["V_d"][0][wb * BLK:(wb + 1) * BLK, :].rearrange("(t p) c -> p t c", p=128), vrow[:],
                    dr["V_d"][1], b_vr)
        P.flush()


def attn_stage(nc, P, dr):
    from contextlib import ExitStack
    B = Buf
    ext = dr["ext"]
    W = 2 * NTOK
    SC = 128 ** -0.5
    with ExitStack() as st:
        sb = lambda n, s, d: st.enter_context(nc.sbuf_tensor(_uid(n), s, d))
        ps = lambda n, s, d: st.enter_context(nc.psum_tensor(_uid(n), s, d))
        cs = sb("cs", [128, 4, 128], BF16); b_c = B("cs")
        DMA(P, "sync", cs[:], dr["aconsts"][0], b_c, ext)
        hm = sb("hm", [128, 1], F32); b_hm = B("hm")
        DMA(P, "sync", hm[:], dr["hmask"][0], b_hm, ext)
        qT = [sb("qTh%d" % i, [128, NTOK], BF16) for i in range(2)]; b_q = [B("q0"), B("q1")]
        kT = [sb("kTh%d" % i, [128, W], BF16) for i in range(2)]; b_k = [B("k0"), B("k1")]
        num = sb("num", [128, NTOK], F32); den = sb("den", [128, NTOK], F32); b_num, b_den = B("num"), B("den")
        outb = [sb("outb%d" % i, [128, NTOK], BF16) for i in range(2)]; b_ob = [B("ob0"), B("ob1")]
        NV = 6
        vt = [sb("vt%d" % i, [128, 128], BF16) for i in range(NV)]; b_vt = [B("vt%d" % i) for i in range(NV)]
        pt = [sb("pt%d" % i, [128, 128], BF16) for i in range(4)]; b_pt = [B("pt%d" % i) for i in range(4)]
        pS = [ps("pS%d" % i, [128, 128], F32) for i in range(4)]; b_pS = [B("pS%d" % i) for i in range(4)]
        pO = [ps("pO%d" % i, [128, 128], F32) for i in range(2)]; b_pO = [B("pO0"), B("pO1")]
        pL = [ps("pL%d" % i, [128, 128], F32) for i in range(2)]; b_pL = [B("pL0"), B("pL1")]
        qT_d, b_qd = dr["qT_d"]; kT_d, b_kd = dr["kT_d"]; V_d, b_vd = dr["V_d"]; mix_d, b_mix = dr["mix_d"]
        iv = 0; ip = 0; blk = 0
        for h in range(8):
            hs = h % 2
            DMA(P, "sync", qT[hs][:], qT_d[:, h, :], b_q[hs], b_qd)
            DMA(P, "sync", kT[hs][:], kT_d[:, h, :], b_k[hs], b_kd)
            for pi_, dil in enumerate((1, 4, 16)):
                nb_all = W // dil // 128
                nb0 = nb_all // 2
                for n in range(nb0, nb_all):
                    for r in range(dil):
                        def wpos(nn):
                            s0 = nn * 128 * dil + r
                            return slice(s0, s0 + 127 * dil + 1, dil)
                        pq_w = wpos(n)
                        pq = slice(pq_w.start - NTOK, pq_w.stop - NTOK, dil)
                        so = blk % 2; blk += 1
                        slots = []
                        for (kn, mi) in ((n, 1), (n - 1, 2)):
                            pk = wpos(kn)
                            halo = kn < nb0
                            sv = iv % NV; iv += 1
                            sp = ip % 4; ip += 1
                            slots.append((sv, sp))
                            DMA(P, "sync", vt[sv][:], V_d[pk, 128 * h:128 * h + 128], b_vt[sv], b_vd)
                            P.pe_group([lambda e, sp=sp, pk=pk, pq=pq, hs=hs: e.matmul(
                                            pS[sp][:], lhsT=kT[hs][:, pk], rhs=qT[hs][:, pq], start=True, stop=False),
                                        lambda e, sp=sp, mi=mi: e.matmul(
                                            pS[sp][:], lhsT=cs[:, 0, :], rhs=cs[:, mi, :], start=False, stop=True)],
                                       [b_k[hs], b_q[hs], b_c], [b_pS[sp]])
                            if halo:
                                ACTF(P, pt[sp][:], pS[sp][:], ACT.Exp, [b_pS[sp], b_hm], [b_pt[sp]], scale=SC, bias=hm[:])
                            else:
                                ACTF(P, pt[sp][:], pS[sp][:], ACT.Exp, [b_pS[sp]], [b_pt[sp]], scale=SC)
                        P.pe_group([lambda e, sv=sv, sp=sp, i=i, so=so: e.matmul(
                            pO[so][:], lhsT=vt[sv][:], rhs=pt[sp][:], start=(i == 0), stop=(i == 1))
                            for i, (sv, sp) in enumerate(slots)],
                            [b_vt[sv] for sv, _ in slots] + [b_pt[sp] for _, sp in slots], [b_pO[so]])
                        P.pe_group([lambda e, sp=sp, i=i, so=so: e.matmul(
                            pL[so][:], lhsT=cs[:, 3, :], rhs=pt[sp][:], start=(i == 0), stop=(i == 1))
                            for i, (_, sp) in enumerate(slots)], [b_c] + [b_pt[sp] for _, sp in slots], [b_pL[so]])
                        if pi_ == 0:
                            CP(P, "dve", num[:, pq], pO[so][:], [b_pO[so]], [b_num])
                            CP(P, "act", den[:, pq], pL[so][:], [b_pL[so]], [b_den])
                        else:
                            TT(P, "dve", num[:, pq], pO[so][:], num[:, pq], ALU.add, [b_pO[so], b_num], [b_num])
                            TT(P, "dve", den[:, pq], pL[so][:], den[:, pq], ALU.add, [b_pL[so], b_den], [b_den])
            P.op("dve", lambda e: e.reciprocal(out=den[:], in_=den[:]), [b_den], [b_den])
            TT(P, "dve", outb[hs][:], num[:], den[:], ALU.mult, [b_num, b_den], [b_ob[hs]])
            DMA(P, "pool", mix_d[:, h, :], outb[hs][:], b_mix, b_ob[hs])
        P.flush()


def tail1_stage(nc, P, dr, x_row0):
    from contextlib import ExitStack
    B = Buf
    ext = dr["ext"]
    with ExitStack() as st:
        sb = lambda n, s, d: st.enter_context(nc.sbuf_tensor(_uid(n), s, d))
        ps = lambda n, s, d: st.enter_context(nc.psum_tensor(_uid(n), s, d))
        T = norm_tiles(nc, sb, ps, B)
        DMA(P, "sync", T["idt"][:], dr["ident_bf"][0], T["b_idt"], ext)
        P.op("dve", lambda e: e.memset(T["eps"][:], 1e-6), [], [T["b_eps"]])
        g2t = sb("g2t_", [128, KT], F32); b_g2 = B("g2t")
        DMA(P, "sync", g2t[:], dr["g2"][0], b_g2, ext)
        wout = sb("wout", [128, KT, D], BF16); b_wo = B("wout")
        DMA(P, "sync", wout[:], dr["wout_bf"][0], b_wo, dr["wout_bf"][1])
        mixt = [sb("mixt%d" % i, [128, KT, 128], BF16) for i in range(2)]; b_mt = [B("mt0"), B("mt1")]
        xin = [sb("xin%d" % i, [128, D], F32) for i in range(2)]; b_xin = [B("xin0"), B("xin1")]
        hn2b = [sb("hn2b%d" % i, [128, KT, 128], BF16) for i in range(2)]; b_h2 = [B("h2b0"), B("h2b1")]
        pH = ps("pH", [128, D], F32); b_pH = B("pH")
        xpad, b_x = dr["xpad"]; mix_d, b_mix = dr["mix_d"]; h_d, b_hd = dr["h_d"]; hn2_d, b_hn2d = dr["hn2T_d"]
        for i in range(NTOK // 128):
            s = i % 2
            DMA(P, "sync", mixt[s][:], mix_d[:, :, i * 128:(i + 1) * 128], b_mt[s], b_mix)
            DMA(P, "sync", xin[s][:], xpad[x_row0 + i * 128:x_row0 + (i + 1) * 128, :], b_xin[s], b_x)
            fns = []
            for fb in range(4):
                for k in range(KT):
                    fns.append(lambda e, s=s, fb=fb, k=k: e.matmul(
                        pH[:, fb * 512:(fb + 1) * 512], lhsT=mixt[s][:, k, :], rhs=wout[:, k, fb * 512:(fb + 1) * 512],
                        start=(k == 0), stop=(k == KT - 1)))
            P.pe_group(fns, [b_mt[s], b_wo], [b_pH])
            TT(P, "dve", T["xt"][s][:], pH[:], xin[s][:], ALU.add, [b_pH, b_xin[s]], [T["b_xt"][s]])
            DMA(P, "pool", h_d[i * 128:(i + 1) * 128, :], T["xt"][s][:], b_hd, T["b_xt"][s])
            norm_transpose(nc, P, T, None, None, hn2b[s][:], b_h2[s], g2t, b_g2, s)
            DMA(P, "pool", hn2_d[:, :, i * 128:(i + 1) * 128], hn2b[s][:], b_hn2d, b_h2[s])
        P.flush()


DFF = 5632
NF = DFF // 128


def tail2_stage(nc, P, dr):
    from contextlib import ExitStack
    B = Buf
    ext = dr["ext"]
    with ExitStack() as st:
        sb = lambda n, s, d: st.enter_context(nc.sbuf_tensor(_uid(n), s, d))
        ps = lambda n, s, d: st.enter_context(nc.psum_tensor(_uid(n), s, d))
        hn2 = [sb("hn2s%d" % i, [128, KT, BLK], BF16) for i in range(2)]; b_hn2 = [B("hn2s0"), B("hn2s1")]
        HT = sb("HT", [128, NF, BLK], BF16); b_HT = B("HT")
        wst = [sb("wst2_%d" % i, [128, KT * 512], BF16) for i in range(3)]; b_wst = [B("w2_%d" % i) for i in range(3)]
        gsig = sb("gsig", [128, BLK], F32); b_gs = B("gsig")
        hin = [sb("hin%d" % i, [128, D], F32) for i in range(4)]; b_hin = [B("hin%d" % i) for i in range(4)]
        junk = sb("junk2", [128, D], F32); b_junk = B("junk2")
        gF = sb("gF", [128, D], F32); b_gF = B("gF")
        DMA(P, "sync", gF[:], dr["final_g"][0].partition_broadcast(128), b_gF, ext)
        ss = sb("ss2", [128, 4], F32); rstd = sb("rstd2", [128, 4], F32); b_ss = [B("ss2_%d" % i) for i in range(4)]
        eps_t = sb("eps2", [128, 1], F32); b_eps = B("eps2")
        P.op("dve", lambda e: e.memset(eps_t[:], 1e-6), [], [b_eps])
        pG = [ps("pG%d" % i, [128, BLK], F32) for i in range(2)]; b_pG = [B("pG0"), B("pG1")]
        pU = [ps("pU%d" % i, [128, BLK], F32) for i in range(2)]; b_pU = [B("pU0"), B("pU1")]
        pD = [ps("pD%d" % i, [128, 1024], F32) for i in range(2)]; b_pD = [B("pD0"), B("pD1")]
        hn2_d, b_hn2d = dr["hn2T_d"]; h_d, b_hd = dr["h_d"]; y_d, b_yd = dr["y"]
        wg_bf, b_wg = dr["wgate_bf"]; wu_bf, b_wub = dr["wup_bf"]; wd_bf, b_wd = dr["wdown_bf"]
        nw = 0
        for sbk in range(NTOK // BLK):
            hs = sbk % 2
            DMA(P, "sync", hn2[hs][:], hn2_d[:, :, sbk * BLK:(sbk + 1) * BLK], b_hn2[hs], b_hn2d)
            for fc in range(NF // 4):
                sg = nw % 3; nw += 1
                vg = wst[sg][:].rearrange("p (k n) -> p k n", k=KT)
                DMA(P, "sync", vg, wg_bf[:, :, fc * 512:(fc + 1) * 512], b_wst[sg], b_wg)
                su = nw % 3; nw += 1
                vu = wst[su][:].rearrange("p (k n) -> p k n", k=KT)
                DMA(P, "sync", vu, wu_bf[:, :, fc * 512:(fc + 1) * 512], b_wst[su], b_wub)
                for j in range(4):
                    f = 4 * fc + j
                    s = f % 2
                    P.pe_group([(lambda e, k=k, s=s, j=j, vg=vg, hs=hs: e.matmul(
                        pG[s][:], lhsT=vg[:, k, j * 128:(j + 1) * 128], rhs=hn2[hs][:, k, :],
                        start=(k == 0), stop=(k == KT - 1))) for k in range(KT)], [b_wst[sg], b_hn2[hs]], [b_pG[s]])
                    P.pe_group([(lambda e, k=k, s=s, j=j, vu=vu, hs=hs: e.matmul(
                        pU[s][:], lhsT=vu[:, k, j * 128:(j + 1) * 128], rhs=hn2[hs][:, k, :],
                        start=(k == 0), stop=(k == KT - 1))) for k in range(KT)], [b_wst[su], b_hn2[hs]], [b_pU[s]])
                    ACTF(P, gsig[:], pG[s][:], ACT.Silu, [b_pG[s]], [b_gs])
                    TT(P, "dve", HT[:, f, :], pU[s][:], gsig[:], ALU.mult, [b_pU[s], b_gs], [b_HT])
            for t in range(4):
                i = sbk * 4 + t
                DMA(P, "sync", hin[t][:], h_d[i * 128:(i + 1) * 128, :], b_hin[t], b_hd)
            for fc in range(NF // 4):
                sd = nw % 3; nw += 1
                vd = wst[sd][:].rearrange("p (f n) -> p f n", f=4)
                DMA(P, "sync", vd, wd_bf[:, fc * 4:(fc + 1) * 4, :], b_wst[sd], b_wd)
                for t in range(4):
                    for hf in range(2):
                        fns = []
                        for j in range(4):
                            f = 4 * fc + j
                            for fb2 in range(2):
                                fb = 2 * hf + fb2
                                fns.append(lambda e, j=j, f=f, fb=fb, fb2=fb2, hf=hf, vd=vd, t=t: e.matmul(
                                    pD[hf][:, fb2 * 512:(fb2 + 1) * 512], lhsT=HT[:, f, t * 128:(t + 1) * 128],
                                    rhs=vd[:, j, fb * 512:(fb + 1) * 512], start=(j == 0), stop=(j == 3)))
                        P.pe_group(fns, [b_HT, b_wst[sd]], [b_pD[hf]])
                        hsl = slice(1024 * hf, 1024 * hf + 1024)
                        TT(P, "dve", hin[t][:, hsl], pD[hf][:], hin[t][:, hsl], ALU.add, [b_pD[hf], b_hin[t]], [b_hin[t]])
            for t in range(4):
                i = sbk * 4 + t
                s2 = t
                P.op("act", lambda e, s2=s2: e.activation(out=junk[:], in_=hin[s2][:], func=ACT.Square,
                                                          accum_out=ss[:, s2:s2 + 1]), [b_hin[s2]], [b_junk, b_ss[s2]])
                P.op("act", lambda e, s2=s2: e.activation(out=rstd[:, s2:s2 + 1], in_=ss[:, s2:s2 + 1], func=ACT.Sqrt,
                                                          scale=1.0 / D, bias=eps_t[:]), [b_ss[s2], b_eps], [b_ss[s2]])
                P.op("dve", lambda e, s2=s2: e.reciprocal(out=rstd[:, s2:s2 + 1], in_=rstd[:, s2:s2 + 1]),
                     [b_ss[s2]], [b_ss[s2]])
                STT(P, "dve", hin[s2][:], hin[s2][:], rstd[:, s2:s2 + 1], gF[:], ALU.mult, ALU.mult,
                    [b_hin[s2], b_ss[s2], b_gF], [b_hin[s2]])
                DMA(P, "pool", y_d[i * 128:(i + 1) * 128, :], hin[s2][:], b_yd, b_hin[s2])
        P.flush(final_bufs=[b_yd])


N_PRE_UNITS = (SEQ - NTOK) // UNIT
N_OWN_UNITS = NTOK // UNIT
FIRST_KV_BLK = (SEQ - 2 * NTOK) // BLK
FIRST_Q_BLK = (SEQ - NTOK) // BLK

_IN_SPECS = [
    ("xpad", [SEQ, D], F32), ("wfm32", [NCT * 128, KT * 128], F32), ("wv32", [128, KT * 1024], F32),
    ("wglu32", [128, 8 * 1024], F32), ("wout32", [128, KT * D], F32), ("wgate32", [128, KT * DFF], F32),
    ("wup32", [128, KT * DFF], F32), ("wdown32", [128, NF * D], F32), ("g1", [128, KT], F32), ("g2", [128, KT], F32),
    ("final_g", [D], F32), ("a_re", [64, 64], F32), ("a_im", [64, 64], F32), ("log_dt", [64], F32),
    ("b_re", [64, 64, 16], F32), ("b_im", [64, 64, 16], F32), ("c_re", [64, 16, 64], F32), ("c_im", [64, 16, 64], F32),
    ("d_skip", [64, 16], F32), ("b_glu", [1024], F32), ("ident32", [128, 128], F32), ("ident_bf", [128, 128], BF16),
    ("aconsts", [128, 4, 128], BF16), ("hmask", [128, 1], F32), ("cos_d", [128, 3, 2 * NTOK], F32),
    ("sin_d", [128, 3, 2 * NTOK], F32),
]


def build_program():
    from contextlib import ExitStack
    nc = bass.Bass("TRN2", target_bir_lowering=False)
    ext = Buf("ext", False)
    dr = {"ext": ext}
    for name, shape, dt in _IN_SPECS:
        dr[name] = (nc.dram_tensor(name, shape, dt, kind="ExternalInput").ap(), ext)
    dr["y"] = (nc.dram_tensor("y", [NTOK, D], F32, kind="ExternalOutput").ap(), Buf("y", False))

    def scratch(name, shape, dt, keep=False):
        dr[name] = (nc.dram_tensor(name, shape, dt).ap(), Buf(name, False, keep))
    scratch("wfm_bf2", [NCT * 128, KT * 128], BF16); scratch("wv_bf2", [128, KT * 1024], BF16)
    scratch("wglu_bf2", [128, 8 * 1024], BF16); scratch("wout_bf2", [128, KT * D], BF16)
    scratch("wgate_bf2", [128, KT * DFF], BF16); scratch("wup_bf2", [128, KT * DFF], BF16)
    scratch("wdown_bf2", [128, NF * D], BF16)
    scratch("uT_d", [128, 8, SEQ], BF16); scratch("qT_d", [128, 8, NTOK], BF16); scratch("kT_d", [128, 8, 2 * NTOK], BF16)
    scratch("V_d", [2 * NTOK, 1024], BF16); scratch("mix_d", [128, 16, NTOK], BF16)
    scratch("h_d", [NTOK, D], F32); scratch("hn2T_d", [128, KT, NTOK], BF16)
    dr["wfm_bf"] = (dr["wfm_bf2"][0].rearrange("(c p) (k m) -> c p k m", p=128, k=KT), dr["wfm_bf2"][1])
    dr["wv_bf"] = (dr["wv_bf2"][0].rearrange("p (k n) -> p k n", k=KT), dr["wv_bf2"][1])
    dr["wglu_bf"] = (dr["wglu_bf2"][0].rearrange("p (k n) -> p k n", k=8), dr["wglu_bf2"][1])
    dr["wout_bf"] = (dr["wout_bf2"][0].rearrange("p (k n) -> p k n", k=KT), dr["wout_bf2"][1])
    dr["wgate_bf"] = (dr["wgate_bf2"][0].rearrange("p (k n) -> p k n", k=KT), dr["wgate_bf2"][1])
    dr["wup_bf"] = (dr["wup_bf2"][0].rearrange("p (k n) -> p k n", k=KT), dr["wup_bf2"][1])
    dr["wdown_bf"] = (dr["wdown_bf2"][0].rearrange("p (f n) -> p f n", f=NF), dr["wdown_bf2"][1])
    dr["ssm_d"] = (dr["mix_d"][0][:, 8:16, :], dr["mix_d"][1])
    with ExitStack() as st:
        P = Prog(nc, st)
        pairs = []
        for nm in ("wfm", "wv", "wglu", "wout", "wgate", "wup", "wdown"):
            pairs.append((dr[nm + "_bf2"][0], dr[nm + "_bf2"][1], dr[nm + "32"][0], Buf(nm + "32", False, keep=True)))
        cast_weights(nc, P, pairs)
        front_stage(nc, P, dr, NBLK_ALL, FIRST_KV_BLK, FIRST_Q_BLK)
        s5_stage(nc, P, dr, N_PRE_UNITS, N_OWN_UNITS)
        attn_stage(nc, P, dr)
        tail1_stage(nc, P, dr, SEQ - NTOK)
        tail2_stage(nc, P, dr)
    return nc


def _tile_rows(w, kt):
    n = w.shape[1]
    return np.ascontiguousarray(w.reshape(kt, 128, n).transpose(1, 0, 2)).reshape(128, kt * n)


def _head_perm(h):
    j = h % 3
    perm = np.zeros(128, np.int64)
    for m in range(128):
        if 32 * j <= m < 32 * j + 32:
            perm[m] = m - 32 * j
        elif m < 32 * j:
            perm[m] = 32 + m
        else:
            perm[m] = m
    return perm


def _prep_shared(inp):
    f32 = np.float32
    w_in = np.asarray(inp["w_in"], f32)[0]
    cols = np.full((NCT, 128), -1, np.int64)
    for base, ct0, ctsw in ((0, CT_Q, CT_QSW), (1024, CT_K, CT_KSW)):
        for h in range(8):
            cols[ct0 + h] = base + h * 128 + _head_perm(h)
            tt, j = h // 3, h % 3
            for i in range(32):
                cols[ctsw + tt, 32 * j + i] = base + h * 128 + (i + 16) % 32
    for k in range(8):
        cols[CT_U + k] = 3072 + k * 128 + np.arange(128)
    flat = cols.reshape(-1)
    wsel = np.where(flat[None, :] >= 0, w_in[:, np.maximum(flat, 0)], 0.0).astype(f32)
    wfm = wsel.reshape(KT, 128, NCT, 128).transpose(2, 1, 0, 3)
    sh = {}
    sh["wfm32"] = np.ascontiguousarray(wfm).reshape(NCT * 128, KT * 128)
    sh["wv32"] = _tile_rows(np.ascontiguousarray(w_in[:, 2048:3072]), KT)
    sh["wglu32"] = _tile_rows(np.asarray(inp["w_glu"], f32)[0], 8)
    sh["wout32"] = _tile_rows(np.asarray(inp["w_out"], f32)[0], KT)
    sh["wgate32"] = _tile_rows(np.asarray(inp["w_gate"], f32)[0], KT)
    sh["wup32"] = _tile_rows(np.asarray(inp["w_up"], f32)[0], KT)
    sh["wdown32"] = _tile_rows(np.asarray(inp["w_down"], f32)[0], NF)
    sh["g1"] = np.ascontiguousarray(np.asarray(inp["norm1_g"], f32)[0].reshape(KT, 128).T)
    sh["g2"] = np.ascontiguousarray(np.asarray(inp["norm2_g"], f32)[0].reshape(KT, 128).T)
    sh["final_g"] = np.ascontiguousarray(np.asarray(inp["final_g"], f32))
    for nm in ("a_re", "a_im", "log_dt", "b_re", "b_im", "c_re", "c_im", "d_skip", "b_glu"):
        sh[nm] = np.ascontiguousarray(np.asarray(inp[nm], f32)[0])
    sh["ident32"] = np.eye(128, dtype=f32)
    sh["ident_bf"] = np.eye(128, dtype=f32).astype(ml_dtypes.bfloat16)
    kk = np.arange(128)[:, None]; qq = np.arange(128)[None, :]
    mcur = np.where(kk <= qq, 0.0, -30000.0); mprev = np.where(kk >= qq, 0.0, -30000.0)
    sh["aconsts"] = np.ascontiguousarray(
        np.stack([np.eye(128), mcur, mprev, np.ones((128, 128))], 1).astype(f32).astype(ml_dtypes.bfloat16))
    return sh


def _rope_tables(t0):
    f32 = np.float32
    pos = (np.arange(2 * NTOK) + (t0 - NTOK)).astype(f32)
    pos = np.maximum(pos, f32(0))
    inv_freq = (f32(500000.0) ** (-(np.arange(0, 32, 2).astype(f32)) / f32(32))).astype(f32)
    i = np.arange(32)
    ang = (pos[None, :] * inv_freq[i % 16][:, None]).astype(f32)
    c32 = np.cos(ang).astype(f32)
    s32 = np.sin(ang).astype(f32) * np.where(i < 16, -1.0, 1.0).astype(f32)[:, None]
    cosT = np.ones((128, 3, 2 * NTOK), f32)
    sinT = np.zeros((128, 3, 2 * NTOK), f32)
    for j in range(3):
        cosT[32 * j:32 * j + 32, j, :] = c32
        sinT[32 * j:32 * j + 32, j, :] = s32
    return cosT, sinT


def kernel(**inputs):
    x = np.asarray(inputs["x"], np.float32)[0]
    sh = _prep_shared(inputs)
    nc = build_program()
    in_maps = []
    for c in range(NCORES):
        t0 = c * NTOK
        xpad = np.zeros((SEQ, D), np.float32)
        n_real = t0 + NTOK
        xpad[SEQ - n_real:] = x[:n_real]
        cosT, sinT = _rope_tables(t0)
        m = dict(sh)
        m["xpad"] = xpad
        m["cos_d"] = cosT
        m["sin_d"] = sinT
        m["hmask"] = np.full((128, 1), -30000.0 if c == 0 else 0.0, np.float32)
        in_maps.append(m)
    res = run_bass_kernel_spmd(nc, in_maps, core_ids=list(range(NCORES)))
    y = np.concatenate([np.asarray(res.results[c]["y"], np.float32) for c in range(NCORES)], axis=0)
    return y.reshape(1, SEQ, D)
```

```python
import math
import numpy as np
import ml_dtypes
import concourse.bass as bass
import concourse.mybir as mybir
from concourse.bass_utils import run_bass_kernel_spmd

F32 = mybir.dt.float32
BF16 = mybir.dt.bfloat16
ALU = mybir.AluOpType
ACT = mybir.ActivationFunctionType
AX = mybir.AxisListType

NCORES = 8
D = 2048
SEQ = 16384
NTOK = SEQ // NCORES
BLK = 512
NBLK_ALL = SEQ // BLK
KT = D // 128

ENGS = ("sync", "act", "pool", "dve", "pe")


TWO_PI = 2.0 * math.pi
_UID = [0]
SKIP = set()
PROFILE = False


def _uid(n):
    _UID[0] += 1
    return "%s_%d" % (n, _UID[0])


class Buf:
    __slots__ = ("name", "w", "r", "dsem", "sb", "keep")

    def __init__(self, name, sb=True, keep=False):
        self.name = name
        self.w = None
        self.r = {}
        self.dsem = None
        self.sb = sb
        self.keep = keep


class Prog:
    def __init__(self, nc, stack, ndsem=80):
        self.nc = nc
        self.csem = {k: stack.enter_context(nc.semaphore("s_" + k)) for k in ("c_act", "c_pool", "c_dve", "c_pe")}
        self.dsem = [stack.enter_context(nc.semaphore("sd%d" % i)) for i in range(ndsem)]
        self.dval = [0] * ndsem
        self.dfree = list(range(ndsem))
        self.dkeep = set()
        self.ops = {e: [] for e in ENGS}
        self.cnt = {e: 0 for e in ENGS}
        self.known = {e: {} for e in ENGS}
        self.bufs = []
        self.nblocks = 0

    def _reg(self, b):
        if b not in self.bufs:
            self.bufs.append(b)

    def _deps(self, eng, reads, writes, skip_same_pe=False):
        need = {}

        def add(tok):
            if tok is None:
                return
            k, v = tok
            if skip_same_pe and k == "c_pe":
                return
            if need.get(k, 0) < v:
                need[k] = v
        for b in reads:
            add(b.w)
        for b in writes:
            add(b.w)
            for k, v in b.r.items():
                add((k, v))
        waits = []
        kn = self.known[eng]
        for k, v in need.items():
            if kn.get(k, 0) < v:
                kn[k] = v
                waits.append((k, v))
        return waits

    def _mark(self, tok, reads, writes):
        for b in reads:
            b.r[tok[0]] = tok[1]
            self._reg(b)
        for b in writes:
            b.w = tok
            b.r = {}
            self._reg(b)

    def op(self, eng, fn, reads=(), writes=()):
        waits = self._deps(eng, reads, writes)
        self.cnt[eng] += 1
        tok = ("c_" + eng, self.cnt[eng])
        self.ops[eng].append((waits, fn, (tok[0], 1)))
        self._mark(tok, reads, writes)

    def pe_group(self, fns, reads=(), writes=()):
        waits = self._deps("pe", reads, writes, skip_same_pe=True)
        self.cnt["pe"] += 1
        tok = ("c_pe", self.cnt["pe"])
        n = len(fns)
        for i, fn in enumerate(fns):
            self.ops["pe"].append((waits if i == 0 else [], fn, (tok[0], 1) if i == n - 1 else None))
        self._mark(tok, reads, writes)

    def dma(self, q, fn, dst, src):
        waits = self._deps(q, [src], [dst])
        key = dst if dst.sb else src
        if key.dsem is None:
            key.dsem = self.dfree.pop(0)
            if key.keep:
                self.dkeep.add(key.dsem)
        i = key.dsem
        self.dval[i] += 16
        tok = (i, self.dval[i])
        self.ops[q].append((waits, fn, (i, 16)))
        src.r[tok[0]] = tok[1]
        dst.w = tok
        dst.r = {}
        self._reg(src)
        self._reg(dst)
        self._reg(key)

    def wait_all(self, eng, bufs):
        waits = self._deps(eng, bufs, [])
        self.ops[eng].append((waits, None, None))

    def _sem(self, k):
        return self.csem[k] if isinstance(k, str) else self.dsem[k]

    def flush(self, final_bufs=(), scope=None):
        if PROFILE and scope:
            with self.nc.named_scope(scope):
                return self._flush(final_bufs)
        return self._flush(final_bufs)

    def _flush(self, final_bufs=()):
        kn = self.known["sync"]
        waits = []
        for i, v in enumerate(self.dval):
            if i in self.dkeep or i in self.dfree:
                continue
            if kn.get(i, 0) < v:
                kn[i] = v
                waits.append((i, v))
        for b in final_bufs:
            if b.w is not None and kn.get(b.w[0], 0) < b.w[1]:
                kn[b.w[0]] = b.w[1]
                waits.append(b.w)
        self.ops["sync"].append((waits, None, None))
        nc = self.nc
        self.nblocks += 1
        with nc.Block() as block:
            def run(name):
                def body(e):
                    for waits, fn, inc in self.ops[name]:
                        for k, v in waits:
                            e.wait_ge(self._sem(k), v)
                        if fn is not None:
                            ins = fn(e)
                            if inc is not None:
                                ins.then_inc(self._sem(inc[0]), inc[1])
                return body
            block.sync(run("sync"))
            block.scalar(run("act"))
            block.gpsimd(run("pool"))
            block.vector(run("dve"))
            block.tensor(run("pe"))
        self.ops = {e: [] for e in ENGS}
        for e in ENGS:
            kn = self.known[e]
            for k in ("act", "pool", "dve", "pe"):
                kn["c_" + k] = self.cnt[k]
            for i, v in enumerate(self.dval):
                if i not in self.dkeep:
                    kn[i] = v
        for b in self.bufs:
            if b.w is not None and b.w[0] in self.dkeep:
                pass
            else:
                b.w = None
            b.r = {k: v for k, v in b.r.items() if k in self.dkeep}
            if b.dsem is not None and b.dsem not in self.dkeep:
                self.dfree.append(b.dsem)
                b.dsem = None
        self.bufs = [b for b in self.bufs if b.w is not None or b.r or b.dsem is not None]


def TT(P, eng, out, in0, in1, op, reads, writes):
    P.op(eng, lambda e: e.tensor_tensor(out=out, in0=in0, in1=in1, op=op), reads, writes)


def TS(P, eng, out, in0, s1, s2, op0, op1, reads, writes):
    if s2 is None:
        P.op(eng, lambda e: e.tensor_scalar(out=out, in0=in0, scalar1=s1, scalar2=None, op0=op0), reads, writes)
    else:
        P.op(eng, lambda e: e.tensor_scalar(out=out, in0=in0, scalar1=s1, scalar2=s2, op0=op0, op1=op1), reads, writes)


def STT(P, eng, out, in0, scalar, in1, op0, op1, reads, writes):
    P.op(eng, lambda e: e.scalar_tensor_tensor(out=out, in0=in0, scalar=scalar, in1=in1, op0=op0, op1=op1), reads, writes)


def ACTF(P, out, in_, func, reads, writes, scale=1.0, bias=None):
    if bias is None:
        P.op("act", lambda e: e.activation(out=out, in_=in_, func=func, scale=scale), reads, writes)
    else:
        P.op("act", lambda e: e.activation(out=out, in_=in_, func=func, scale=scale, bias=bias), reads, writes)


def CP(P, eng, out, in_, reads, writes):
    if eng == "act":
        P.op("act", lambda e: e.activation(out=out, in_=in_, func=ACT.Copy), reads, writes)
    else:
        P.op(eng, lambda e: e.tensor_copy(out=out, in_=in_), reads, writes)


def DMA(P, q, out, in_, dst, src, slow=False):
    if slow:
        P.dma(q, lambda e: e.dma_start(out=out, in_=in_, allow_slow_non_contiguous=True), dst, src)
    else:
        P.dma(q, lambda e: e.dma_start(out=out, in_=in_), dst, src)


def cmul(P, eng, o_re, o_im, a_re, a_im, b_re, b_im, t1, t2, reads, writes, tb):
    TT(P, eng, t1, a_re, b_re, ALU.mult, reads, [tb])
    TT(P, eng, t2, a_im, b_im, ALU.mult, reads, [tb])
    TT(P, eng, o_re, t1, t2, ALU.subtract, [tb], writes)
    TT(P, eng, t1, a_re, b_im, ALU.mult, reads, [tb])
    TT(P, eng, t2, a_im, b_re, ALU.mult, reads, [tb])
    TT(P, eng, o_im, t1, t2, ALU.add, [tb], writes)


T0 = 8
UNIT = 1024
NSC = UNIT // T0
I32 = mybir.dt.int32


def s5_stage(nc, P, dr, n_pre_units, n_own_units):
    from contextlib import ExitStack
    B = Buf
    n_units = n_pre_units + n_own_units
    ext = dr["a_re"][1]
    with ExitStack() as st0:
        sbp = lambda n, s, d: st0.enter_context(nc.sbuf_tensor(_uid(n), s, d))
        sc = sbp("sc", [128, 24, 32], F32); b_sc = B("sc")
        pw = sbp("pw", [128, 3, T0 + 1, 32], F32); b_pw = B("pw")
        WBT = sbp("WBT", [128, 8, T0, 2, 128], BF16); b_WBT = B("WBT")
        WCT = sbp("WCT", [128, 8, 2, 128], BF16); b_WCT = B("WCT")
        Rc = sbp("Rc", [128, 32, NSC], F32); Rs = sbp("Rs", [128, 32, NSC], F32); b_R = B("R")
        Dcol = sbp("Dcol", [128, 8], F32); b_D = B("Dcol")
        wglu = sbp("wglu", [128, 8, 1024], BF16); b_wglu = B("wglu")
        bglu = sbp("bglu", [128, 8], F32); b_bglu = B("bglu")
        cr = sbp("cr", [128, 32, 1], F32); ci = sbp("ci", [128, 32, 1], F32); b_c = B("carry")
        P.op("dve", lambda e: e.memset(cr[:], 0.0), [], [b_c])
        P.op("dve", lambda e: e.memset(ci[:], 0.0), [], [b_c])
        S = lambda j: sc[:, j, :]
        DT, MAG, PHI, SINP, COSP, ABR, ABI, CR, CI, T1, T2, T3, RHO, C8, S8, NUMR, NUMI, DEN = range(18)
        DMA(P, "sync", Dcol[:], dr["d_skip"][0].rearrange("(k gl) p -> (gl p) k", gl=8), b_D, ext, slow=True)
        DMA(P, "sync", wglu[:], dr["wglu_bf"][0], b_wglu, dr["wglu_bf"][1])
        DMA(P, "sync", bglu[:], dr["b_glu"][0].rearrange("(k p) -> p k", p=128), b_bglu, ext, slow=True)

        with ExitStack() as st:
            sb = lambda n, s, d: st.enter_context(nc.sbuf_tensor(_uid(n), s, d))
            ps = lambda n, s, d: st.enter_context(nc.psum_tensor(_uid(n), s, d))
            are = sb("are", [128, 32], F32); aim = sb("aim", [128, 32], F32); ldt = sb("ldt", [128, 32], F32)
            b_par = B("par")
            DMA(P, "sync", are[:], dr["a_re"][0].rearrange("(pi g) n -> (g n) pi", g=2), b_par, ext, slow=True)
            DMA(P, "sync", aim[:], dr["a_im"][0].rearrange("(pi g) n -> (g n) pi", g=2), b_par, ext, slow=True)
            for g2 in range(2):
                src = dr["log_dt"][0].rearrange("(pi g) -> g pi", g=2)[g2:g2 + 1, :].to_broadcast([64, 32])
                DMA(P, "sync", ldt[64 * g2:64 * g2 + 64, :], src, b_par, ext, slow=True)
            Bre = sb("Bre", [128, 32, 16], F32); Bim = sb("Bim", [128, 32, 16], F32); b_B = B("B")
            DMA(P, "sync", Bre[:], dr["b_re"][0].rearrange("(pi g) n q -> (g n) pi q", g=2), b_B, ext, slow=True)
            DMA(P, "sync", Bim[:], dr["b_im"][0].rearrange("(pi g) n q -> (g n) pi q", g=2), b_B, ext, slow=True)
            id32 = sb("id32", [128, 128], F32); b_id = B("id32")
            DMA(P, "sync", id32[:], dr["ident32"][0], b_id, ext)
            Cx = [sb("Cx%d" % i, [128, 8, 128], F32) for i in range(2)]; b_Cx = B("Cx")
            for i in range(2):
                P.op("pool", lambda e, i=i: e.memset(Cx[i][:], 0.0), [], [b_Cx])
            for i, nm in enumerate(("c_re", "c_im")):
                cv = dr[nm][0].rearrange("(pi8 pi4 g) p n -> pi4 g p pi8 n", pi4=4, g=2)
                for pi4 in range(4):
                    for g2 in range(2):
                        p0 = pi4 * 32 + g2 * 16
                        DMA(P, "sync", Cx[i][p0:p0 + 16, :, 64 * g2:64 * g2 + 64], cv[pi4, g2], b_Cx, ext, slow=True)
            ki = sb("ki", [128, 32], I32); b_ki = B("ki")

            def sin_of(out, ang):
                TS(P, "dve", S(T1), ang, 1.0 / TWO_PI, None, ALU.mult, None, [b_sc], [b_sc])
                CP(P, "dve", ki[:], S(T1), [b_sc], [b_ki])
                CP(P, "dve", S(T1), ki[:], [b_ki], [b_sc])
                STT(P, "dve", S(T2), S(T1), -TWO_PI, ang, ALU.mult, ALU.add, [b_sc], [b_sc])
                TS(P, "dve", S(T3), S(T2), math.pi, -TWO_PI, ALU.is_gt, ALU.mult, [b_sc], [b_sc])
                TT(P, "dve", S(T2), S(T2), S(T3), ALU.add, [b_sc], [b_sc])
                TS(P, "dve", S(T3), S(T2), -math.pi, TWO_PI, ALU.is_lt, ALU.mult, [b_sc], [b_sc])
                TT(P, "dve", S(T2), S(T2), S(T3), ALU.add, [b_sc], [b_sc])
                ACTF(P, out, S(T2), ACT.Sin, [b_sc], [b_sc])

            ACTF(P, S(DT), ldt[:], ACT.Exp, [b_par], [b_sc])
            TT(P, "dve", S(MAG), are[:], S(DT), ALU.mult, [b_par, b_sc], [b_sc])
            ACTF(P, S(RHO), S(MAG), ACT.Exp, [b_sc], [b_sc], scale=float(T0))
            ACTF(P, S(MAG), S(MAG), ACT.Exp, [b_sc], [b_sc])
            TT(P, "dve", S(PHI), aim[:], S(DT), ALU.mult, [b_par, b_sc], [b_sc])
            sin_of(S(SINP), S(PHI))
            TS(P, "dve", S(NUMR), S(PHI), math.pi / 2, None, ALU.add, None, [b_sc], [b_sc])
            sin_of(S(COSP), S(NUMR))
            TT(P, "dve", S(ABR), S(MAG), S(COSP), ALU.mult, [b_sc], [b_sc])
            TT(P, "dve", S(ABI), S(MAG), S(SINP), ALU.mult, [b_sc], [b_sc])
            TS(P, "dve", S(T1), S(ABR), -1.0, None, ALU.add, None, [b_sc], [b_sc])
            TT(P, "dve", S(NUMR), S(T1), are[:], ALU.mult, [b_sc, b_par], [b_sc])
            TT(P, "dve", S(T2), S(ABI), aim[:], ALU.mult, [b_sc, b_par], [b_sc])
            TT(P, "dve", S(NUMR), S(NUMR), S(T2), ALU.add, [b_sc], [b_sc])
            TT(P, "dve", S(NUMI), S(ABI), are[:], ALU.mult, [b_sc, b_par], [b_sc])
            TT(P, "dve", S(T2), S(T1), aim[:], ALU.mult, [b_sc, b_par], [b_sc])
            TT(P, "dve", S(NUMI), S(NUMI), S(T2), ALU.subtract, [b_sc], [b_sc])
            TT(P, "dve", S(DEN), are[:], are[:], ALU.mult, [b_par], [b_sc])
            TT(P, "dve", S(T2), aim[:], aim[:], ALU.mult, [b_par], [b_sc])
            TT(P, "dve", S(DEN), S(DEN), S(T2), ALU.add, [b_sc], [b_sc])
            P.op("dve", lambda e: e.reciprocal(out=S(DEN), in_=S(DEN)), [b_sc], [b_sc])
            TT(P, "dve", S(CR), S(NUMR), S(DEN), ALU.mult, [b_sc], [b_sc])
            TT(P, "dve", S(CI), S(NUMI), S(DEN), ALU.mult, [b_sc], [b_sc])
            CP(P, "dve", S(C8), S(COSP), [b_sc], [b_sc])
            CP(P, "dve", S(S8), S(SINP), [b_sc], [b_sc])
            for _ in range(3):
                TT(P, "dve", S(T1), S(C8), S(C8), ALU.mult, [b_sc], [b_sc])
                TT(P, "dve", S(T2), S(S8), S(S8), ALU.mult, [b_sc], [b_sc])
                TT(P, "dve", S(T3), S(C8), S(S8), ALU.mult, [b_sc], [b_sc])
                TT(P, "dve", S(C8), S(T1), S(T2), ALU.subtract, [b_sc], [b_sc])
                TS(P, "dve", S(S8), S(T3), 2.0, None, ALU.mult, None, [b_sc], [b_sc])
            P.op("dve", lambda e: e.memset(pw[:, 0, 0, :], 1.0), [], [b_pw])
            P.op("dve", lambda e: e.memset(pw[:, 1, 0, :], 0.0), [], [b_pw])
            for k in range(1, T0 + 1):
                pr, pi_ = pw[:, 0, k - 1, :], pw[:, 1, k - 1, :]
                TT(P, "dve", S(T1), pr, S(ABR), ALU.mult, [b_pw, b_sc], [b_sc])
                TT(P, "dve", S(T2), pi_, S(ABI), ALU.mult, [b_pw, b_sc], [b_sc])
                TT(P, "dve", pw[:, 0, k, :], S(T1), S(T2), ALU.subtract, [b_sc], [b_pw])
                TT(P, "dve", S(T1), pr, S(ABI), ALU.mult, [b_pw, b_sc], [b_sc])
                TT(P, "dve", S(T2), pi_, S(ABR), ALU.mult, [b_pw, b_sc], [b_sc])
                TT(P, "dve", pw[:, 1, k, :], S(T1), S(T2), ALU.add, [b_sc], [b_pw])
            TS(P, "dve", pw[:, 2, :, :], pw[:, 1, :, :], -1.0, None, ALU.mult, None, [b_pw], [b_pw])
            Bb = sb("Bb", [128, 2, 32, 16], F32); b_Bb = B("Bb")
            tA = sb("tA", [128, T0, 32, 16], F32); tB = sb("tB", [128, T0, 32, 16], F32); b_t = B("tAB")
            bc16 = lambda j: S(j).unsqueeze(2).to_broadcast([128, 32, 16])
            cmul(P, "dve", Bb[:, 0], Bb[:, 1], bc16(CR), bc16(CI), Bre[:], Bim[:], tA[:, 0], tB[:, 0],
                 [b_sc, b_B], [b_Bb], b_t)
            WBx = sb("WBx", [128, T0, 2, 32, 32], F32); b_WBx = B("WBx")
            P.op("pool", lambda e: e.memset(WBx[:], 0.0), [], [b_WBx])
            for g2 in range(2):
                ps_ = slice(64 * g2, 64 * g2 + 64)
                cs_ = slice(16 * g2, 16 * g2 + 16)
                pwb = lambda ri: pw[ps_, ri, 0:T0, :].unsqueeze(3).to_broadcast([64, T0, 32, 16])
                bbb = lambda ri: Bb[ps_, ri].unsqueeze(1).to_broadcast([64, T0, 32, 16])
                cmul(P, "dve", WBx[ps_, :, 0, :, cs_], WBx[ps_, :, 1, :, cs_], pwb(0), pwb(1), bbb(0), bbb(1),
                     tA[ps_], tB[ps_], [b_pw, b_Bb], [b_WBx], b_t)
            pTr = [ps("pTr%d" % i, [128, 4, 128], F32) for i in range(2)]
            b_pTr = [B("pTr0"), B("pTr1")]
            nt = 0
            for pi8 in range(8):
                for tau in range(T0):
                    s = nt % 2; nt += 1
                    fns = []
                    for ri in range(2):
                        fns.append(lambda e, s=s, ri=ri, pi8=pi8, tau=tau: e.transpose(
                            out=pTr[s][:, ri, :],
                            in_=WBx[:, tau, ri, 4 * pi8:4 * pi8 + 4, :].rearrange("p a b -> p (a b)"),
                            identity=id32[:]))
                    P.pe_group(fns, [b_WBx, b_id], [b_pTr[s]])
                    CP(P, "act" if nt % 2 else "dve", WBT[:, pi8, tau, :, :], pTr[s][:, 0:2, :], [b_pTr[s]], [b_WBT])
            for pi8 in range(0, 8, 2):
                s = nt % 2; nt += 1
                fns = []
                for j in range(2):
                    for ri in range(2):
                        fns.append(lambda e, s=s, ri=ri, j=j, pi8=pi8: e.transpose(
                            out=pTr[s][:, 2 * j + ri, :], in_=Cx[ri][:, pi8 + j, :], identity=id32[:]))
                P.pe_group(fns, [b_Cx, b_id], [b_pTr[s]])
                for j in range(2):
                    CP(P, "dve", WCT[:, pi8 + j, 0, :], pTr[s][:, 2 * j, :], [b_pTr[s]], [b_WCT])
                    TS(P, "dve", WCT[:, pi8 + j, 1, :], pTr[s][:, 2 * j + 1, :], -1.0, None, ALU.mult, None,
                       [b_pTr[s]], [b_WCT])
            CP(P, "dve", Rc[:, :, 0], S(C8), [b_sc], [b_R])
            CP(P, "dve", Rs[:, :, 0], S(S8), [b_sc], [b_R])
            m = 1
            tAv = tA[:].rearrange("p a b c -> p (a b c)")
            tBv = tB[:].rearrange("p a b c -> p (a b c)")
            while m < NSC:
                bc = lambda t: t[:, :, m - 1:m].to_broadcast([128, 32, m])
                t1v = tAv[:, 0:32 * m].rearrange("p (a b) -> p a b", a=32)
                t2v = tBv[:, 0:32 * m].rearrange("p (a b) -> p a b", a=32)
                cmul(P, "dve", Rc[:, :, m:2 * m], Rs[:, :, m:2 * m], Rc[:, :, 0:m], Rs[:, :, 0:m], bc(Rc), bc(Rs),
                     t1v, t2v, [b_R], [b_R], b_t)
                m *= 2
            P.flush(scope="s5pre")

        NQ = 8
        uT_d, b_uTd = dr["uT_d"]
        ssm_d, b_ssmd = dr["ssm_d"]

        def alloc_set(sb, ps, tag):
            Sx = {}
            for nm in ("bA", "bB", "bC", "bD"):
                Sx[nm] = sb(nm + tag, [128, NQ, NSC], F32); Sx["b_" + nm] = B(nm + tag)
            for nm in ("bE", "bF"):
                Sx[nm] = sb(nm + tag, [128, NQ, NSC + 1], F32); Sx["b_" + nm] = B(nm + tag)
            Sx["psS"] = [ps("psS%d%s" % (i, tag), [128, 4, NSC], F32) for i in range(2)]; Sx["b_psS"] = B("psS" + tag)
            return Sx

        def quarter(Sx, uTb, b_uTb, q4, own):
            bA, bB, bC, bD, bE, bF = Sx["bA"], Sx["bB"], Sx["bC"], Sx["bD"], Sx["bE"], Sx["bF"]
            b_A, b_Bq, b_C, b_Dq, b_E, b_F = Sx["b_bA"], Sx["b_bB"], Sx["b_bC"], Sx["b_bD"], Sx["b_bE"], Sx["b_bF"]
            psS, b_psS = Sx["psS"], Sx["b_psS"]
            u3 = uTb[:].rearrange("p k (c j) -> p k j c", j=T0)
            psl = slice(NQ * q4, NQ * q4 + NQ)
            for h8 in range(NQ // 4):
                pi8 = (NQ // 4) * q4 + h8
                fns = []
                for pi4 in range(4):
                    r0 = 32 * pi4
                    for ri in range(2):
                        for j in range(T0):
                            fns.append(lambda e, r0=r0, ri=ri, j=j, pi8=pi8, pi4=pi4, u3=u3, psS=psS: e.matmul(
                                psS[ri][:, pi4, :], lhsT=WBT[r0:r0 + 32, pi8, T0 - 1 - j, ri, :],
                                rhs=u3[r0:r0 + 32, pi8, j, :], start=(j == 0), stop=(j == T0 - 1),
                                tile_position=(r0, 0)))
                P.pe_group(fns, [b_WBT, b_uTb], [b_psS])
                CP(P, "act", bA[:, 4 * h8:4 * h8 + 4, :], psS[0][:], [b_psS], [b_A])
                CP(P, "act", bB[:, 4 * h8:4 * h8 + 4, :], psS[1][:], [b_psS], [b_Bq])
            rc, rs = Rc[:, psl, :], Rs[:, psl, :]
            E1, F1 = bE[:, :, 1:], bF[:, :, 1:]
            TT(P, "dve", bC[:], bA[:], rc, ALU.mult, [b_A, b_R], [b_C])
            TT(P, "dve", E1, bB[:], rs, ALU.mult, [b_Bq, b_R], [b_E])
            TT(P, "dve", bC[:], bC[:], E1, ALU.add, [b_C, b_E], [b_C])
            TT(P, "dve", bD[:], bB[:], rc, ALU.mult, [b_Bq, b_R], [b_Dq])
            TT(P, "dve", F1, bA[:], rs, ALU.mult, [b_A, b_R], [b_F])
            TT(P, "dve", bD[:], bD[:], F1, ALU.subtract, [b_Dq, b_F], [b_Dq])
            for p_ in range(NQ):
                pi = NQ * q4 + p_
                P.op("dve", lambda e, pi=pi, p_=p_, bA=bA, bC=bC: e.tensor_tensor_scan(
                    out=bA[:, p_, :], data0=sc[:, RHO, pi:pi + 1].to_broadcast([128, NSC]), data1=bC[:, p_, :],
                    initial=cr[:, pi, :], op0=ALU.mult, op1=ALU.add), [b_C, b_sc, b_c], [b_A])
                P.op("dve", lambda e, pi=pi, p_=p_, bB=bB, bD=bD: e.tensor_tensor_scan(
                    out=bB[:, p_, :], data0=sc[:, RHO, pi:pi + 1].to_broadcast([128, NSC]), data1=bD[:, p_, :],
                    initial=ci[:, pi, :], op0=ALU.mult, op1=ALU.add), [b_Dq, b_sc, b_c], [b_Bq])
            co = slice(0, NSC) if own else slice(NSC - 1, NSC)
            rco, rso = rc[:, :, co], rs[:, :, co]
            TT(P, "dve", E1[:, :, co], bA[:, :, co], rco, ALU.mult, [b_A, b_R], [b_E])
            TT(P, "dve", bC[:, :, co], bB[:, :, co], rso, ALU.mult, [b_Bq, b_R], [b_C])
            TT(P, "dve", E1[:, :, co], E1[:, :, co], bC[:, :, co], ALU.subtract, [b_E, b_C], [b_E])
            TT(P, "dve", F1[:, :, co], bB[:, :, co], rco, ALU.mult, [b_Bq, b_R], [b_F])
            TT(P, "dve", bD[:, :, co], bA[:, :, co], rso, ALU.mult, [b_A, b_R], [b_Dq])
            TT(P, "dve", F1[:, :, co], F1[:, :, co], bD[:, :, co], ALU.add, [b_F, b_Dq], [b_F])
            if own:
                CP(P, "dve", bE[:, :, 0:1], cr[:, psl, :], [b_c], [b_E])
                CP(P, "dve", bF[:, :, 0:1], ci[:, psl, :], [b_c], [b_F])
            CP(P, "dve", cr[:, psl, :], bE[:, :, NSC:NSC + 1], [b_E], [b_c])
            CP(P, "dve", ci[:, psl, :], bF[:, :, NSC:NSC + 1], [b_F], [b_c])

        with ExitStack() as st:
            sb = lambda n, s, d: st.enter_context(nc.sbuf_tensor(_uid(n), s, d))
            ps = lambda n, s, d: st.enter_context(nc.psum_tensor(_uid(n), s, d))
            uTp = [sb("uTp%d" % i, [128, 8, UNIT], BF16) for i in range(2)]; b_uTp = [B("uTp0"), B("uTp1")]
            sets = [alloc_set(sb, ps, "p%d" % i) for i in range(2)]
            nq = 0
            for u in range(n_pre_units):
                su = u % 2
                DMA(P, "sync", uTp[su][:], uT_d[:, :, u * UNIT:(u + 1) * UNIT], b_uTp[su], b_uTd)
                for q4 in range(32 // NQ):
                    quarter(sets[nq % 2], uTp[su], b_uTp[su], q4, False)
                    nq += 1
            P.flush(scope="s5prefix")

        with ExitStack() as st:
            sb = lambda n, s, d: st.enter_context(nc.sbuf_tensor(_uid(n), s, d))
            ps = lambda n, s, d: st.enter_context(nc.psum_tensor(_uid(n), s, d))
            uT = sb("uT", [128, 8, UNIT], BF16); b_uT = B("uT")
            Sx = alloc_set(sb, ps, "o")
            bE, bF, b_E, b_F = Sx["bE"], Sx["bF"], Sx["b_bE"], Sx["b_bF"]
            psH = [ps("psH%d" % i, [128, T0, NSC], F32) for i in range(2)]; b_psH = B("psH")
            psY = ps("psY", [128, UNIT], F32); b_psY = B("psY")
            hA = sb("hA", [128, T0, NSC], F32); hB = sb("hB", [128, T0, NSC], F32)
            hC = sb("hC", [128, T0, NSC], F32); b_h = B("hABC")
            Hre = [sb("Hre%d" % i, [128, UNIT], BF16) for i in range(2)]
            Him = [sb("Him%d" % i, [128, UNIT], BF16) for i in range(2)]
            b_H = [B("H0"), B("H1")]
            ysb = sb("ysb", [128, UNIT], F32); b_y = B("ysb")
            gl1 = sb("gl1", [128, UNIT], F32); gl2 = sb("gl2", [128, UNIT], F32); b_g = B("g12")
            ygT = sb("ygT", [128, 8, UNIT], BF16); b_yg = B("ygT")
            gate = sb("gate", [128, 512], F32); b_gate = B("gate")
            sso = [sb("sso%d" % i, [128, UNIT], BF16) for i in range(2)]; b_sso = [B("sso0"), B("sso1")]
            for ou in range(n_own_units):
                u = n_pre_units + ou
                DMA(P, "sync", uT[:], uT_d[:, :, u * UNIT:(u + 1) * UNIT], b_uT, b_uTd)
                u3 = uT[:].rearrange("p k (c j) -> p k j c", j=T0)
                for q4 in range(32 // NQ):
                    quarter(Sx, uT, b_uT, q4, True)
                    for h8 in range(NQ // 4):
                        pi8 = (NQ // 4) * q4 + h8
                        for pi4 in range(4):
                            pi = 4 * pi8 + pi4
                            p_ = 4 * h8 + pi4
                            r0 = 32 * pi4
                            hs = pi % 2
                            fns = []
                            for ri in range(2):
                                for i in range(T0):
                                    for tau in range(i + 1):
                                        fns.append(lambda e, r0=r0, ri=ri, i=i, tau=tau, pi8=pi8, u3=u3: e.matmul(
                                            psH[ri][:, i, :], lhsT=WBT[r0:r0 + 32, pi8, tau, ri, :],
                                            rhs=u3[r0:r0 + 32, pi8, i - tau, :], start=(tau == 0), stop=(tau == i),
                                            tile_position=(r0, 0)))
                            P.pe_group(fns, [b_WBT, b_uT], [b_psH])
                            xr_b = bE[:, p_, 0:NSC].unsqueeze(1).to_broadcast([128, T0, NSC])
                            xi_b = bF[:, p_, 0:NSC].unsqueeze(1).to_broadcast([128, T0, NSC])
                            pwr_b = pw[:, 0, 1:T0 + 1, pi].unsqueeze(2).to_broadcast([128, T0, NSC])
                            pwi_b = pw[:, 1, 1:T0 + 1, pi].unsqueeze(2).to_broadcast([128, T0, NSC])
                            hre_v = Hre[hs][:].rearrange("p (c i) -> p i c", i=T0)
                            him_v = Him[hs][:].rearrange("p (c i) -> p i c", i=T0)
                            TT(P, "dve", hA[:], xr_b, pwr_b, ALU.mult, [b_E, b_pw], [b_h])
                            TT(P, "dve", hB[:], xi_b, pwi_b, ALU.mult, [b_F, b_pw], [b_h])
                            TT(P, "dve", hA[:], hA[:], hB[:], ALU.subtract, [b_h], [b_h])
                            TT(P, "dve", hre_v, psH[0][:], hA[:], ALU.add, [b_psH, b_h], [b_H[hs]])
                            TT(P, "dve", hC[:], xi_b, pwr_b, ALU.mult, [b_F, b_pw], [b_h])
                            TT(P, "dve", hB[:], xr_b, pwi_b, ALU.mult, [b_E, b_pw], [b_h])
                            TT(P, "dve", hC[:], hC[:], hB[:], ALU.add, [b_h], [b_h])
                            TT(P, "dve", him_v, psH[1][:], hC[:], ALU.add, [b_psH, b_h], [b_H[hs]])
                            fns = []
                            for half in range(2):
                                hsl = slice(512 * half, 512 * half + 512)
                                fns.append(lambda e, hs=hs, hsl=hsl, r0=r0, pi8=pi8: e.matmul(
                                    psY[r0:r0 + 32, hsl], lhsT=WCT[:, pi8, 0, r0:r0 + 32], rhs=Hre[hs][:, hsl],
                                    start=True, stop=False, tile_position=(0, r0)))
                                fns.append(lambda e, hs=hs, hsl=hsl, r0=r0, pi8=pi8: e.matmul(
                                    psY[r0:r0 + 32, hsl], lhsT=WCT[:, pi8, 1, r0:r0 + 32], rhs=Him[hs][:, hsl],
                                    start=False, stop=True, tile_position=(0, r0)))
                            P.pe_group(fns, [b_WCT, b_H[hs]], [b_psY])
                        STT(P, "dve", ysb[:], uT[:, pi8, :], Dcol[:, pi8:pi8 + 1], psY[:], ALU.mult, ALU.add,
                            [b_uT, b_D, b_psY], [b_y])
                        ACTF(P, gl1[:], ysb[:], ACT.Square, [b_y], [b_g])
                        TS(P, "dve", gl1[:], gl1[:], 0.044715, 1.0, ALU.mult, ALU.add, [b_g], [b_g])
                        TT(P, "dve", gl1[:], gl1[:], ysb[:], ALU.mult, [b_g, b_y], [b_g])
                        ACTF(P, gl2[:], gl1[:], ACT.Sigmoid, [b_g], [b_g], scale=1.5957691216057308)
                        TT(P, "dve", ygT[:, pi8, :], gl2[:], ysb[:], ALU.mult, [b_g, b_y], [b_yg])
                for mt in range(8):
                    so = mt % 2
                    for half in range(2):
                        hsl = slice(512 * half, 512 * half + 512)
                        fns = [lambda e, kt=kt, mt=mt, hsl=hsl: e.matmul(
                            psY[:, hsl], lhsT=wglu[:, kt, 128 * mt:128 * mt + 128], rhs=ygT[:, kt, hsl],
                            start=(kt == 0), stop=(kt == 7)) for kt in range(8)]
                        P.pe_group(fns, [b_wglu, b_yg], [b_psY])
                        ACTF(P, gate[:], psY[:, hsl], ACT.Sigmoid, [b_psY, b_bglu], [b_gate], bias=bglu[:, mt:mt + 1])
                        TT(P, "dve", sso[so][:, hsl], gate[:], ygT[:, mt, hsl], ALU.mult, [b_gate, b_yg], [b_sso[so]])
                    DMA(P, "act", ssm_d[:, mt, ou * UNIT:(ou + 1) * UNIT], sso[so][:], b_ssmd, b_sso[so])
            P.flush(scope="s5own")


def cast_weights(nc, P, pairs):
    for dst, db, src, sbuf_ in pairs:
        rows, cols = dst.shape
        step = max(1, (1 << 20) // cols)
        for r in range(0, rows, step):
            r1 = min(rows, r + step)
            DMA(P, "pool", dst[r:r1, :], src[r:r1, :], db, sbuf_)


NCT = 30
CT_Q, CT_QSW, CT_K, CT_KSW, CT_U = 0, 8, 11, 19, 22


def norm_transpose(nc, P, T, x_rows_ap, b_src, hn_out3, b_hn, gain, b_gain, s):
    xt, junk, ss, rstd, xn, pT, idt, eps_t = T["xt"], T["junk"], T["ss"], T["rstd"], T["xn"], T["pT"], T["idt"], T["eps"]
    s2 = s % 2
    bx, bj, bs, br, bxn, bpT = T["b_xt"][s], T["b_junk"], T["b_ss"][s], T["b_rstd"][s], T["b_xn"][s2], T["b_pT"][s2]
    if x_rows_ap is not None:
        DMA(P, "sync", xt[s][:], x_rows_ap, bx, b_src)
    P.op("act", lambda e: e.activation(out=junk[:], in_=xt[s][:], func=ACT.Square, accum_out=ss[:, s:s + 1]),
         [bx], [bj, bs])
    P.op("act", lambda e: e.activation(out=rstd[:, s:s + 1], in_=ss[:, s:s + 1], func=ACT.Sqrt, scale=1.0 / D,
                                       bias=eps_t[:]), [bs, T["b_eps"]], [br])
    P.op("dve", lambda e: e.reciprocal(out=rstd[:, s:s + 1], in_=rstd[:, s:s + 1]), [br], [br])
    P.op("act", lambda e: e.activation(out=xn[s2][:], in_=xt[s][:], func=ACT.Copy, scale=rstd[:, s:s + 1]),
         [bx, br], [bxn])
    P.pe_group([(lambda e, k=k: e.transpose(out=pT[s2][:, k * 128:(k + 1) * 128], in_=xn[s2][:, k * 128:(k + 1) * 128],
                                            identity=idt[:])) for k in range(KT)], [bxn, T["b_idt"]], [bpT])
    TT(P, "dve", hn_out3, pT[s2][:].rearrange("p (k c) -> p k c", k=KT),
       gain[:].unsqueeze(2).to_broadcast([128, KT, 128]), ALU.mult, [bpT, b_gain], [b_hn])


def norm_tiles(nc, sb, ps, B, nx=2):
    T = {}
    T["xt"] = [sb("xt%d" % i, [128, D], F32) for i in range(nx)]
    T["junk"] = sb("junk", [128, D], BF16)
    T["ss"] = sb("ss", [128, nx], F32); T["rstd"] = sb("rstd", [128, nx], F32)
    T["xn"] = [sb("xn%d" % i, [128, D], BF16) for i in range(2)]
    T["pT"] = [ps("pT%d" % i, [128, D], BF16) for i in range(2)]
    T["idt"] = sb("idt", [128, 128], BF16); T["eps"] = sb("eps_t", [128, 1], F32)
    T["b_xt"] = [B("xt%d" % i) for i in range(nx)]; T["b_junk"] = B("junk"); T["b_ss"] = [B("ss%d" % i) for i in range(nx)]
    T["b_rstd"] = [B("r%d" % i) for i in range(nx)]; T["b_xn"] = [B("xn0"), B("xn1")]; T["b_pT"] = [B("pT0"), B("pT1")]
    T["b_idt"] = B("idt"); T["b_eps"] = B("eps")
    return T


def front_stage(nc, P, dr, n_blocks, first_kv, first_q):
    from contextlib import ExitStack
    B = Buf
    ext = dr["ext"]
    with ExitStack() as st:
        sb = lambda n, s, d: st.enter_context(nc.sbuf_tensor(_uid(n), s, d))
        ps = lambda n, s, d: st.enter_context(nc.psum_tensor(_uid(n), s, d))
        T = norm_tiles(nc, sb, ps, B, nx=4)
        DMA(P, "sync", T["idt"][:], dr["ident_bf"][0], T["b_idt"], ext)
        P.op("dve", lambda e: e.memset(T["eps"][:], 1e-6), [], [T["b_eps"]])
        g1t = sb("g1t", [128, KT], F32); b_g1 = B("g1t")
        DMA(P, "sync", g1t[:], dr["g1"][0], b_g1, ext)
        hnT = [sb("hnT%d" % i, [128, KT, BLK], BF16) for i in range(2)]; b_hn = [B("hn0"), B("hn1")]
        wu = sb("wu", [128, 8, KT, 128], BF16); b_wu = B("wu")
        wfm_bf, b_wfm = dr["wfm_bf"]
        DMA(P, "sync", wu[:], wfm_bf[CT_U:CT_U + 8].rearrange("c p k m -> p c k m"), b_wu, b_wfm)
        ublk = [sb("ublk%d" % i, [128, 8, BLK], BF16) for i in range(2)]; b_ub = [B("ub0"), B("ub1")]
        NWS = 2
        wst = [sb("wst%d" % i, [128, 4 * KT * 128], BF16) for i in range(NWS)]; b_wst = [B("wst%d" % i) for i in range(NWS)]
        pP = [ps("pP%d" % i, [128, BLK], F32) for i in range(3)]; b_pP = [B("pP%d" % i) for i in range(3)]
        cosb = sb("cosb", [128, 3, BLK], F32); sinb = sb("sinb", [128, 3, BLK], F32); b_cs = B("cossin")
        swsin = sb("swsin", [128, 3, BLK], F32); b_sw = B("swsin")
        rtmp = sb("rtmp", [128, BLK], F32); b_rt = B("rtmp")
        rtmp2 = sb("rtmp2", [128, BLK], F32); b_rt2 = B("rtmp2")
        qkblk = [sb("qkblk%d" % i, [128, 8, BLK], BF16) for i in range(2)]; b_qk = [B("qk0"), B("qk1")]
        vrow = sb("vrow", [128, 4, 1024], BF16); b_vr = B("vrow")
        xpad, b_x = dr["xpad"]
        uT_d, b_uTd = dr["uT_d"]
        wv_bf, b_wv = dr["wv_bf"]
        st_ = {"np": 0, "nw": 0, "nqk": 0}

        def proj_tile(lhs_w, hn, b_w, b_h):
            s = st_["np"] % 3; st_["np"] += 1
            P.pe_group([(lambda e, k=k, s=s: e.matmul(pP[s][:], lhsT=lhs_w[:, k, :], rhs=hn[:, k, :],
                                                      start=(k == 0), stop=(k == KT - 1))) for k in range(KT)],
                       [b_w, b_h], [b_pP[s]])
            return s

        def load_chunk(ct0, n):
            s = st_["nw"] % NWS; st_["nw"] += 1
            v4 = wst[s][:, 0:n * KT * 128].rearrange("p (c k m) -> p c k m", c=n, k=KT)
            DMA(P, "sync", v4, wfm_bf[ct0:ct0 + n].rearrange("c p k m -> p c k m"), b_wst[s], b_wfm)
            return s, v4

        def qk_proj(hs, ct_main, ct_sw, out_d, b_outd, col0):
            hn = hnT[hs]
            s, v4 = load_chunk(ct_sw, 3)
            for j in range(3):
                sp = proj_tile(v4[:, j], hn[:], b_wst[s], b_hn[hs])
                CP(P, "act", swsin[:, j, :], pP[sp][:], [b_pP[sp]], [b_sw])
            qs = st_["nqk"] % 2; st_["nqk"] += 1
            for c4 in range(2):
                s, v4 = load_chunk(ct_main + 4 * c4, 4)
                for j in range(4):
                    h = 4 * c4 + j
                    jb = h % 3
                    sp = proj_tile(v4[:, j], hn[:], b_wst[s], b_hn[hs])
                    TT(P, "dve", rtmp[:], pP[sp][:], cosb[:, jb, :], ALU.mult, [b_pP[sp], b_cs], [b_rt])
                    TT(P, "pool", rtmp2[:], swsin[:, h // 3, :], sinb[:, jb, :], ALU.mult, [b_sw, b_cs], [b_rt2])
                    TT(P, "dve", qkblk[qs][:, h, :], rtmp[:], rtmp2[:], ALU.add, [b_rt, b_rt2], [b_qk[qs]])
            DMA(P, "pool", out_d[:, :, col0:col0 + BLK], qkblk[qs][:], b_outd, b_qk[qs])

        ntile = [0]

        def emit_tile(b, t):
            s = ntile[0] % 4; ntile[0] += 1
            row = b * BLK + t * 128
            norm_transpose(nc, P, T, xpad[row:row + 128, :], b_x, hnT[b % 2][:, :, t * 128:(t + 1) * 128], b_hn[b % 2],
                           g1t, b_g1, s)

        for t in range(4):
            emit_tile(0, t)
        for b in range(n_blocks):
            hs = b % 2
            us = b % 2
            for ct in range(8):
                sp = proj_tile(wu[:, ct], hnT[hs][:], b_wu, b_hn[hs])
                CP(P, "act", ublk[us][:, ct, :], pP[sp][:], [b_pP[sp]], [b_ub[us]])
                if ct % 2 == 1 and b + 1 < n_blocks:
                    emit_tile(b + 1, ct // 2)
            DMA(P, "pool", uT_d[:, :, b * BLK:(b + 1) * BLK], ublk[us][:], b_uTd, b_ub[us])
            if b < first_kv:
                continue
            wb = b - first_kv
            DMA(P, "sync", cosb[:], dr["cos_d"][0][:, :, wb * BLK:(wb + 1) * BLK], b_cs, ext)
            DMA(P, "sync", sinb[:], dr["sin_d"][0][:, :, wb * BLK:(wb + 1) * BLK], b_cs, ext)
            if b >= first_q:
                qk_proj(hs, CT_Q, CT_QSW, dr["qT_d"][0], dr["qT_d"][1], (b - first_q) * BLK)
            qk_proj(hs, CT_K, CT_KSW, dr["kT_d"][0], dr["kT_d"][1], wb * BLK)
            for half in range(2):
                if "v" in SKIP:
                    break
                s = st_["nw"] % NWS; st_["nw"] += 1
                vv = wst[s][:, 0:KT * 512].rearrange("p (k n) -> p k n", k=KT)
                DMA(P, "sync", vv, wv_bf[:, :, half * 512:(half + 1) * 512], b_wst[s], b_wv)
                for t in range(4):
                    sp = st_["np"] % 3; st_["np"] += 1
                    P.pe_group([(lambda e, k=k, sp=sp, t=t, vv=vv, hs=hs: e.matmul(
                        pP[sp][:], lhsT=hnT[hs][:, k, t * 128:(t + 1) * 128], rhs=vv[:, k, :],
                        start=(k == 0), stop=(k == KT - 1))) for k in range(KT)], [b_wst[s], b_hn[hs]], [b_pP[sp]])
                    CP(P, "act", vrow[:, t, half * 512:(half + 1) * 512], pP[sp][:], [b_pP[sp]], [b_vr])
            if "v" not in SKIP:
                DMA(P, "pool", dr["V_d"][0][wb * BLK:(wb + 1) * BLK, :].rearrange("(t p) c -> p t c", p=128), vrow[:],
                    dr["V_d"][1], b_vr)
        P.flush(scope="front")


def attn_stage(nc, P, dr):
    from contextlib import ExitStack
    B = Buf
    ext = dr["ext"]
    W = 2 * NTOK
    SC = 128 ** -0.5
    with ExitStack() as st:
        sb = lambda n, s, d: st.enter_context(nc.sbuf_tensor(_uid(n), s, d))
        ps = lambda n, s, d: st.enter_context(nc.psum_tensor(_uid(n), s, d))
        cs = sb("cs", [128, 4, 128], BF16); b_c = B("cs")
        DMA(P, "sync", cs[:], dr["aconsts"][0], b_c, ext)
        hm = sb("hm", [128, 1], F32); b_hm = B("hm")
        DMA(P, "sync", hm[:], dr["hmask"][0], b_hm, ext)
        qT = [sb("qTh%d" % i, [128, NTOK], BF16) for i in range(2)]; b_q = [B("q0"), B("q1")]
        kT = [sb("kTh%d" % i, [128, W], BF16) for i in range(2)]; b_k = [B("k0"), B("k1")]
        num = sb("num", [128, NTOK], F32); den = sb("den", [128, NTOK], F32); b_num, b_den = B("num"), B("den")
        outb = [sb("outb%d" % i, [128, NTOK], BF16) for i in range(2)]; b_ob = [B("ob0"), B("ob1")]
        NV = 6
        vt = [sb("vt%d" % i, [128, 128], BF16) for i in range(NV)]; b_vt = [B("vt%d" % i) for i in range(NV)]
        pt = [sb("pt%d" % i, [128, 128], BF16) for i in range(4)]; b_pt = [B("pt%d" % i) for i in range(4)]
        pS = [ps("pS%d" % i, [128, 128], F32) for i in range(4)]; b_pS = [B("pS%d" % i) for i in range(4)]
        pO = [ps("pO%d" % i, [128, 128], F32) for i in range(2)]; b_pO = [B("pO0"), B("pO1")]
        pL = [ps("pL%d" % i, [128, 128], F32) for i in range(2)]; b_pL = [B("pL0"), B("pL1")]
        qT_d, b_qd = dr["qT_d"]; kT_d, b_kd = dr["kT_d"]; V_d, b_vd = dr["V_d"]; mix_d, b_mix = dr["mix_d"]
        iv = 0; ip = 0; blk = 0
        for h in range(8):
            hs = h % 2
            DMA(P, "sync", qT[hs][:], qT_d[:, h, :], b_q[hs], b_qd)
            DMA(P, "sync", kT[hs][:], kT_d[:, h, :], b_k[hs], b_kd)
            for pi_, dil in enumerate((1, 4, 16)):
                nb_all = W // dil // 128
                nb0 = nb_all // 2
                for n in range(nb0, nb_all):
                    for r in range(dil):
                        def wpos(nn):
                            s0 = nn * 128 * dil + r
                            return slice(s0, s0 + 127 * dil + 1, dil)
                        pq_w = wpos(n)
                        pq = slice(pq_w.start - NTOK, pq_w.stop - NTOK, dil)
                        so = blk % 2; blk += 1
                        slots = []
                        for (kn, mi) in ((n, 1), (n - 1, 2)):
                            pk = wpos(kn)
                            halo = kn < nb0
                            sv = iv % NV; iv += 1
                            sp = ip % 4; ip += 1
                            slots.append((sv, sp))
                            DMA(P, "sync", vt[sv][:], V_d[pk, 128 * h:128 * h + 128], b_vt[sv], b_vd)
                            P.pe_group([lambda e, sp=sp, pk=pk, pq=pq, hs=hs: e.matmul(
                                            pS[sp][:], lhsT=kT[hs][:, pk], rhs=qT[hs][:, pq], start=True, stop=False),
                                        lambda e, sp=sp, mi=mi: e.matmul(
                                            pS[sp][:], lhsT=cs[:, 0, :], rhs=cs[:, mi, :], start=False, stop=True)],
                                       [b_k[hs], b_q[hs], b_c], [b_pS[sp]])
                            if halo:
                                ACTF(P, pt[sp][:], pS[sp][:], ACT.Exp, [b_pS[sp], b_hm], [b_pt[sp]], scale=SC, bias=hm[:])
                            else:
                                ACTF(P, pt[sp][:], pS[sp][:], ACT.Exp, [b_pS[sp]], [b_pt[sp]], scale=SC)
                        P.pe_group([lambda e, sv=sv, sp=sp, i=i, so=so: e.matmul(
                            pO[so][:], lhsT=vt[sv][:], rhs=pt[sp][:], start=(i == 0), stop=(i == 1))
                            for i, (sv, sp) in enumerate(slots)],
                            [b_vt[sv] for sv, _ in slots] + [b_pt[sp] for _, sp in slots], [b_pO[so]])
                        P.pe_group([lambda e, sp=sp, i=i, so=so: e.matmul(
                            pL[so][:], lhsT=cs[:, 3, :], rhs=pt[sp][:], start=(i == 0), stop=(i == 1))
                            for i, (_, sp) in enumerate(slots)], [b_c] + [b_pt[sp] for _, sp in slots], [b_pL[so]])
                        if pi_ == 0:
                            CP(P, "dve", num[:, pq], pO[so][:], [b_pO[so]], [b_num])
                            CP(P, "act", den[:, pq], pL[so][:], [b_pL[so]], [b_den])
                        else:
                            TT(P, "dve", num[:, pq], pO[so][:], num[:, pq], ALU.add, [b_pO[so], b_num], [b_num])
                            TT(P, "dve", den[:, pq], pL[so][:], den[:, pq], ALU.add, [b_pL[so], b_den], [b_den])
            P.op("dve", lambda e: e.reciprocal(out=den[:], in_=den[:]), [b_den], [b_den])
            TT(P, "dve", outb[hs][:], num[:], den[:], ALU.mult, [b_num, b_den], [b_ob[hs]])
            DMA(P, "pool", mix_d[:, h, :], outb[hs][:], b_mix, b_ob[hs])
        P.flush(scope="attn")


def tail1_stage(nc, P, dr, x_row0):
    from contextlib import ExitStack
    B = Buf
    ext = dr["ext"]
    with ExitStack() as st:
        sb = lambda n, s, d: st.enter_context(nc.sbuf_tensor(_uid(n), s, d))
        ps = lambda n, s, d: st.enter_context(nc.psum_tensor(_uid(n), s, d))
        T = norm_tiles(nc, sb, ps, B)
        DMA(P, "sync", T["idt"][:], dr["ident_bf"][0], T["b_idt"], ext)
        P.op("dve", lambda e: e.memset(T["eps"][:], 1e-6), [], [T["b_eps"]])
        g2t = sb("g2t_", [128, KT], F32); b_g2 = B("g2t")
        DMA(P, "sync", g2t[:], dr["g2"][0], b_g2, ext)
        wout = sb("wout", [128, KT, D], BF16); b_wo = B("wout")
        DMA(P, "sync", wout[:], dr["wout_bf"][0], b_wo, dr["wout_bf"][1])
        mixt = [sb("mixt%d" % i, [128, KT, 128], BF16) for i in range(2)]; b_mt = [B("mt0"), B("mt1")]
        xin = [sb("xin%d" % i, [128, D], F32) for i in range(2)]; b_xin = [B("xin0"), B("xin1")]
        hn2b = [sb("hn2b%d" % i, [128, KT, 128], BF16) for i in range(2)]; b_h2 = [B("h2b0"), B("h2b1")]
        pH = ps("pH", [128, D], F32); b_pH = B("pH")
        xpad, b_x = dr["xpad"]; mix_d, b_mix = dr["mix_d"]; h_d, b_hd = dr["h_d"]; hn2_d, b_hn2d = dr["hn2T_d"]
        for i in range(NTOK // 128):
            s = i % 2
            DMA(P, "sync", mixt[s][:], mix_d[:, :, i * 128:(i + 1) * 128], b_mt[s], b_mix)
            DMA(P, "sync", xin[s][:], xpad[x_row0 + i * 128:x_row0 + (i + 1) * 128, :], b_xin[s], b_x)
            fns = []
            for fb in range(4):
                for k in range(KT):
                    fns.append(lambda e, s=s, fb=fb, k=k: e.matmul(
                        pH[:, fb * 512:(fb + 1) * 512], lhsT=mixt[s][:, k, :], rhs=wout[:, k, fb * 512:(fb + 1) * 512],
                        start=(k == 0), stop=(k == KT - 1)))
            P.pe_group(fns, [b_mt[s], b_wo], [b_pH])
            TT(P, "dve", T["xt"][s][:], pH[:], xin[s][:], ALU.add, [b_pH, b_xin[s]], [T["b_xt"][s]])
            DMA(P, "pool", h_d[i * 128:(i + 1) * 128, :], T["xt"][s][:], b_hd, T["b_xt"][s])
            norm_transpose(nc, P, T, None, None, hn2b[s][:], b_h2[s], g2t, b_g2, s)
            DMA(P, "pool", hn2_d[:, :, i * 128:(i + 1) * 128], hn2b[s][:], b_hn2d, b_h2[s])
        P.flush(scope="tail1")


DFF = 5632
NF = DFF // 128


def tail2_stage(nc, P, dr):
    from contextlib import ExitStack
    B = Buf
    ext = dr["ext"]
    with ExitStack() as st:
        sb = lambda n, s, d: st.enter_context(nc.sbuf_tensor(_uid(n), s, d))
        ps = lambda n, s, d: st.enter_context(nc.psum_tensor(_uid(n), s, d))
        hn2 = [sb("hn2s%d" % i, [128, KT, BLK], BF16) for i in range(2)]; b_hn2 = [B("hn2s0"), B("hn2s1")]
        HT = sb("HT", [128, NF, BLK], BF16); b_HT = B("HT")
        wst = [sb("wst2_%d" % i, [128, KT * 512], BF16) for i in range(3)]; b_wst = [B("w2_%d" % i) for i in range(3)]
        gsig = sb("gsig", [128, BLK], F32); b_gs = B("gsig")
        hin = [sb("hin%d" % i, [128, D], F32) for i in range(4)]; b_hin = [B("hin%d" % i) for i in range(4)]
        junk = sb("junk2", [128, D], F32); b_junk = B("junk2")
        gF = sb("gF", [128, D], F32); b_gF = B("gF")
        DMA(P, "sync", gF[:], dr["final_g"][0].partition_broadcast(128), b_gF, ext)
        ss = sb("ss2", [128, 4], F32); rstd = sb("rstd2", [128, 4], F32); b_ss = [B("ss2_%d" % i) for i in range(4)]
        eps_t = sb("eps2", [128, 1], F32); b_eps = B("eps2")
        P.op("dve", lambda e: e.memset(eps_t[:], 1e-6), [], [b_eps])
        pG = [ps("pG%d" % i, [128, BLK], F32) for i in range(2)]; b_pG = [B("pG0"), B("pG1")]
        pU = [ps("pU%d" % i, [128, BLK], F32) for i in range(2)]; b_pU = [B("pU0"), B("pU1")]
        pD = [ps("pD%d" % i, [128, 1024], F32) for i in range(2)]; b_pD = [B("pD0"), B("pD1")]
        hn2_d, b_hn2d = dr["hn2T_d"]; h_d, b_hd = dr["h_d"]; y_d, b_yd = dr["y"]
        wg_bf, b_wg = dr["wgate_bf"]; wu_bf, b_wub = dr["wup_bf"]; wd_bf, b_wd = dr["wdown_bf"]
        nw = 0
        for sbk in range(NTOK // BLK):
            hs = sbk % 2
            DMA(P, "sync", hn2[hs][:], hn2_d[:, :, sbk * BLK:(sbk + 1) * BLK], b_hn2[hs], b_hn2d)
            for fc in range(NF // 4):
                sg = nw % 3; nw += 1
                vg = wst[sg][:].rearrange("p (k n) -> p k n", k=KT)
                DMA(P, "sync", vg, wg_bf[:, :, fc * 512:(fc + 1) * 512], b_wst[sg], b_wg)
                su = nw % 3; nw += 1
                vu = wst[su][:].rearrange("p (k n) -> p k n", k=KT)
                DMA(P, "sync", vu, wu_bf[:, :, fc * 512:(fc + 1) * 512], b_wst[su], b_wub)
                for j in range(4):
                    f = 4 * fc + j
                    s = f % 2
                    P.pe_group([(lambda e, k=k, s=s, j=j, vg=vg, hs=hs: e.matmul(
                        pG[s][:], lhsT=vg[:, k, j * 128:(j + 1) * 128], rhs=hn2[hs][:, k, :],
                        start=(k == 0), stop=(k == KT - 1))) for k in range(KT)], [b_wst[sg], b_hn2[hs]], [b_pG[s]])
                    P.pe_group([(lambda e, k=k, s=s, j=j, vu=vu, hs=hs: e.matmul(
                        pU[s][:], lhsT=vu[:, k, j * 128:(j + 1) * 128], rhs=hn2[hs][:, k, :],
                        start=(k == 0), stop=(k == KT - 1))) for k in range(KT)], [b_wst[su], b_hn2[hs]], [b_pU[s]])
                    ACTF(P, gsig[:], pG[s][:], ACT.Silu, [b_pG[s]], [b_gs])
                    TT(P, "dve", HT[:, f, :], pU[s][:], gsig[:], ALU.mult, [b_pU[s], b_gs], [b_HT])
            for t in range(4):
                i = sbk * 4 + t
                DMA(P, "sync", hin[t][:], h_d[i * 128:(i + 1) * 128, :], b_hin[t], b_hd)
            for fc in range(NF // 4):
                sd = nw % 3; nw += 1
                vd = wst[sd][:].rearrange("p (f n) -> p f n", f=4)
                DMA(P, "sync", vd, wd_bf[:, fc * 4:(fc + 1) * 4, :], b_wst[sd], b_wd)
                for t in range(4):
                    for hf in range(2):
                        fns = []
                        for j in range(4):
                            f = 4 * fc + j
                            for fb2 in range(2):
                                fb = 2 * hf + fb2
                                fns.append(lambda e, j=j, f=f, fb=fb, fb2=fb2, hf=hf, vd=vd, t=t: e.matmul(
                                    pD[hf][:, fb2 * 512:(fb2 + 1) * 512], lhsT=HT[:, f, t * 128:(t + 1) * 128],
                                    rhs=vd[:, j, fb * 512:(fb + 1) * 512], start=(j == 0), stop=(j == 3)))
                        P.pe_group(fns, [b_HT, b_wst[sd]], [b_pD[hf]])
                        hsl = slice(1024 * hf, 1024 * hf + 1024)
                        TT(P, "dve", hin[t][:, hsl], pD[hf][:], hin[t][:, hsl], ALU.add, [b_pD[hf], b_hin[t]], [b_hin[t]])
            for t in range(4):
                i = sbk * 4 + t
                s2 = t
                P.op("act", lambda e, s2=s2: e.activation(out=junk[:], in_=hin[s2][:], func=ACT.Square,
                                                          accum_out=ss[:, s2:s2 + 1]), [b_hin[s2]], [b_junk, b_ss[s2]])
                P.op("act", lambda e, s2=s2: e.activation(out=rstd[:, s2:s2 + 1], in_=ss[:, s2:s2 + 1], func=ACT.Sqrt,
                                                          scale=1.0 / D, bias=eps_t[:]), [b_ss[s2], b_eps], [b_ss[s2]])
                P.op("dve", lambda e, s2=s2: e.reciprocal(out=rstd[:, s2:s2 + 1], in_=rstd[:, s2:s2 + 1]),
                     [b_ss[s2]], [b_ss[s2]])
                STT(P, "dve", hin[s2][:], hin[s2][:], rstd[:, s2:s2 + 1], gF[:], ALU.mult, ALU.mult,
                    [b_hin[s2], b_ss[s2], b_gF], [b_hin[s2]])
                DMA(P, "pool", y_d[i * 128:(i + 1) * 128, :], hin[s2][:], b_yd, b_hin[s2])
        P.flush(final_bufs=[b_yd], scope="tail2")


N_PRE_UNITS = (SEQ - NTOK) // UNIT
N_OWN_UNITS = NTOK // UNIT
FIRST_KV_BLK = (SEQ - 2 * NTOK) // BLK
FIRST_Q_BLK = (SEQ - NTOK) // BLK

_IN_SPECS = [
    ("xpad", [SEQ, D], F32), ("wfm32", [NCT * 128, KT * 128], F32), ("wv32", [128, KT * 1024], F32),
    ("wglu32", [128, 8 * 1024], F32), ("wout32", [128, KT * D], F32), ("wgate32", [128, KT * DFF], F32),
    ("wup32", [128, KT * DFF], F32), ("wdown32", [128, NF * D], F32), ("g1", [128, KT], F32), ("g2", [128, KT], F32),
    ("final_g", [D], F32), ("a_re", [64, 64], F32), ("a_im", [64, 64], F32), ("log_dt", [64], F32),
    ("b_re", [64, 64, 16], F32), ("b_im", [64, 64, 16], F32), ("c_re", [64, 16, 64], F32), ("c_im", [64, 16, 64], F32),
    ("d_skip", [64, 16], F32), ("b_glu", [1024], F32), ("ident32", [128, 128], F32), ("ident_bf", [128, 128], BF16),
    ("aconsts", [128, 4, 128], BF16), ("hmask", [128, 1], F32), ("cos_d", [128, 3, 2 * NTOK], F32),
    ("sin_d", [128, 3, 2 * NTOK], F32),
]


def build_program():
    from contextlib import ExitStack
    nc = bass.Bass("TRN2", target_bir_lowering=False)
    ext = Buf("ext", False)
    dr = {"ext": ext}
    for name, shape, dt in _IN_SPECS:
        dr[name] = (nc.dram_tensor(name, shape, dt, kind="ExternalInput").ap(), ext)
    dr["y"] = (nc.dram_tensor("y", [NTOK, D], F32, kind="ExternalOutput").ap(), Buf("y", False))

    def scratch(name, shape, dt, keep=False):
        dr[name] = (nc.dram_tensor(name, shape, dt).ap(), Buf(name, False, keep))
    scratch("wfm_bf2", [NCT * 128, KT * 128], BF16); scratch("wv_bf2", [128, KT * 1024], BF16)
    scratch("wglu_bf2", [128, 8 * 1024], BF16); scratch("wout_bf2", [128, KT * D], BF16)
    scratch("wgate_bf2", [128, KT * DFF], BF16); scratch("wup_bf2", [128, KT * DFF], BF16)
    scratch("wdown_bf2", [128, NF * D], BF16)
    scratch("uT_d", [128, 8, SEQ], BF16); scratch("qT_d", [128, 8, NTOK], BF16); scratch("kT_d", [128, 8, 2 * NTOK], BF16)
    scratch("V_d", [2 * NTOK, 1024], BF16); scratch("mix_d", [128, 16, NTOK], BF16)
    scratch("h_d", [NTOK, D], F32); scratch("hn2T_d", [128, KT, NTOK], BF16)
    dr["wfm_bf"] = (dr["wfm_bf2"][0].rearrange("(c p) (k m) -> c p k m", p=128, k=KT), dr["wfm_bf2"][1])
    dr["wv_bf"] = (dr["wv_bf2"][0].rearrange("p (k n) -> p k n", k=KT), dr["wv_bf2"][1])
    dr["wglu_bf"] = (dr["wglu_bf2"][0].rearrange("p (k n) -> p k n", k=8), dr["wglu_bf2"][1])
    dr["wout_bf"] = (dr["wout_bf2"][0].rearrange("p (k n) -> p k n", k=KT), dr["wout_bf2"][1])
    dr["wgate_bf"] = (dr["wgate_bf2"][0].rearrange("p (k n) -> p k n", k=KT), dr["wgate_bf2"][1])
    dr["wup_bf"] = (dr["wup_bf2"][0].rearrange("p (k n) -> p k n", k=KT), dr["wup_bf2"][1])
    dr["wdown_bf"] = (dr["wdown_bf2"][0].rearrange("p (f n) -> p f n", f=NF), dr["wdown_bf2"][1])
    dr["ssm_d"] = (dr["mix_d"][0][:, 8:16, :], dr["mix_d"][1])
    with ExitStack() as st:
        P = Prog(nc, st)
        pairs = []
        for nm in ("wfm", "wv", "wglu", "wout", "wgate", "wup", "wdown"):
            pairs.append((dr[nm + "_bf2"][0], dr[nm + "_bf2"][1], dr[nm + "32"][0], Buf(nm + "32", False, keep=True)))
        cast_weights(nc, P, pairs)
        front_stage(nc, P, dr, NBLK_ALL, FIRST_KV_BLK, FIRST_Q_BLK)
        s5_stage(nc, P, dr, N_PRE_UNITS, N_OWN_UNITS)
        attn_stage(nc, P, dr)
        tail1_stage(nc, P, dr, SEQ - NTOK)
        tail2_stage(nc, P, dr)
    return nc


def _tile_rows(w, kt):
    n = w.shape[1]
    return np.ascontiguousarray(w.reshape(kt, 128, n).transpose(1, 0, 2)).reshape(128, kt * n)


def _head_perm(h):
    j = h % 3
    perm = np.zeros(128, np.int64)
    for m in range(128):
        if 32 * j <= m < 32 * j + 32:
            perm[m] = m - 32 * j
        elif m < 32 * j:
            perm[m] = 32 + m
        else:
            perm[m] = m
    return perm


def _prep_shared(inp):
    f32 = np.float32
    w_in = np.asarray(inp["w_in"], f32)[0]
    cols = np.full((NCT, 128), -1, np.int64)
    for base, ct0, ctsw in ((0, CT_Q, CT_QSW), (1024, CT_K, CT_KSW)):
        for h in range(8):
            cols[ct0 + h] = base + h * 128 + _head_perm(h)
            tt, j = h // 3, h % 3
            for i in range(32):
                cols[ctsw + tt, 32 * j + i] = base + h * 128 + (i + 16) % 32
    for k in range(8):
        cols[CT_U + k] = 3072 + k * 128 + np.arange(128)
    flat = cols.reshape(-1)
    wsel = np.where(flat[None, :] >= 0, w_in[:, np.maximum(flat, 0)], 0.0).astype(f32)
    wfm = wsel.reshape(KT, 128, NCT, 128).transpose(2, 1, 0, 3)
    sh = {}
    sh["wfm32"] = np.ascontiguousarray(wfm).reshape(NCT * 128, KT * 128)
    sh["wv32"] = _tile_rows(np.ascontiguousarray(w_in[:, 2048:3072]), KT)
    sh["wglu32"] = _tile_rows(np.asarray(inp["w_glu"], f32)[0], 8)
    sh["wout32"] = _tile_rows(np.asarray(inp["w_out"], f32)[0], KT)
    sh["wgate32"] = _tile_rows(np.asarray(inp["w_gate"], f32)[0], KT)
    sh["wup32"] = _tile_rows(np.asarray(inp["w_up"], f32)[0], KT)
    sh["wdown32"] = _tile_rows(np.asarray(inp["w_down"], f32)[0], NF)
    sh["g1"] = np.ascontiguousarray(np.asarray(inp["norm1_g"], f32)[0].reshape(KT, 128).T)
    sh["g2"] = np.ascontiguousarray(np.asarray(inp["norm2_g"], f32)[0].reshape(KT, 128).T)
    sh["final_g"] = np.ascontiguousarray(np.asarray(inp["final_g"], f32))
    for nm in ("a_re", "a_im", "log_dt", "b_re", "b_im", "c_re", "c_im", "d_skip", "b_glu"):
        sh[nm] = np.ascontiguousarray(np.asarray(inp[nm], f32)[0])
    sh["ident32"] = np.eye(128, dtype=f32)
    sh["ident_bf"] = np.eye(128, dtype=f32).astype(ml_dtypes.bfloat16)
    kk = np.arange(128)[:, None]; qq = np.arange(128)[None, :]
    mcur = np.where(kk <= qq, 0.0, -30000.0); mprev = np.where(kk >= qq, 0.0, -30000.0)
    sh["aconsts"] = np.ascontiguousarray(
        np.stack([np.eye(128), mcur, mprev, np.ones((128, 128))], 1).astype(f32).astype(ml_dtypes.bfloat16))
    return sh


def _rope_tables(t0):
    f32 = np.float32
    pos = (np.arange(2 * NTOK) + (t0 - NTOK)).astype(f32)
    pos = np.maximum(pos, f32(0))
    inv_freq = (f32(500000.0) ** (-(np.arange(0, 32, 2).astype(f32)) / f32(32))).astype(f32)
    i = np.arange(32)
    ang = (pos[None, :] * inv_freq[i % 16][:, None]).astype(f32)
    c32 = np.cos(ang).astype(f32)
    s32 = np.sin(ang).astype(f32) * np.where(i < 16, -1.0, 1.0).astype(f32)[:, None]
    cosT = np.ones((128, 3, 2 * NTOK), f32)
    sinT = np.zeros((128, 3, 2 * NTOK), f32)
    for j in range(3):
        cosT[32 * j:32 * j + 32, j, :] = c32
        sinT[32 * j:32 * j + 32, j, :] = s32
    return cosT, sinT


def kernel(**inputs):
    x = np.asarray(inputs["x"], np.float32)[0]
    sh = _prep_shared(inputs)
    nc = build_program()
    in_maps = []
    for c in range(NCORES):
        t0 = c * NTOK
        xpad = np.zeros((SEQ, D), np.float32)
        n_real = t0 + NTOK
        xpad[SEQ - n_real:] = x[:n_real]
        cosT, sinT = _rope_tables(t0)
        m = dict(sh)
        m["xpad"] = xpad
        m["cos_d"] = cosT
        m["sin_d"] = sinT
        m["hmask"] = np.full((128, 1), -30000.0 if c == 0 else 0.0, np.float32)
        in_maps.append(m)
    res = run_bass_kernel_spmd(nc, in_maps, core_ids=list(range(NCORES)))
    y = np.concatenate([np.asarray(res.results[c]["y"], np.float32) for c in range(NCORES)], axis=0)
    return y.reshape(1, SEQ, D)
```

```python
import math
import numpy as np
import ml_dtypes
import concourse.bass as bass
import concourse.mybir as mybir
from concourse.bass_utils import run_bass_kernel_spmd

F32 = mybir.dt.float32
BF16 = mybir.dt.bfloat16
ALU = mybir.AluOpType
ACT = mybir.ActivationFunctionType
AX = mybir.AxisListType

NCORES = 8
D = 2048
SEQ = 16384
NTOK = SEQ // NCORES
BLK = 512
NBLK_ALL = SEQ // BLK
KT = D // 128

ENGS = ("sync", "act", "pool", "dve", "pe")


TWO_PI = 2.0 * math.pi
_UID = [0]
SKIP = set()
PROFILE = False


def _uid(n):
    _UID[0] += 1
    return "%s_%d" % (n, _UID[0])


class Buf:
    __slots__ = ("name", "w", "r", "dsem", "sb", "keep")

    def __init__(self, name, sb=True, keep=False):
        self.name = name
        self.w = None
        self.r = {}
        self.dsem = None
        self.sb = sb
        self.keep = keep


class Prog:
    def __init__(self, nc, stack, ndsem=80):
        self.nc = nc
        self.csem = {k: stack.enter_context(nc.semaphore("s_" + k)) for k in ("c_act", "c_pool", "c_dve", "c_pe")}
        self.dsem = [stack.enter_context(nc.semaphore("sd%d" % i)) for i in range(ndsem)]
        self.dval = [0] * ndsem
        self.dfree = list(range(ndsem))
        self.dkeep = set()
        self.ops = {e: [] for e in ENGS}
        self.cnt = {e: 0 for e in ENGS}
        self.known = {e: {} for e in ENGS}
        self.bufs = []
        self.nblocks = 0

    def _reg(self, b):
        if b not in self.bufs:
            self.bufs.append(b)

    def _deps(self, eng, reads, writes, skip_same_pe=False):
        need = {}

        def add(tok):
            if tok is None:
                return
            k, v = tok
            if skip_same_pe and k == "c_pe":
                return
            if need.get(k, 0) < v:
                need[k] = v
        for b in reads:
            add(b.w)
        for b in writes:
            add(b.w)
            for k, v in b.r.items():
                add((k, v))
        waits = []
        kn = self.known[eng]
        for k, v in need.items():
            if kn.get(k, 0) < v:
                kn[k] = v
                waits.append((k, v))
        return waits

    def _mark(self, tok, reads, writes):
        for b in reads:
            b.r[tok[0]] = tok[1]
            self._reg(b)
        for b in writes:
            b.w = tok
            b.r = {}
            self._reg(b)

    def op(self, eng, fn, reads=(), writes=()):
        waits = self._deps(eng, reads, writes)
        self.cnt[eng] += 1
        tok = ("c_" + eng, self.cnt[eng])
        self.ops[eng].append((waits, fn, (tok[0], 1)))
        self._mark(tok, reads, writes)

    def pe_group(self, fns, reads=(), writes=()):
        waits = self._deps("pe", reads, writes, skip_same_pe=True)
        self.cnt["pe"] += 1
        tok = ("c_pe", self.cnt["pe"])
        n = len(fns)
        for i, fn in enumerate(fns):
            self.ops["pe"].append((waits if i == 0 else [], fn, (tok[0], 1) if i == n - 1 else None))
        self._mark(tok, reads, writes)

    def dma(self, q, fn, dst, src):
        waits = self._deps(q, [src], [dst])
        key = dst if dst.sb else src
        if key.dsem is None:
            key.dsem = self.dfree.pop(0)
            if key.keep:
                self.dkeep.add(key.dsem)
        i = key.dsem
        self.dval[i] += 16
        tok = (i, self.dval[i])
        self.ops[q].append((waits, fn, (i, 16)))
        src.r[tok[0]] = tok[1]
        dst.w = tok
        dst.r = {}
        self._reg(src)
        self._reg(dst)
        self._reg(key)

    def wait_all(self, eng, bufs):
        waits = self._deps(eng, bufs, [])
        self.ops[eng].append((waits, None, None))

    def _sem(self, k):
        return self.csem[k] if isinstance(k, str) else self.dsem[k]

    def flush(self, final_bufs=(), scope=None):
        if PROFILE and scope:
            with self.nc.named_scope(scope):
                return self._flush(final_bufs)
        return self._flush(final_bufs)

    def _flush(self, final_bufs=()):
        kn = self.known["sync"]
        waits = []
        for i, v in enumerate(self.dval):
            if i in self.dkeep or i in self.dfree:
                continue
            if kn.get(i, 0) < v:
                kn[i] = v
                waits.append((i, v))
        for b in final_bufs:
            if b.w is not None and kn.get(b.w[0], 0) < b.w[1]:
                kn[b.w[0]] = b.w[1]
                waits.append(b.w)
        self.ops["sync"].append((waits, None, None))
        nc = self.nc
        self.nblocks += 1
        with nc.Block() as block:
            def run(name):
                def body(e):
                    for waits, fn, inc in self.ops[name]:
                        for k, v in waits:
                            e.wait_ge(self._sem(k), v)
                        if fn is not None:
                            ins = fn(e)
                            if inc is not None:
                                ins.then_inc(self._sem(inc[0]), inc[1])
                return body
            block.sync(run("sync"))
            block.scalar(run("act"))
            block.gpsimd(run("pool"))
            block.vector(run("dve"))
            block.tensor(run("pe"))
        self.ops = {e: [] for e in ENGS}
        for e in ENGS:
            kn = self.known[e]
            for k in ("act", "pool", "dve", "pe"):
                kn["c_" + k] = self.cnt[k]
            for i, v in enumerate(self.dval):
                if i not in self.dkeep:
                    kn[i] = v
        for b in self.bufs:
            if b.w is not None and b.w[0] in self.dkeep:
                pass
            else:
                b.w = None
            b.r = {k: v for k, v in b.r.items() if k in self.dkeep}
            if b.dsem is not None and b.dsem not in self.dkeep:
                self.dfree.append(b.dsem)
                b.dsem = None
        self.bufs = [b for b in self.bufs if b.w is not None or b.r or b.dsem is not None]


def TT(P, eng, out, in0, in1, op, reads, writes):
    P.op(eng, lambda e: e.tensor_tensor(out=out, in0=in0, in1=in1, op=op), reads, writes)


def TS(P, eng, out, in0, s1, s2, op0, op1, reads, writes):
    if s2 is None:
        P.op(eng, lambda e: e.tensor_scalar(out=out, in0=in0, scalar1=s1, scalar2=None, op0=op0), reads, writes)
    else:
        P.op(eng, lambda e: e.tensor_scalar(out=out, in0=in0, scalar1=s1, scalar2=s2, op0=op0, op1=op1), reads, writes)


def STT(P, eng, out, in0, scalar, in1, op0, op1, reads, writes):
    P.op(eng, lambda e: e.scalar_tensor_tensor(out=out, in0=in0, scalar=scalar, in1=in1, op0=op0, op1=op1), reads, writes)


def ACTF(P, out, in_, func, reads, writes, scale=1.0, bias=None):
    if bias is None:
        P.op("act", lambda e: e.activation(out=out, in_=in_, func=func, scale=scale), reads, writes)
    else:
        P.op("act", lambda e: e.activation(out=out, in_=in_, func=func, scale=scale, bias=bias), reads, writes)


def CP(P, eng, out, in_, reads, writes):
    if eng == "act":
        P.op("act", lambda e: e.activation(out=out, in_=in_, func=ACT.Copy), reads, writes)
    else:
        P.op(eng, lambda e: e.tensor_copy(out=out, in_=in_), reads, writes)


def DMA(P, q, out, in_, dst, src, slow=False):
    if slow:
        P.dma(q, lambda e: e.dma_start(out=out, in_=in_, allow_slow_non_contiguous=True), dst, src)
    else:
        P.dma(q, lambda e: e.dma_start(out=out, in_=in_), dst, src)


def cmul(P, eng, o_re, o_im, a_re, a_im, b_re, b_im, t1, t2, reads, writes, tb):
    TT(P, eng, t1, a_re, b_re, ALU.mult, reads, [tb])
    TT(P, eng, t2, a_im, b_im, ALU.mult, reads, [tb])
    TT(P, eng, o_re, t1, t2, ALU.subtract, [tb], writes)
    TT(P, eng, t1, a_re, b_im, ALU.mult, reads, [tb])
    TT(P, eng, t2, a_im, b_re, ALU.mult, reads, [tb])
    TT(P, eng, o_im, t1, t2, ALU.add, [tb], writes)


T0 = 8
UNIT = 1024
NSC = UNIT // T0
I32 = mybir.dt.int32


def s5_stage(nc, P, dr, n_pre_units, n_own_units):
    from contextlib import ExitStack
    B = Buf
    n_units = n_pre_units + n_own_units
    ext = dr["a_re"][1]
    with ExitStack() as st0:
        sbp = lambda n, s, d: st0.enter_context(nc.sbuf_tensor(_uid(n), s, d))
        sc = sbp("sc", [128, 24, 32], F32); b_sc = B("sc")
        pw = sbp("pw", [128, 3, T0 + 1, 32], F32); b_pw = B("pw")
        WBT = sbp("WBT", [128, 8, T0, 2, 128], BF16); b_WBT = B("WBT")
        WCT = sbp("WCT", [128, 8, 2, 128], BF16); b_WCT = B("WCT")
        Rc = sbp("Rc", [128, 32, NSC], F32); Rs = sbp("Rs", [128, 32, NSC], F32); b_R = B("R")
        Dcol = sbp("Dcol", [128, 8], F32); b_D = B("Dcol")
        wglu = sbp("wglu", [128, 8, 1024], BF16); b_wglu = B("wglu")
        bglu = sbp("bglu", [128, 8], F32); b_bglu = B("bglu")
        cr = sbp("cr", [128, 32, 1], F32); ci = sbp("ci", [128, 32, 1], F32); b_c = B("carry")
        P.op("dve", lambda e: e.memset(cr[:], 0.0), [], [b_c])
        P.op("dve", lambda e: e.memset(ci[:], 0.0), [], [b_c])
        S = lambda j: sc[:, j, :]
        DT, MAG, PHI, SINP, COSP, ABR, ABI, CR, CI, T1, T2, T3, RHO, C8, S8, NUMR, NUMI, DEN = range(18)
        DMA(P, "sync", Dcol[:], dr["d_skip"][0].rearrange("(k gl) p -> (gl p) k", gl=8), b_D, ext, slow=True)
        DMA(P, "sync", wglu[:], dr["wglu_bf"][0], b_wglu, dr["wglu_bf"][1])
        DMA(P, "sync", bglu[:], dr["b_glu"][0].rearrange("(k p) -> p k", p=128), b_bglu, ext, slow=True)

        with ExitStack() as st:
            sb = lambda n, s, d: st.enter_context(nc.sbuf_tensor(_uid(n), s, d))
            ps = lambda n, s, d: st.enter_context(nc.psum_tensor(_uid(n), s, d))
            are = sb("are", [128, 32], F32); aim = sb("aim", [128, 32], F32); ldt = sb("ldt", [128, 32], F32)
            b_par = B("par")
            DMA(P, "sync", are[:], dr["a_re"][0].rearrange("(pi g) n -> (g n) pi", g=2), b_par, ext, slow=True)
            DMA(P, "sync", aim[:], dr["a_im"][0].rearrange("(pi g) n -> (g n) pi", g=2), b_par, ext, slow=True)
            for g2 in range(2):
                src = dr["log_dt"][0].rearrange("(pi g) -> g pi", g=2)[g2:g2 + 1, :].to_broadcast([64, 32])
                DMA(P, "sync", ldt[64 * g2:64 * g2 + 64, :], src, b_par, ext, slow=True)
            Bre = sb("Bre", [128, 32, 16], F32); Bim = sb("Bim", [128, 32, 16], F32); b_B = B("B")
            DMA(P, "sync", Bre[:], dr["b_re"][0].rearrange("(pi g) n q -> (g n) pi q", g=2), b_B, ext, slow=True)
            DMA(P, "sync", Bim[:], dr["b_im"][0].rearrange("(pi g) n q -> (g n) pi q", g=2), b_B, ext, slow=True)
            id32 = sb("id32", [128, 128], F32); b_id = B("id32")
            DMA(P, "sync", id32[:], dr["ident32"][0], b_id, ext)
            Cx = [sb("Cx%d" % i, [128, 8, 128], F32) for i in range(2)]; b_Cx = B("Cx")
            for i in range(2):
                P.op("pool", lambda e, i=i: e.memset(Cx[i][:], 0.0), [], [b_Cx])
            for i, nm in enumerate(("c_re", "c_im")):
                cv = dr[nm][0].rearrange("(pi8 pi4 g) p n -> pi4 g p pi8 n", pi4=4, g=2)
                for pi4 in range(4):
                    for g2 in range(2):
                        p0 = pi4 * 32 + g2 * 16
                        DMA(P, "sync", Cx[i][p0:p0 + 16, :, 64 * g2:64 * g2 + 64], cv[pi4, g2], b_Cx, ext, slow=True)
            ki = sb("ki", [128, 32], I32); b_ki = B("ki")

            def sin_of(out, ang):
                TS(P, "dve", S(T1), ang, 1.0 / TWO_PI, None, ALU.mult, None, [b_sc], [b_sc])
                CP(P, "dve", ki[:], S(T1), [b_sc], [b_ki])
                CP(P, "dve", S(T1), ki[:], [b_ki], [b_sc])
                STT(P, "dve", S(T2), S(T1), -TWO_PI, ang, ALU.mult, ALU.add, [b_sc], [b_sc])
                TS(P, "dve", S(T3), S(T2), math.pi, -TWO_PI, ALU.is_gt, ALU.mult, [b_sc], [b_sc])
                TT(P, "dve", S(T2), S(T2), S(T3), ALU.add, [b_sc], [b_sc])
                TS(P, "dve", S(T3), S(T2), -math.pi, TWO_PI, ALU.is_lt, ALU.mult, [b_sc], [b_sc])
                TT(P, "dve", S(T2), S(T2), S(T3), ALU.add, [b_sc], [b_sc])
                ACTF(P, out, S(T2), ACT.Sin, [b_sc], [b_sc])

            ACTF(P, S(DT), ldt[:], ACT.Exp, [b_par], [b_sc])
            TT(P, "dve", S(MAG), are[:], S(DT), ALU.mult, [b_par, b_sc], [b_sc])
            ACTF(P, S(RHO), S(MAG), ACT.Exp, [b_sc], [b_sc], scale=float(T0))
            ACTF(P, S(MAG), S(MAG), ACT.Exp, [b_sc], [b_sc])
            TT(P, "dve", S(PHI), aim[:], S(DT), ALU.mult, [b_par, b_sc], [b_sc])
            sin_of(S(SINP), S(PHI))
            TS(P, "dve", S(NUMR), S(PHI), math.pi / 2, None, ALU.add, None, [b_sc], [b_sc])
            sin_of(S(COSP), S(NUMR))
            TT(P, "dve", S(ABR), S(MAG), S(COSP), ALU.mult, [b_sc], [b_sc])
            TT(P, "dve", S(ABI), S(MAG), S(SINP), ALU.mult, [b_sc], [b_sc])
            TS(P, "dve", S(T1), S(ABR), -1.0, None, ALU.add, None, [b_sc], [b_sc])
            TT(P, "dve", S(NUMR), S(T1), are[:], ALU.mult, [b_sc, b_par], [b_sc])
            TT(P, "dve", S(T2), S(ABI), aim[:], ALU.mult, [b_sc, b_par], [b_sc])
            TT(P, "dve", S(NUMR), S(NUMR), S(T2), ALU.add, [b_sc], [b_sc])
            TT(P, "dve", S(NUMI), S(ABI), are[:], ALU.mult, [b_sc, b_par], [b_sc])
            TT(P, "dve", S(T2), S(T1), aim[:], ALU.mult, [b_sc, b_par], [b_sc])
            TT(P, "dve", S(NUMI), S(NUMI), S(T2), ALU.subtract, [b_sc], [b_sc])
            TT(P, "dve", S(DEN), are[:], are[:], ALU.mult, [b_par], [b_sc])
            TT(P, "dve", S(T2), aim[:], aim[:], ALU.mult, [b_par], [b_sc])
            TT(P, "dve", S(DEN), S(DEN), S(T2), ALU.add, [b_sc], [b_sc])
            P.op("dve", lambda e: e.reciprocal(out=S(DEN), in_=S(DEN)), [b_sc], [b_sc])
            TT(P, "dve", S(CR), S(NUMR), S(DEN), ALU.mult, [b_sc], [b_sc])
            TT(P, "dve", S(CI), S(NUMI), S(DEN), ALU.mult, [b_sc], [b_sc])
            CP(P, "dve", S(C8), S(COSP), [b_sc], [b_sc])
            CP(P, "dve", S(S8), S(SINP), [b_sc], [b_sc])
            for _ in range(3):
                TT(P, "dve", S(T1), S(C8), S(C8), ALU.mult, [b_sc], [b_sc])
                TT(P, "dve", S(T2), S(S8), S(S8), ALU.mult, [b_sc], [b_sc])
                TT(P, "dve", S(T3), S(C8), S(S8), ALU.mult, [b_sc], [b_sc])
                TT(P, "dve", S(C8), S(T1), S(T2), ALU.subtract, [b_sc], [b_sc])
                TS(P, "dve", S(S8), S(T3), 2.0, None, ALU.mult, None, [b_sc], [b_sc])
            P.op("dve", lambda e: e.memset(pw[:, 0, 0, :], 1.0), [], [b_pw])
            P.op("dve", lambda e: e.memset(pw[:, 1, 0, :], 0.0), [], [b_pw])
            for k in range(1, T0 + 1):
                pr, pi_ = pw[:, 0, k - 1, :], pw[:, 1, k - 1, :]
                TT(P, "dve", S(T1), pr, S(ABR), ALU.mult, [b_pw, b_sc], [b_sc])
                TT(P, "dve", S(T2), pi_, S(ABI), ALU.mult, [b_pw, b_sc], [b_sc])
                TT(P, "dve", pw[:, 0, k, :], S(T1), S(T2), ALU.subtract, [b_sc], [b_pw])
                TT(P, "dve", S(T1), pr, S(ABI), ALU.mult, [b_pw, b_sc], [b_sc])
                TT(P, "dve", S(T2), pi_, S(ABR), ALU.mult, [b_pw, b_sc], [b_sc])
                TT(P, "dve", pw[:, 1, k, :], S(T1), S(T2), ALU.add, [b_sc], [b_pw])
            TS(P, "dve", pw[:, 2, :, :], pw[:, 1, :, :], -1.0, None, ALU.mult, None, [b_pw], [b_pw])
            Bb = sb("Bb", [128, 2, 32, 16], F32); b_Bb = B("Bb")
            tA = sb("tA", [128, T0, 32, 16], F32); tB = sb("tB", [128, T0, 32, 16], F32); b_t = B("tAB")
            bc16 = lambda j: S(j).unsqueeze(2).to_broadcast([128, 32, 16])
            cmul(P, "dve", Bb[:, 0], Bb[:, 1], bc16(CR), bc16(CI), Bre[:], Bim[:], tA[:, 0], tB[:, 0],
                 [b_sc, b_B], [b_Bb], b_t)
            WBx = sb("WBx", [128, T0, 2, 32, 32], F32); b_WBx = B("WBx")
            P.op("pool", lambda e: e.memset(WBx[:], 0.0), [], [b_WBx])
            for g2 in range(2):
                ps_ = slice(64 * g2, 64 * g2 + 64)
                cs_ = slice(16 * g2, 16 * g2 + 16)
                pwb = lambda ri: pw[ps_, ri, 0:T0, :].unsqueeze(3).to_broadcast([64, T0, 32, 16])
                bbb = lambda ri: Bb[ps_, ri].unsqueeze(1).to_broadcast([64, T0, 32, 16])
                cmul(P, "dve", WBx[ps_, :, 0, :, cs_], WBx[ps_, :, 1, :, cs_], pwb(0), pwb(1), bbb(0), bbb(1),
                     tA[ps_], tB[ps_], [b_pw, b_Bb], [b_WBx], b_t)
            pTr = [ps("pTr%d" % i, [128, 4, 128], F32) for i in range(2)]
            b_pTr = [B("pTr0"), B("pTr1")]
            nt = 0
            for pi8 in range(8):
                for tau in range(T0):
                    s = nt % 2; nt += 1
                    fns = []
                    for ri in range(2):
                        fns.append(lambda e, s=s, ri=ri, pi8=pi8, tau=tau: e.transpose(
                            out=pTr[s][:, ri, :],
                            in_=WBx[:, tau, ri, 4 * pi8:4 * pi8 + 4, :].rearrange("p a b -> p (a b)"),
                            identity=id32[:]))
                    P.pe_group(fns, [b_WBx, b_id], [b_pTr[s]])
                    CP(P, "act" if nt % 2 else "dve", WBT[:, pi8, tau, :, :], pTr[s][:, 0:2, :], [b_pTr[s]], [b_WBT])
            for pi8 in range(0, 8, 2):
                s = nt % 2; nt += 1
                fns = []
                for j in range(2):
                    for ri in range(2):
                        fns.append(lambda e, s=s, ri=ri, j=j, pi8=pi8: e.transpose(
                            out=pTr[s][:, 2 * j + ri, :], in_=Cx[ri][:, pi8 + j, :], identity=id32[:]))
                P.pe_group(fns, [b_Cx, b_id], [b_pTr[s]])
                for j in range(2):
                    CP(P, "dve", WCT[:, pi8 + j, 0, :], pTr[s][:, 2 * j, :], [b_pTr[s]], [b_WCT])
                    TS(P, "dve", WCT[:, pi8 + j, 1, :], pTr[s][:, 2 * j + 1, :], -1.0, None, ALU.mult, None,
                       [b_pTr[s]], [b_WCT])
            CP(P, "dve", Rc[:, :, 0], S(C8), [b_sc], [b_R])
            CP(P, "dve", Rs[:, :, 0], S(S8), [b_sc], [b_R])
            m = 1
            tAv = tA[:].rearrange("p a b c -> p (a b c)")
            tBv = tB[:].rearrange("p a b c -> p (a b c)")
            while m < NSC:
                bc = lambda t: t[:, :, m - 1:m].to_broadcast([128, 32, m])
                t1v = tAv[:, 0:32 * m].rearrange("p (a b) -> p a b", a=32)
                t2v = tBv[:, 0:32 * m].rearrange("p (a b) -> p a b", a=32)
                cmul(P, "dve", Rc[:, :, m:2 * m], Rs[:, :, m:2 * m], Rc[:, :, 0:m], Rs[:, :, 0:m], bc(Rc), bc(Rs),
                     t1v, t2v, [b_R], [b_R], b_t)
                m *= 2
            P.flush(scope="s5pre")

        NQ = 8
        uT_d, b_uTd = dr["uT_d"]
        ssm_d, b_ssmd = dr["ssm_d"]

        def alloc_set(sb, ps, tag):
            Sx = {}
            for nm in ("bA", "bB", "bC", "bD"):
                Sx[nm] = sb(nm + tag, [128, NQ, NSC], F32); Sx["b_" + nm] = B(nm + tag)
            for nm in ("bE", "bF"):
                Sx[nm] = sb(nm + tag, [128, NQ, NSC + 1], F32); Sx["b_" + nm] = B(nm + tag)
            Sx["psS"] = [ps("psS%d%s" % (i, tag), [128, 4, NSC], F32) for i in range(2)]; Sx["b_psS"] = B("psS" + tag)
            Sx["b_Ap"] = [B("Ap%d%s" % (i, tag)) for i in range(NQ)]; Sx["b_Bp"] = [B("Bp%d%s" % (i, tag)) for i in range(NQ)]
            return Sx

        def quarter(Sx, uTb, b_uTb, q4, own):
            bA, bB, bC, bD, bE, bF = Sx["bA"], Sx["bB"], Sx["bC"], Sx["bD"], Sx["bE"], Sx["bF"]
            b_A, b_Bq, b_C, b_Dq, b_E, b_F = Sx["b_bA"], Sx["b_bB"], Sx["b_bC"], Sx["b_bD"], Sx["b_bE"], Sx["b_bF"]
            psS, b_psS = Sx["psS"], Sx["b_psS"]
            u3 = uTb[:].rearrange("p k (c j) -> p k j c", j=T0)
            psl = slice(NQ * q4, NQ * q4 + NQ)
            for h8 in range(NQ // 4):
                pi8 = (NQ // 4) * q4 + h8
                fns = []
                for pi4 in range(4):
                    r0 = 32 * pi4
                    for ri in range(2):
                        for j in range(T0):
                            fns.append(lambda e, r0=r0, ri=ri, j=j, pi8=pi8, pi4=pi4, u3=u3, psS=psS: e.matmul(
                                psS[ri][:, pi4, :], lhsT=WBT[r0:r0 + 32, pi8, T0 - 1 - j, ri, :],
                                rhs=u3[r0:r0 + 32, pi8, j, :], start=(j == 0), stop=(j == T0 - 1),
                                tile_position=(r0, 0)))
                P.pe_group(fns, [b_WBT, b_uTb], [b_psS])
                CP(P, "act", bA[:, 4 * h8:4 * h8 + 4, :], psS[0][:], [b_psS], [b_A] + Sx["b_Ap"])
                CP(P, "act", bB[:, 4 * h8:4 * h8 + 4, :], psS[1][:], [b_psS], [b_Bq] + Sx["b_Bp"])
            rc, rs = Rc[:, psl, :], Rs[:, psl, :]
            E1, F1 = bE[:, :, 1:], bF[:, :, 1:]
            b_Ap, b_Bp = Sx["b_Ap"], Sx["b_Bp"]
            TT(P, "dve", bC[:], bA[:], rc, ALU.mult, [b_A, b_R] + b_Ap, [b_C])
            TT(P, "pool", E1, bB[:], rs, ALU.mult, [b_Bq, b_R] + b_Bp, [b_E])
            TT(P, "dve", bD[:], bB[:], rc, ALU.mult, [b_Bq, b_R] + b_Bp, [b_Dq])
            TT(P, "pool", F1, bA[:], rs, ALU.mult, [b_A, b_R] + b_Ap, [b_F])
            TT(P, "dve", bC[:], bC[:], E1, ALU.add, [b_C, b_E], [b_C])
            TT(P, "dve", bD[:], bD[:], F1, ALU.subtract, [b_Dq, b_F], [b_Dq])
            for p_ in range(NQ):
                pi = NQ * q4 + p_
                P.op("dve", lambda e, pi=pi, p_=p_, bA=bA, bC=bC: e.tensor_tensor_scan(
                    out=bA[:, p_, :], data0=sc[:, RHO, pi:pi + 1].to_broadcast([128, NSC]), data1=bC[:, p_, :],
                    initial=cr[:, pi, :], op0=ALU.mult, op1=ALU.add), [b_C, b_sc, b_c, b_A], [b_Ap[p_]])
                P.op("dve", lambda e, pi=pi, p_=p_, bB=bB, bD=bD: e.tensor_tensor_scan(
                    out=bB[:, p_, :], data0=sc[:, RHO, pi:pi + 1].to_broadcast([128, NSC]), data1=bD[:, p_, :],
                    initial=ci[:, pi, :], op0=ALU.mult, op1=ALU.add), [b_Dq, b_sc, b_c, b_Bq], [b_Bp[p_]])
            co = slice(0, NSC) if own else slice(NSC - 1, NSC)
            rco, rso = rc[:, :, co], rs[:, :, co]
            TT(P, "dve", E1[:, :, co], bA[:, :, co], rco, ALU.mult, [b_R] + b_Ap, [b_E])
            TT(P, "pool", bC[:, :, co], bB[:, :, co], rso, ALU.mult, [b_R] + b_Bp, [b_C])
            TT(P, "dve", F1[:, :, co], bB[:, :, co], rco, ALU.mult, [b_R] + b_Bp, [b_F])
            TT(P, "pool", bD[:, :, co], bA[:, :, co], rso, ALU.mult, [b_R] + b_Ap, [b_Dq])
            TT(P, "dve", E1[:, :, co], E1[:, :, co], bC[:, :, co], ALU.subtract, [b_E, b_C], [b_E])
            TT(P, "dve", F1[:, :, co], F1[:, :, co], bD[:, :, co], ALU.add, [b_F, b_Dq], [b_F])
            if own:
                CP(P, "dve", bE[:, :, 0:1], cr[:, psl, :], [b_c], [b_E])
                CP(P, "dve", bF[:, :, 0:1], ci[:, psl, :], [b_c], [b_F])
            CP(P, "dve", cr[:, psl, :], bE[:, :, NSC:NSC + 1], [b_E], [b_c])
            CP(P, "dve", ci[:, psl, :], bF[:, :, NSC:NSC + 1], [b_F], [b_c])

        with ExitStack() as st:
            sb = lambda n, s, d: st.enter_context(nc.sbuf_tensor(_uid(n), s, d))
            ps = lambda n, s, d: st.enter_context(nc.psum_tensor(_uid(n), s, d))
            uTp = [sb("uTp%d" % i, [128, 8, UNIT], BF16) for i in range(2)]; b_uTp = [B("uTp0"), B("uTp1")]
            sets = [alloc_set(sb, ps, "p%d" % i) for i in range(2)]
            nq = 0
            for u in range(n_pre_units):
                su = u % 2
                DMA(P, "sync", uTp[su][:], uT_d[:, :, u * UNIT:(u + 1) * UNIT], b_uTp[su], b_uTd)
                for q4 in range(32 // NQ):
                    quarter(sets[nq % 2], uTp[su], b_uTp[su], q4, False)
                    nq += 1
            P.flush(scope="s5prefix")

        with ExitStack() as st:
            sb = lambda n, s, d: st.enter_context(nc.sbuf_tensor(_uid(n), s, d))
            ps = lambda n, s, d: st.enter_context(nc.psum_tensor(_uid(n), s, d))
            uT = sb("uT", [128, 8, UNIT], BF16); b_uT = B("uT")
            Sx = alloc_set(sb, ps, "o")
            bE, bF, b_E, b_F = Sx["bE"], Sx["bF"], Sx["b_bE"], Sx["b_bF"]
            psH = [ps("psH%d" % i, [128, T0, NSC], F32) for i in range(2)]; b_psH = B("psH")
            psY = ps("psY", [128, UNIT], F32); b_psY = B("psY")
            hset = [[sb("h%s%d" % (c, i), [128, T0, NSC], F32) for c in "ABCD"] for i in range(2)]
            b_hset = [[B("h%s%d" % (c, i)) for c in "ABCD"] for i in range(2)]
            Hre = [sb("Hre%d" % i, [128, UNIT], BF16) for i in range(2)]
            Him = [sb("Him%d" % i, [128, UNIT], BF16) for i in range(2)]
            b_H = [B("H0"), B("H1")]
            ysb = sb("ysb", [128, UNIT], F32); b_y = B("ysb")
            gl1 = sb("gl1", [128, UNIT], F32); gl2 = sb("gl2", [128, UNIT], F32); b_g = B("g12")
            ygT = sb("ygT", [128, 8, UNIT], BF16); b_yg = B("ygT")
            gate = sb("gate", [128, 512], F32); b_gate = B("gate")
            sso = [sb("sso%d" % i, [128, UNIT], BF16) for i in range(2)]; b_sso = [B("sso0"), B("sso1")]
            for ou in range(n_own_units):
                u = n_pre_units + ou
                DMA(P, "sync", uT[:], uT_d[:, :, u * UNIT:(u + 1) * UNIT], b_uT, b_uTd)
                u3 = uT[:].rearrange("p k (c j) -> p k j c", j=T0)
                for q4 in range(32 // NQ):
                    quarter(Sx, uT, b_uT, q4, True)
                    for h8 in range(NQ // 4):
                        pi8 = (NQ // 4) * q4 + h8
                        for pi4 in range(4):
                            pi = 4 * pi8 + pi4
                            p_ = 4 * h8 + pi4
                            r0 = 32 * pi4
                            hs = pi % 2
                            fns = []
                            for ri in range(2):
                                for i in range(T0):
                                    for tau in range(i + 1):
                                        fns.append(lambda e, r0=r0, ri=ri, i=i, tau=tau, pi8=pi8, u3=u3: e.matmul(
                                            psH[ri][:, i, :], lhsT=WBT[r0:r0 + 32, pi8, tau, ri, :],
                                            rhs=u3[r0:r0 + 32, pi8, i - tau, :], start=(tau == 0), stop=(tau == i),
                                            tile_position=(r0, 0)))
                            P.pe_group(fns, [b_WBT, b_uT], [b_psH])
                            xr_b = bE[:, p_, 0:NSC].unsqueeze(1).to_broadcast([128, T0, NSC])
                            xi_b = bF[:, p_, 0:NSC].unsqueeze(1).to_broadcast([128, T0, NSC])
                            pwr_b = pw[:, 0, 1:T0 + 1, pi].unsqueeze(2).to_broadcast([128, T0, NSC])
                            pwi_b = pw[:, 1, 1:T0 + 1, pi].unsqueeze(2).to_broadcast([128, T0, NSC])
                            hre_v = Hre[hs][:].rearrange("p (c i) -> p i c", i=T0)
                            him_v = Him[hs][:].rearrange("p (c i) -> p i c", i=T0)
                            hp = pi % 2
                            hA_, hB_, hC_, hD_ = hset[hp]
                            bh = b_hset[hp]
                            TT(P, "pool", hA_[:], xr_b, pwr_b, ALU.mult, [b_E, b_pw], [bh[0]])
                            TT(P, "pool", hB_[:], xi_b, pwi_b, ALU.mult, [b_F, b_pw], [bh[1]])
                            TT(P, "pool", hC_[:], xi_b, pwr_b, ALU.mult, [b_F, b_pw], [bh[2]])
                            TT(P, "pool", hD_[:], xr_b, pwi_b, ALU.mult, [b_E, b_pw], [bh[3]])
                            TT(P, "pool", hA_[:], hA_[:], hB_[:], ALU.subtract, [bh[0], bh[1]], [bh[0]])
                            TT(P, "pool", hC_[:], hC_[:], hD_[:], ALU.add, [bh[2], bh[3]], [bh[2]])
                            TT(P, "dve", hre_v, psH[0][:], hA_[:], ALU.add, [b_psH, bh[0]], [b_H[hs]])
                            TT(P, "dve", him_v, psH[1][:], hC_[:], ALU.add, [b_psH, bh[2]], [b_H[hs]])
                            fns = []
                            for half in range(2):
                                hsl = slice(512 * half, 512 * half + 512)
                                fns.append(lambda e, hs=hs, hsl=hsl, r0=r0, pi8=pi8: e.matmul(
                                    psY[r0:r0 + 32, hsl], lhsT=WCT[:, pi8, 0, r0:r0 + 32], rhs=Hre[hs][:, hsl],
                                    start=True, stop=False, tile_position=(0, r0)))
                                fns.append(lambda e, hs=hs, hsl=hsl, r0=r0, pi8=pi8: e.matmul(
                                    psY[r0:r0 + 32, hsl], lhsT=WCT[:, pi8, 1, r0:r0 + 32], rhs=Him[hs][:, hsl],
                                    start=False, stop=True, tile_position=(0, r0)))
                            P.pe_group(fns, [b_WCT, b_H[hs]], [b_psY])
                        STT(P, "dve", ysb[:], uT[:, pi8, :], Dcol[:, pi8:pi8 + 1], psY[:], ALU.mult, ALU.add,
                            [b_uT, b_D, b_psY], [b_y])
                        ACTF(P, gl1[:], ysb[:], ACT.Square, [b_y], [b_g])
                        TS(P, "dve", gl1[:], gl1[:], 0.044715, 1.0, ALU.mult, ALU.add, [b_g], [b_g])
                        TT(P, "dve", gl1[:], gl1[:], ysb[:], ALU.mult, [b_g, b_y], [b_g])
                        ACTF(P, gl2[:], gl1[:], ACT.Sigmoid, [b_g], [b_g], scale=1.5957691216057308)
                        TT(P, "dve", ygT[:, pi8, :], gl2[:], ysb[:], ALU.mult, [b_g, b_y], [b_yg])
                for mt in range(8):
                    so = mt % 2
                    for half in range(2):
                        hsl = slice(512 * half, 512 * half + 512)
                        fns = [lambda e, kt=kt, mt=mt, hsl=hsl: e.matmul(
                            psY[:, hsl], lhsT=wglu[:, kt, 128 * mt:128 * mt + 128], rhs=ygT[:, kt, hsl],
                            start=(kt == 0), stop=(kt == 7)) for kt in range(8)]
                        P.pe_group(fns, [b_wglu, b_yg], [b_psY])
                        ACTF(P, gate[:], psY[:, hsl], ACT.Sigmoid, [b_psY, b_bglu], [b_gate], bias=bglu[:, mt:mt + 1])
                        TT(P, "dve", sso[so][:, hsl], gate[:], ygT[:, mt, hsl], ALU.mult, [b_gate, b_yg], [b_sso[so]])
                    DMA(P, "act", ssm_d[:, mt, ou * UNIT:(ou + 1) * UNIT], sso[so][:], b_ssmd, b_sso[so])
            P.flush(scope="s5own")


def cast_weights(nc, P, pairs):
    for dst, db, src, sbuf_ in pairs:
        rows, cols = dst.shape
        step = max(1, (1 << 20) // cols)
        for r in range(0, rows, step):
            r1 = min(rows, r + step)
            DMA(P, "pool", dst[r:r1, :], src[r:r1, :], db, sbuf_)


NCT = 30
CT_Q, CT_QSW, CT_K, CT_KSW, CT_U = 0, 8, 11, 19, 22


def norm_transpose(nc, P, T, x_rows_ap, b_src, hn_out3, b_hn, gain, b_gain, s):
    xt, junk, ss, rstd, xn, pT, idt, eps_t = T["xt"], T["junk"], T["ss"], T["rstd"], T["xn"], T["pT"], T["idt"], T["eps"]
    s2 = s % 2
    bx, bj, bs, br, bxn, bpT = T["b_xt"][s], T["b_junk"], T["b_ss"][s], T["b_rstd"][s], T["b_xn"][s2], T["b_pT"][s2]
    if x_rows_ap is not None:
        DMA(P, "sync", xt[s][:], x_rows_ap, bx, b_src)
    P.op("act", lambda e: e.activation(out=junk[:], in_=xt[s][:], func=ACT.Square, accum_out=ss[:, s:s + 1]),
         [bx], [bj, bs])
    P.op("act", lambda e: e.activation(out=rstd[:, s:s + 1], in_=ss[:, s:s + 1], func=ACT.Sqrt, scale=1.0 / D,
                                       bias=eps_t[:]), [bs, T["b_eps"]], [br])
    P.op("dve", lambda e: e.reciprocal(out=rstd[:, s:s + 1], in_=rstd[:, s:s + 1]), [br], [br])
    P.op("act", lambda e: e.activation(out=xn[s2][:], in_=xt[s][:], func=ACT.Copy, scale=rstd[:, s:s + 1]),
         [bx, br], [bxn])
    P.pe_group([(lambda e, k=k: e.transpose(out=pT[s2][:, k * 128:(k + 1) * 128], in_=xn[s2][:, k * 128:(k + 1) * 128],
                                            identity=idt[:])) for k in range(KT)], [bxn, T["b_idt"]], [bpT])
    TT(P, "dve", hn_out3, pT[s2][:].rearrange("p (k c) -> p k c", k=KT),
       gain[:].unsqueeze(2).to_broadcast([128, KT, 128]), ALU.mult, [bpT, b_gain], [b_hn])


def norm_tiles(nc, sb, ps, B, nx=2):
    T = {}
    T["xt"] = [sb("xt%d" % i, [128, D], F32) for i in range(nx)]
    T["junk"] = sb("junk", [128, D], BF16)
    T["ss"] = sb("ss", [128, nx], F32); T["rstd"] = sb("rstd", [128, nx], F32)
    T["xn"] = [sb("xn%d" % i, [128, D], BF16) for i in range(2)]
    T["pT"] = [ps("pT%d" % i, [128, D], BF16) for i in range(2)]
    T["idt"] = sb("idt", [128, 128], BF16); T["eps"] = sb("eps_t", [128, 1], F32)
    T["b_xt"] = [B("xt%d" % i) for i in range(nx)]; T["b_junk"] = B("junk"); T["b_ss"] = [B("ss%d" % i) for i in range(nx)]
    T["b_rstd"] = [B("r%d" % i) for i in range(nx)]; T["b_xn"] = [B("xn0"), B("xn1")]; T["b_pT"] = [B("pT0"), B("pT1")]
    T["b_idt"] = B("idt"); T["b_eps"] = B("eps")
    return T


def front_stage(nc, P, dr, n_blocks, first_kv, first_q):
    from contextlib import ExitStack
    B = Buf
    ext = dr["ext"]
    with ExitStack() as st:
        sb = lambda n, s, d: st.enter_context(nc.sbuf_tensor(_uid(n), s, d))
        ps = lambda n, s, d: st.enter_context(nc.psum_tensor(_uid(n), s, d))
        T = norm_tiles(nc, sb, ps, B, nx=4)
        DMA(P, "sync", T["idt"][:], dr["ident_bf"][0], T["b_idt"], ext)
        P.op("dve", lambda e: e.memset(T["eps"][:], 1e-6), [], [T["b_eps"]])
        g1t = sb("g1t", [128, KT], F32); b_g1 = B("g1t")
        DMA(P, "sync", g1t[:], dr["g1"][0], b_g1, ext)
        hnT = [sb("hnT%d" % i, [128, KT, BLK], BF16) for i in range(2)]; b_hn = [B("hn0"), B("hn1")]
        wu = sb("wu", [128, 8, KT, 128], BF16); b_wu = B("wu")
        wfm_bf, b_wfm = dr["wfm_bf"]
        DMA(P, "sync", wu[:], wfm_bf[CT_U:CT_U + 8].rearrange("c p k m -> p c k m"), b_wu, b_wfm)
        ublk = [sb("ublk%d" % i, [128, 8, BLK], BF16) for i in range(2)]; b_ub = [B("ub0"), B("ub1")]
        NWS = 2
        wst = [sb("wst%d" % i, [128, 4 * KT * 128], BF16) for i in range(NWS)]; b_wst = [B("wst%d" % i) for i in range(NWS)]
        pP = [ps("pP%d" % i, [128, BLK], F32) for i in range(3)]; b_pP = [B("pP%d" % i) for i in range(3)]
        cosb = sb("cosb", [128, 3, BLK], F32); sinb = sb("sinb", [128, 3, BLK], F32); b_cs = B("cossin")
        swsin = sb("swsin", [128, 3, BLK], F32); b_sw = B("swsin")
        rtmp = sb("rtmp", [128, BLK], F32); b_rt = B("rtmp")
        rtmp2 = sb("rtmp2", [128, BLK], F32); b_rt2 = B("rtmp2")
        qkblk = [sb("qkblk%d" % i, [128, 8, BLK], BF16) for i in range(2)]; b_qk = [B("qk0"), B("qk1")]
        vrow = sb("vrow", [128, 4, 1024], BF16); b_vr = B("vrow")
        xpad, b_x = dr["xpad"]
        uT_d, b_uTd = dr["uT_d"]
        wv_bf, b_wv = dr["wv_bf"]
        st_ = {"np": 0, "nw": 0, "nqk": 0}

        def proj_tile(lhs_w, hn, b_w, b_h):
            s = st_["np"] % 3; st_["np"] += 1
            P.pe_group([(lambda e, k=k, s=s: e.matmul(pP[s][:], lhsT=lhs_w[:, k, :], rhs=hn[:, k, :],
                                                      start=(k == 0), stop=(k == KT - 1))) for k in range(KT)],
                       [b_w, b_h], [b_pP[s]])
            return s

        def load_chunk(ct0, n):
            s = st_["nw"] % NWS; st_["nw"] += 1
            v4 = wst[s][:, 0:n * KT * 128].rearrange("p (c k m) -> p c k m", c=n, k=KT)
            DMA(P, "sync", v4, wfm_bf[ct0:ct0 + n].rearrange("c p k m -> p c k m"), b_wst[s], b_wfm)
            return s, v4

        def qk_proj(hs, ct_main, ct_sw, out_d, b_outd, col0):
            hn = hnT[hs]
            s, v4 = load_chunk(ct_sw, 3)
            for j in range(3):
                sp = proj_tile(v4[:, j], hn[:], b_wst[s], b_hn[hs])
                CP(P, "act", swsin[:, j, :], pP[sp][:], [b_pP[sp]], [b_sw])
            qs = st_["nqk"] % 2; st_["nqk"] += 1
            for c4 in range(2):
                s, v4 = load_chunk(ct_main + 4 * c4, 4)
                for j in range(4):
                    h = 4 * c4 + j
                    jb = h % 3
                    sp = proj_tile(v4[:, j], hn[:], b_wst[s], b_hn[hs])
                    TT(P, "dve", rtmp[:], pP[sp][:], cosb[:, jb, :], ALU.mult, [b_pP[sp], b_cs], [b_rt])
                    TT(P, "pool", rtmp2[:], swsin[:, h // 3, :], sinb[:, jb, :], ALU.mult, [b_sw, b_cs], [b_rt2])
                    TT(P, "dve", qkblk[qs][:, h, :], rtmp[:], rtmp2[:], ALU.add, [b_rt, b_rt2], [b_qk[qs]])
            DMA(P, "pool", out_d[:, :, col0:col0 + BLK], qkblk[qs][:], b_outd, b_qk[qs])

        nt = 0
        for b in range(n_blocks):
            hs = b % 2
            for t in range(4):
                s = nt % 4; nt += 1
                row = b * BLK + t * 128
                norm_transpose(nc, P, T, xpad[row:row + 128, :], b_x, hnT[hs][:, :, t * 128:(t + 1) * 128], b_hn[hs],
                               g1t, b_g1, s)
            us = b % 2
            for ct in range(8):
                sp = proj_tile(wu[:, ct], hnT[hs][:], b_wu, b_hn[hs])
                CP(P, "act", ublk[us][:, ct, :], pP[sp][:], [b_pP[sp]], [b_ub[us]])
            DMA(P, "pool", uT_d[:, :, b * BLK:(b + 1) * BLK], ublk[us][:], b_uTd, b_ub[us])
            if b < first_kv:
                continue
            wb = b - first_kv
            DMA(P, "sync", cosb[:], dr["cos_d"][0][:, :, wb * BLK:(wb + 1) * BLK], b_cs, ext)
            DMA(P, "sync", sinb[:], dr["sin_d"][0][:, :, wb * BLK:(wb + 1) * BLK], b_cs, ext)
            if b >= first_q:
                qk_proj(hs, CT_Q, CT_QSW, dr["qT_d"][0], dr["qT_d"][1], (b - first_q) * BLK)
            qk_proj(hs, CT_K, CT_KSW, dr["kT_d"][0], dr["kT_d"][1], wb * BLK)
            for half in range(2):
                if "v" in SKIP:
                    break
                s = st_["nw"] % NWS; st_["nw"] += 1
                vv = wst[s][:, 0:KT * 512].rearrange("p (k n) -> p k n", k=KT)
                DMA(P, "sync", vv, wv_bf[:, :, half * 512:(half + 1) * 512], b_wst[s], b_wv)
                for t in range(4):
                    sp = st_["np"] % 3; st_["np"] += 1
                    P.pe_group([(lambda e, k=k, sp=sp, t=t, vv=vv, hs=hs: e.matmul(
                        pP[sp][:], lhsT=hnT[hs][:, k, t * 128:(t + 1) * 128], rhs=vv[:, k, :],
                        start=(k == 0), stop=(k == KT - 1))) for k in range(KT)], [b_wst[s], b_hn[hs]], [b_pP[sp]])
                    CP(P, "act", vrow[:, t, half * 512:(half + 1) * 512], pP[sp][:], [b_pP[sp]], [b_vr])
            if "v" not in SKIP:
                DMA(P, "pool", dr["V_d"][0][wb * BLK:(wb + 1) * BLK, :].rearrange("(t p) c -> p t c", p=128), vrow[:],
                    dr["V_d"][1], b_vr)
        P.flush(scope="front")


def attn_stage(nc, P, dr):
    from contextlib import ExitStack
    B = Buf
    ext = dr["ext"]
    W = 2 * NTOK
    SC = 128 ** -0.5
    with ExitStack() as st:
        sb = lambda n, s, d: st.enter_context(nc.sbuf_tensor(_uid(n), s, d))
        ps = lambda n, s, d: st.enter_context(nc.psum_tensor(_uid(n), s, d))
        cs = sb("cs", [128, 4, 128], BF16); b_c = B("cs")
        DMA(P, "sync", cs[:], dr["aconsts"][0], b_c, ext)
        hm = sb("hm", [128, 1], F32); b_hm = B("hm")
        DMA(P, "sync", hm[:], dr["hmask"][0], b_hm, ext)
        qT = [sb("qTh%d" % i, [128, NTOK], BF16) for i in range(2)]; b_q = [B("q0"), B("q1")]
        kT = [sb("kTh%d" % i, [128, W], BF16) for i in range(2)]; b_k = [B("k0"), B("k1")]
        num = sb("num", [128, NTOK], F32); den = sb("den", [128, NTOK], F32); b_num, b_den = B("num"), B("den")
        outb = [sb("outb%d" % i, [128, NTOK], BF16) for i in range(2)]; b_ob = [B("ob0"), B("ob1")]
        NV = 6
        vt = [sb("vt%d" % i, [128, 128], BF16) for i in range(NV)]; b_vt = [B("vt%d" % i) for i in range(NV)]
        pt = [sb("pt%d" % i, [128, 128], BF16) for i in range(4)]; b_pt = [B("pt%d" % i) for i in range(4)]
        pS = [ps("pS%d" % i, [128, 128], F32) for i in range(4)]; b_pS = [B("pS%d" % i) for i in range(4)]
        pO = [ps("pO%d" % i, [128, 128], F32) for i in range(2)]; b_pO = [B("pO0"), B("pO1")]
        pL = [ps("pL%d" % i, [128, 128], F32) for i in range(2)]; b_pL = [B("pL0"), B("pL1")]
        qT_d, b_qd = dr["qT_d"]; kT_d, b_kd = dr["kT_d"]; V_d, b_vd = dr["V_d"]; mix_d, b_mix = dr["mix_d"]
        iv = 0; ip = 0; blk = 0
        for h in range(8):
            hs = h % 2
            DMA(P, "sync", qT[hs][:], qT_d[:, h, :], b_q[hs], b_qd)
            DMA(P, "sync", kT[hs][:], kT_d[:, h, :], b_k[hs], b_kd)
            for pi_, dil in enumerate((1, 4, 16)):
                nb_all = W // dil // 128
                nb0 = nb_all // 2
                for n in range(nb0, nb_all):
                    for r in range(dil):
                        def wpos(nn):
                            s0 = nn * 128 * dil + r
                            return slice(s0, s0 + 127 * dil + 1, dil)
                        pq_w = wpos(n)
                        pq = slice(pq_w.start - NTOK, pq_w.stop - NTOK, dil)
                        so = blk % 2; blk += 1
                        slots = []
                        for (kn, mi) in ((n, 1), (n - 1, 2)):
                            pk = wpos(kn)
                            halo = kn < nb0
                            sv = iv % NV; iv += 1
                            sp = ip % 4; ip += 1
                            slots.append((sv, sp))
                            DMA(P, "sync", vt[sv][:], V_d[pk, 128 * h:128 * h + 128], b_vt[sv], b_vd)
                            P.pe_group([lambda e, sp=sp, pk=pk, pq=pq, hs=hs: e.matmul(
                                            pS[sp][:], lhsT=kT[hs][:, pk], rhs=qT[hs][:, pq], start=True, stop=False),
                                        lambda e, sp=sp, mi=mi: e.matmul(
                                            pS[sp][:], lhsT=cs[:, 0, :], rhs=cs[:, mi, :], start=False, stop=True)],
                                       [b_k[hs], b_q[hs], b_c], [b_pS[sp]])
                            if halo:
                                ACTF(P, pt[sp][:], pS[sp][:], ACT.Exp, [b_pS[sp], b_hm], [b_pt[sp]], scale=SC, bias=hm[:])
                            else:
                                ACTF(P, pt[sp][:], pS[sp][:], ACT.Exp, [b_pS[sp]], [b_pt[sp]], scale=SC)
                        P.pe_group([lambda e, sv=sv, sp=sp, i=i, so=so: e.matmul(
                            pO[so][:], lhsT=vt[sv][:], rhs=pt[sp][:], start=(i == 0), stop=(i == 1))
                            for i, (sv, sp) in enumerate(slots)],
                            [b_vt[sv] for sv, _ in slots] + [b_pt[sp] for _, sp in slots], [b_pO[so]])
                        P.pe_group([lambda e, sp=sp, i=i, so=so: e.matmul(
                            pL[so][:], lhsT=cs[:, 3, :], rhs=pt[sp][:], start=(i == 0), stop=(i == 1))
                            for i, (_, sp) in enumerate(slots)], [b_c] + [b_pt[sp] for _, sp in slots], [b_pL[so]])
                        if pi_ == 0:
                            CP(P, "dve", num[:, pq], pO[so][:], [b_pO[so]], [b_num])
                            CP(P, "act", den[:, pq], pL[so][:], [b_pL[so]], [b_den])
                        else:
                            TT(P, "dve", num[:, pq], pO[so][:], num[:, pq], ALU.add, [b_pO[so], b_num], [b_num])
                            TT(P, "dve", den[:, pq], pL[so][:], den[:, pq], ALU.add, [b_pL[so], b_den], [b_den])
            P.op("dve", lambda e: e.reciprocal(out=den[:], in_=den[:]), [b_den], [b_den])
            TT(P, "dve", outb[hs][:], num[:], den[:], ALU.mult, [b_num, b_den], [b_ob[hs]])
            DMA(P, "pool", mix_d[:, h, :], outb[hs][:], b_mix, b_ob[hs])
        P.flush(scope="attn")


def tail1_stage(nc, P, dr, x_row0):
    from contextlib import ExitStack
    B = Buf
    ext = dr["ext"]
    with ExitStack() as st:
        sb = lambda n, s, d: st.enter_context(nc.sbuf_tensor(_uid(n), s, d))
        ps = lambda n, s, d: st.enter_context(nc.psum_tensor(_uid(n), s, d))
        T = norm_tiles(nc, sb, ps, B)
        DMA(P, "sync", T["idt"][:], dr["ident_bf"][0], T["b_idt"], ext)
        P.op("dve", lambda e: e.memset(T["eps"][:], 1e-6), [], [T["b_eps"]])
        g2t = sb("g2t_", [128, KT], F32); b_g2 = B("g2t")
        DMA(P, "sync", g2t[:], dr["g2"][0], b_g2, ext)
        wout = sb("wout", [128, KT, D], BF16); b_wo = B("wout")
        DMA(P, "sync", wout[:], dr["wout_bf"][0], b_wo, dr["wout_bf"][1])
        mixt = [sb("mixt%d" % i, [128, KT, 128], BF16) for i in range(2)]; b_mt = [B("mt0"), B("mt1")]
        xin = [sb("xin%d" % i, [128, D], F32) for i in range(2)]; b_xin = [B("xin0"), B("xin1")]
        hn2b = [sb("hn2b%d" % i, [128, KT, 128], BF16) for i in range(2)]; b_h2 = [B("h2b0"), B("h2b1")]
        pH = ps("pH", [128, D], F32); b_pH = B("pH")
        xpad, b_x = dr["xpad"]; mix_d, b_mix = dr["mix_d"]; h_d, b_hd = dr["h_d"]; hn2_d, b_hn2d = dr["hn2T_d"]
        for i in range(NTOK // 128):
            s = i % 2
            DMA(P, "sync", mixt[s][:], mix_d[:, :, i * 128:(i + 1) * 128], b_mt[s], b_mix)
            DMA(P, "sync", xin[s][:], xpad[x_row0 + i * 128:x_row0 + (i + 1) * 128, :], b_xin[s], b_x)
            fns = []
            for fb in range(4):
                for k in range(KT):
                    fns.append(lambda e, s=s, fb=fb, k=k: e.matmul(
                        pH[:, fb * 512:(fb + 1) * 512], lhsT=mixt[s][:, k, :], rhs=wout[:, k, fb * 512:(fb + 1) * 512],
                        start=(k == 0), stop=(k == KT - 1)))
            P.pe_group(fns, [b_mt[s], b_wo], [b_pH])
            TT(P, "dve", T["xt"][s][:], pH[:], xin[s][:], ALU.add, [b_pH, b_xin[s]], [T["b_xt"][s]])
            DMA(P, "pool", h_d[i * 128:(i + 1) * 128, :], T["xt"][s][:], b_hd, T["b_xt"][s])
            norm_transpose(nc, P, T, None, None, hn2b[s][:], b_h2[s], g2t, b_g2, s)
            DMA(P, "pool", hn2_d[:, :, i * 128:(i + 1) * 128], hn2b[s][:], b_hn2d, b_h2[s])
        P.flush(scope="tail1")


DFF = 5632
NF = DFF // 128


def tail2_stage(nc, P, dr):
    from contextlib import ExitStack
    B = Buf
    ext = dr["ext"]
    with ExitStack() as st:
        sb = lambda n, s, d: st.enter_context(nc.sbuf_tensor(_uid(n), s, d))
        ps = lambda n, s, d: st.enter_context(nc.psum_tensor(_uid(n), s, d))
        hn2 = [sb("hn2s%d" % i, [128, KT, BLK], BF16) for i in range(2)]; b_hn2 = [B("hn2s0"), B("hn2s1")]
        HT = sb("HT", [128, NF, BLK], BF16); b_HT = B("HT")
        wst = [sb("wst2_%d" % i, [128, KT * 512], BF16) for i in range(3)]; b_wst = [B("w2_%d" % i) for i in range(3)]
        gsig = sb("gsig", [128, BLK], F32); b_gs = B("gsig")
        hin = [sb("hin%d" % i, [128, D], F32) for i in range(4)]; b_hin = [B("hin%d" % i) for i in range(4)]
        junk = sb("junk2", [128, D], F32); b_junk = B("junk2")
        gF = sb("gF", [128, D], F32); b_gF = B("gF")
        DMA(P, "sync", gF[:], dr["final_g"][0].partition_broadcast(128), b_gF, ext)
        ss = sb("ss2", [128, 4], F32); rstd = sb("rstd2", [128, 4], F32); b_ss = [B("ss2_%d" % i) for i in range(4)]
        eps_t = sb("eps2", [128, 1], F32); b_eps = B("eps2")
        P.op("dve", lambda e: e.memset(eps_t[:], 1e-6), [], [b_eps])
        pG = [ps("pG%d" % i, [128, BLK], F32) for i in range(2)]; b_pG = [B("pG0"), B("pG1")]
        pU = [ps("pU%d" % i, [128, BLK], F32) for i in range(2)]; b_pU = [B("pU0"), B("pU1")]
        pD = [ps("pD%d" % i, [128, 1024], F32) for i in range(2)]; b_pD = [B("pD0"), B("pD1")]
        hn2_d, b_hn2d = dr["hn2T_d"]; h_d, b_hd = dr["h_d"]; y_d, b_yd = dr["y"]
        wg_bf, b_wg = dr["wgate_bf"]; wu_bf, b_wub = dr["wup_bf"]; wd_bf, b_wd = dr["wdown_bf"]
        nw = 0
        for sbk in range(NTOK // BLK):
            hs = sbk % 2
            DMA(P, "sync", hn2[hs][:], hn2_d[:, :, sbk * BLK:(sbk + 1) * BLK], b_hn2[hs], b_hn2d)
            for fc in range(NF // 4):
                sg = nw % 3; nw += 1
                vg = wst[sg][:].rearrange("p (k n) -> p k n", k=KT)
                DMA(P, "sync", vg, wg_bf[:, :, fc * 512:(fc + 1) * 512], b_wst[sg], b_wg)
                su = nw % 3; nw += 1
                vu = wst[su][:].rearrange("p (k n) -> p k n", k=KT)
                DMA(P, "sync", vu, wu_bf[:, :, fc * 512:(fc + 1) * 512], b_wst[su], b_wub)
                for j in range(4):
                    f = 4 * fc + j
                    s = f % 2
                    P.pe_group([(lambda e, k=k, s=s, j=j, vg=vg, hs=hs: e.matmul(
                        pG[s][:], lhsT=vg[:, k, j * 128:(j + 1) * 128], rhs=hn2[hs][:, k, :],
                        start=(k == 0), stop=(k == KT - 1))) for k in range(KT)], [b_wst[sg], b_hn2[hs]], [b_pG[s]])
                    P.pe_group([(lambda e, k=k, s=s, j=j, vu=vu, hs=hs: e.matmul(
                        pU[s][:], lhsT=vu[:, k, j * 128:(j + 1) * 128], rhs=hn2[hs][:, k, :],
                        start=(k == 0), stop=(k == KT - 1))) for k in range(KT)], [b_wst[su], b_hn2[hs]], [b_pU[s]])
                    ACTF(P, gsig[:], pG[s][:], ACT.Silu, [b_pG[s]], [b_gs])
                    TT(P, "dve", HT[:, f, :], pU[s][:], gsig[:], ALU.mult, [b_pU[s], b_gs], [b_HT])
            for t in range(4):
                i = sbk * 4 + t
                DMA(P, "sync", hin[t][:], h_d[i * 128:(i + 1) * 128, :], b_hin[t], b_hd)
            for fc in range(NF // 4):
                sd = nw % 3; nw += 1
                vd = wst[sd][:].rearrange("p (f n) -> p f n", f=4)
                DMA(P, "sync", vd, wd_bf[:, fc * 4:(fc + 1) * 4, :], b_wst[sd], b_wd)
                for t in range(4):
                    for hf in range(2):
                        fns = []
                        for j in range(4):
                            f = 4 * fc + j
                            for fb2 in range(2):
                                fb = 2 * hf + fb2
                                fns.append(lambda e, j=j, f=f, fb=fb, fb2=fb2, hf=hf, vd=vd, t=t: e.matmul(
                                    pD[hf][:, fb2 * 512:(fb2 + 1) * 512], lhsT=HT[:, f, t * 128:(t + 1) * 128],
                                    rhs=vd[:, j, fb * 512:(fb + 1) * 512], start=(j == 0), stop=(j == 3)))
                        P.pe_group(fns, [b_HT, b_wst[sd]], [b_pD[hf]])
                        hsl = slice(1024 * hf, 1024 * hf + 1024)
                        TT(P, "dve", hin[t][:, hsl], pD[hf][:], hin[t][:, hsl], ALU.add, [b_pD[hf], b_hin[t]], [b_hin[t]])
            for t in range(4):
                i = sbk * 4 + t
                s2 = t
                P.op("act", lambda e, s2=s2: e.activation(out=junk[:], in_=hin[s2][:], func=ACT.Square,
                                                          accum_out=ss[:, s2:s2 + 1]), [b_hin[s2]], [b_junk, b_ss[s2]])
                P.op("act", lambda e, s2=s2: e.activation(out=rstd[:, s2:s2 + 1], in_=ss[:, s2:s2 + 1], func=ACT.Sqrt,
                                                          scale=1.0 / D, bias=eps_t[:]), [b_ss[s2], b_eps], [b_ss[s2]])
                P.op("dve", lambda e, s2=s2: e.reciprocal(out=rstd[:, s2:s2 + 1], in_=rstd[:, s2:s2 + 1]),
                     [b_ss[s2]], [b_ss[s2]])
                STT(P, "dve", hin[s2][:], hin[s2][:], rstd[:, s2:s2 + 1], gF[:], ALU.mult, ALU.mult,
                    [b_hin[s2], b_ss[s2], b_gF], [b_hin[s2]])
                DMA(P, "pool", y_d[i * 128:(i + 1) * 128, :], hin[s2][:], b_yd, b_hin[s2])
        P.flush(final_bufs=[b_yd], scope="tail2")


N_PRE_UNITS = (SEQ - NTOK) // UNIT
N_OWN_UNITS = NTOK // UNIT
FIRST_KV_BLK = (SEQ - 2 * NTOK) // BLK
FIRST_Q_BLK = (SEQ - NTOK) // BLK

_IN_SPECS = [
    ("xpad", [SEQ, D], F32), ("wfm32", [NCT * 128, KT * 128], F32), ("wv32", [128, KT * 1024], F32),
    ("wglu32", [128, 8 * 1024], F32), ("wout32", [128, KT * D], F32), ("wgate32", [128, KT * DFF], F32),
    ("wup32", [128, KT * DFF], F32), ("wdown32", [128, NF * D], F32), ("g1", [128, KT], F32), ("g2", [128, KT], F32),
    ("final_g", [D], F32), ("a_re", [64, 64], F32), ("a_im", [64, 64], F32), ("log_dt", [64], F32),
    ("b_re", [64, 64, 16], F32), ("b_im", [64, 64, 16], F32), ("c_re", [64, 16, 64], F32), ("c_im", [64, 16, 64], F32),
    ("d_skip", [64, 16], F32), ("b_glu", [1024], F32), ("ident32", [128, 128], F32), ("ident_bf", [128, 128], BF16),
    ("aconsts", [128, 4, 128], BF16), ("hmask", [128, 1], F32), ("cos_d", [128, 3, 2 * NTOK], F32),
    ("sin_d", [128, 3, 2 * NTOK], F32),
]


def build_program():
    from contextlib import ExitStack
    nc = bass.Bass("TRN2", target_bir_lowering=False)
    ext = Buf("ext", False)
    dr = {"ext": ext}
    for name, shape, dt in _IN_SPECS:
        dr[name] = (nc.dram_tensor(name, shape, dt, kind="ExternalInput").ap(), ext)
    dr["y"] = (nc.dram_tensor("y", [NTOK, D], F32, kind="ExternalOutput").ap(), Buf("y", False))

    def scratch(name, shape, dt, keep=False):
        dr[name] = (nc.dram_tensor(name, shape, dt).ap(), Buf(name, False, keep))
    scratch("wfm_bf2", [NCT * 128, KT * 128], BF16); scratch("wv_bf2", [128, KT * 1024], BF16)
    scratch("wglu_bf2", [128, 8 * 1024], BF16); scratch("wout_bf2", [128, KT * D], BF16)
    scratch("wgate_bf2", [128, KT * DFF], BF16); scratch("wup_bf2", [128, KT * DFF], BF16)
    scratch("wdown_bf2", [128, NF * D], BF16)
    scratch("uT_d", [128, 8, SEQ], BF16); scratch("qT_d", [128, 8, NTOK], BF16); scratch("kT_d", [128, 8, 2 * NTOK], BF16)
    scratch("V_d", [2 * NTOK, 1024], BF16); scratch("mix_d", [128, 16, NTOK], BF16)
    scratch("h_d", [NTOK, D], F32); scratch("hn2T_d", [128, KT, NTOK], BF16)
    dr["wfm_bf"] = (dr["wfm_bf2"][0].rearrange("(c p) (k m) -> c p k m", p=128, k=KT), dr["wfm_bf2"][1])
    dr["wv_bf"] = (dr["wv_bf2"][0].rearrange("p (k n) -> p k n", k=KT), dr["wv_bf2"][1])
    dr["wglu_bf"] = (dr["wglu_bf2"][0].rearrange("p (k n) -> p k n", k=8), dr["wglu_bf2"][1])
    dr["wout_bf"] = (dr["wout_bf2"][0].rearrange("p (k n) -> p k n", k=KT), dr["wout_bf2"][1])
    dr["wgate_bf"] = (dr["wgate_bf2"][0].rearrange("p (k n) -> p k n", k=KT), dr["wgate_bf2"][1])
    dr["wup_bf"] = (dr["wup_bf2"][0].rearrange("p (k n) -> p k n", k=KT), dr["wup_bf2"][1])
    dr["wdown_bf"] = (dr["wdown_bf2"][0].rearrange("p (f n) -> p f n", f=NF), dr["wdown_bf2"][1])
    dr["ssm_d"] = (dr["mix_d"][0][:, 8:16, :], dr["mix_d"][1])
    with ExitStack() as st:
        P = Prog(nc, st)
        pairs = []
        for nm in ("wfm", "wv", "wglu", "wout", "wgate", "wup", "wdown"):
            pairs.append((dr[nm + "_bf2"][0], dr[nm + "_bf2"][1], dr[nm + "32"][0], Buf(nm + "32", False, keep=True)))
        cast_weights(nc, P, pairs)
        front_stage(nc, P, dr, NBLK_ALL, FIRST_KV_BLK, FIRST_Q_BLK)
        s5_stage(nc, P, dr, N_PRE_UNITS, N_OWN_UNITS)
        attn_stage(nc, P, dr)
        tail1_stage(nc, P, dr, SEQ - NTOK)
        tail2_stage(nc, P, dr)
    return nc


def _tile_rows(w, kt):
    n = w.shape[1]
    return np.ascontiguousarray(w.reshape(kt, 128, n).transpose(1, 0, 2)).reshape(128, kt * n)


def _head_perm(h):
    j = h % 3
    perm = np.zeros(128, np.int64)
    for m in range(128):
        if 32 * j <= m < 32 * j + 32:
            perm[m] = m - 32 * j
        elif m < 32 * j:
            perm[m] = 32 + m
        else:
            perm[m] = m
    return perm


def _prep_shared(inp):
    f32 = np.float32
    w_in = np.asarray(inp["w_in"], f32)[0]
    cols = np.full((NCT, 128), -1, np.int64)
    for base, ct0, ctsw in ((0, CT_Q, CT_QSW), (1024, CT_K, CT_KSW)):
        for h in range(8):
            cols[ct0 + h] = base + h * 128 + _head_perm(h)
            tt, j = h // 3, h % 3
            for i in range(32):
                cols[ctsw + tt, 32 * j + i] = base + h * 128 + (i + 16) % 32
    for k in range(8):
        cols[CT_U + k] = 3072 + k * 128 + np.arange(128)
    flat = cols.reshape(-1)
    wsel = np.where(flat[None, :] >= 0, w_in[:, np.maximum(flat, 0)], 0.0).astype(f32)
    wfm = wsel.reshape(KT, 128, NCT, 128).transpose(2, 1, 0, 3)
    sh = {}
    sh["wfm32"] = np.ascontiguousarray(wfm).reshape(NCT * 128, KT * 128)
    sh["wv32"] = _tile_rows(np.ascontiguousarray(w_in[:, 2048:3072]), KT)
    sh["wglu32"] = _tile_rows(np.asarray(inp["w_glu"], f32)[0], 8)
    sh["wout32"] = _tile_rows(np.asarray(inp["w_out"], f32)[0], KT)
    sh["wgate32"] = _tile_rows(np.asarray(inp["w_gate"], f32)[0], KT)
    sh["wup32"] = _tile_rows(np.asarray(inp["w_up"], f32)[0], KT)
    sh["wdown32"] = _tile_rows(np.asarray(inp["w_down"], f32)[0], NF)
    sh["g1"] = np.ascontiguousarray(np.asarray(inp["norm1_g"], f32)[0].reshape(KT, 128).T)
    sh["g2"] = np.ascontiguousarray(np.asarray(inp["norm2_g"], f32)[0].reshape(KT, 128).T)
    sh["final_g"] = np.ascontiguousarray(np.asarray(inp["final_g"], f32))
    for nm in ("a_re", "a_im", "log_dt", "b_re", "b_im", "c_re", "c_im", "d_skip", "b_glu"):
        sh[nm] = np.ascontiguousarray(np.asarray(inp[nm], f32)[0])
    sh["ident32"] = np.eye(128, dtype=f32)
    sh["ident_bf"] = np.eye(128, dtype=f32).astype(ml_dtypes.bfloat16)
    kk = np.arange(128)[:, None]; qq = np.arange(128)[None, :]
    mcur = np.where(kk <= qq, 0.0, -30000.0); mprev = np.where(kk >= qq, 0.0, -30000.0)
    sh["aconsts"] = np.ascontiguousarray(
        np.stack([np.eye(128), mcur, mprev, np.ones((128, 128))], 1).astype(f32).astype(ml_dtypes.bfloat16))
    return sh


def _rope_tables(t0):
    f32 = np.float32
    pos = (np.arange(2 * NTOK) + (t0 - NTOK)).astype(f32)
    pos = np.maximum(pos, f32(0))
    inv_freq = (f32(500000.0) ** (-(np.arange(0, 32, 2).astype(f32)) / f32(32))).astype(f32)
    i = np.arange(32)
    ang = (pos[None, :] * inv_freq[i % 16][:, None]).astype(f32)
    c32 = np.cos(ang).astype(f32)
    s32 = np.sin(ang).astype(f32) * np.where(i < 16, -1.0, 1.0).astype(f32)[:, None]
    cosT = np.ones((128, 3, 2 * NTOK), f32)
    sinT = np.zeros((128, 3, 2 * NTOK), f32)
    for j in range(3):
        cosT[32 * j:32 * j + 32, j, :] = c32
        sinT[32 * j:32 * j + 32, j, :] = s32
    return cosT, sinT


def kernel(**inputs):
    x = np.asarray(inputs["x"], np.float32)[0]
    sh = _prep_shared(inputs)
    nc = build_program()
    in_maps = []
    for c in range(NCORES):
        t0 = c * NTOK
        xpad = np.zeros((SEQ, D), np.float32)
        n_real = t0 + NTOK
        xpad[SEQ - n_real:] = x[:n_real]
        cosT, sinT = _rope_tables(t0)
        m = dict(sh)
        m["xpad"] = xpad
        m["cos_d"] = cosT
        m["sin_d"] = sinT
        m["hmask"] = np.full((128, 1), -30000.0 if c == 0 else 0.0, np.float32)
        in_maps.append(m)
    res = run_bass_kernel_spmd(nc, in_maps, core_ids=list(range(NCORES)))
    y = np.concatenate([np.asarray(res.results[c]["y"], np.float32) for c in range(NCORES)], axis=0)
    return y.reshape(1, SEQ, D)
```

```python
import math
import numpy as np
import ml_dtypes
import concourse.bass as bass
import concourse.mybir as mybir
from concourse.bass_utils import run_bass_kernel_spmd

F32 = mybir.dt.float32
BF16 = mybir.dt.bfloat16
ALU = mybir.AluOpType
ACT = mybir.ActivationFunctionType
AX = mybir.AxisListType

NCORES = 8
D = 2048
SEQ = 16384
NTOK = SEQ // NCORES
BLK = 512
NBLK_ALL = SEQ // BLK
KT = D // 128

ENGS = ("sync", "act", "pool", "dve", "pe")


TWO_PI = 2.0 * math.pi
_UID = [0]
SKIP = set()
PROFILE = False


def _uid(n):
    _UID[0] += 1
    return "%s_%d" % (n, _UID[0])


class Buf:
    __slots__ = ("name", "w", "r", "dsem", "sb", "keep")

    def __init__(self, name, sb=True, keep=False):
        self.name = name
        self.w = None
        self.r = {}
        self.dsem = None
        self.sb = sb
        self.keep = keep


class Prog:
    def __init__(self, nc, stack, ndsem=80):
        self.nc = nc
        self.csem = {k: stack.enter_context(nc.semaphore("s_" + k)) for k in ("c_act", "c_pool", "c_dve", "c_pe")}
        self.dsem = [stack.enter_context(nc.semaphore("sd%d" % i)) for i in range(ndsem)]
        self.dval = [0] * ndsem
        self.dfree = list(range(ndsem))
        self.dkeep = set()
        self.ops = {e: [] for e in ENGS}
        self.cnt = {e: 0 for e in ENGS}
        self.known = {e: {} for e in ENGS}
        self.bufs = []
        self.nblocks = 0

    def _reg(self, b):
        if b not in self.bufs:
            self.bufs.append(b)

    def _deps(self, eng, reads, writes, skip_same_pe=False):
        need = {}

        def add(tok):
            if tok is None:
                return
            k, v = tok
            if skip_same_pe and k == "c_pe":
                return
            if need.get(k, 0) < v:
                need[k] = v
        for b in reads:
            add(b.w)
        for b in writes:
            add(b.w)
            for k, v in b.r.items():
                add((k, v))
        waits = []
        kn = self.known[eng]
        for k, v in need.items():
            if kn.get(k, 0) < v:
                kn[k] = v
                waits.append((k, v))
        return waits

    def _mark(self, tok, reads, writes):
        for b in reads:
            b.r[tok[0]] = tok[1]
            self._reg(b)
        for b in writes:
            b.w = tok
            b.r = {}
            self._reg(b)

    def op(self, eng, fn, reads=(), writes=()):
        waits = self._deps(eng, reads, writes)
        self.cnt[eng] += 1
        tok = ("c_" + eng, self.cnt[eng])
        self.ops[eng].append((waits, fn, (tok[0], 1)))
        self._mark(tok, reads, writes)

    def pe_group(self, fns, reads=(), writes=()):
        waits = self._deps("pe", reads, writes, skip_same_pe=True)
        self.cnt["pe"] += 1
        tok = ("c_pe", self.cnt["pe"])
        n = len(fns)
        for i, fn in enumerate(fns):
            self.ops["pe"].append((waits if i == 0 else [], fn, (tok[0], 1) if i == n - 1 else None))
        self._mark(tok, reads, writes)

    def dma(self, q, fn, dst, src):
        waits = self._deps(q, [src], [dst])
        key = dst if dst.sb else src
        if key.dsem is None:
            key.dsem = self.dfree.pop(0)
            if key.keep:
                self.dkeep.add(key.dsem)
        i = key.dsem
        self.dval[i] += 16
        tok = (i, self.dval[i])
        self.ops[q].append((waits, fn, (i, 16)))
        src.r[tok[0]] = tok[1]
        dst.w = tok
        dst.r = {}
        self._reg(src)
        self._reg(dst)
        self._reg(key)

    def wait_all(self, eng, bufs):
        waits = self._deps(eng, bufs, [])
        self.ops[eng].append((waits, None, None))

    def _sem(self, k):
        return self.csem[k] if isinstance(k, str) else self.dsem[k]

    def flush(self, final_bufs=(), scope=None):
        if PROFILE and scope:
            with self.nc.named_scope(scope):
                return self._flush(final_bufs)
        return self._flush(final_bufs)

    def _flush(self, final_bufs=()):
        kn = self.known["sync"]
        waits = []
        for i, v in enumerate(self.dval):
            if i in self.dkeep or i in self.dfree:
                continue
            if kn.get(i, 0) < v:
                kn[i] = v
                waits.append((i, v))
        for b in final_bufs:
            if b.w is not None and kn.get(b.w[0], 0) < b.w[1]:
                kn[b.w[0]] = b.w[1]
                waits.append(b.w)
        self.ops["sync"].append((waits, None, None))
        nc = self.nc
        self.nblocks += 1
        with nc.Block() as block:
            def run(name):
                def body(e):
                    for waits, fn, inc in self.ops[name]:
                        for k, v in waits:
                            e.wait_ge(self._sem(k), v)
                        if fn is not None:
                            ins = fn(e)
                            if inc is not None:
                                ins.then_inc(self._sem(inc[0]), inc[1])
                return body
            block.sync(run("sync"))
            block.scalar(run("act"))
            block.gpsimd(run("pool"))
            block.vector(run("dve"))
            block.tensor(run("pe"))
        self.ops = {e: [] for e in ENGS}
        for e in ENGS:
            kn = self.known[e]
            for k in ("act", "pool", "dve", "pe"):
                kn["c_" + k] = self.cnt[k]
            for i, v in enumerate(self.dval):
                if i not in self.dkeep:
                    kn[i] = v
        for b in self.bufs:
            if b.w is not None and b.w[0] in self.dkeep:
                pass
            else:
                b.w = None
            b.r = {k: v for k, v in b.r.items() if k in self.dkeep}
            if b.dsem is not None and b.dsem not in self.dkeep:
                self.dfree.append(b.dsem)
                b.dsem = None
        self.bufs = [b for b in self.bufs if b.w is not None or b.r or b.dsem is not None]


def TT(P, eng, out, in0, in1, op, reads, writes):
    P.op(eng, lambda e: e.tensor_tensor(out=out, in0=in0, in1=in1, op=op), reads, writes)


def TS(P, eng, out, in0, s1, s2, op0, op1, reads, writes):
    if s2 is None:
        P.op(eng, lambda e: e.tensor_scalar(out=out, in0=in0, scalar1=s1, scalar2=None, op0=op0), reads, writes)
    else:
        P.op(eng, lambda e: e.tensor_scalar(out=out, in0=in0, scalar1=s1, scalar2=s2, op0=op0, op1=op1), reads, writes)


def STT(P, eng, out, in0, scalar, in1, op0, op1, reads, writes):
    P.op(eng, lambda e: e.scalar_tensor_tensor(out=out, in0=in0, scalar=scalar, in1=in1, op0=op0, op1=op1), reads, writes)


def ACTF(P, out, in_, func, reads, writes, scale=1.0, bias=None):
    if bias is None:
        P.op("act", lambda e: e.activation(out=out, in_=in_, func=func, scale=scale), reads, writes)
    else:
        P.op("act", lambda e: e.activation(out=out, in_=in_, func=func, scale=scale, bias=bias), reads, writes)


def CP(P, eng, out, in_, reads, writes):
    if eng == "act":
        P.op("act", lambda e: e.activation(out=out, in_=in_, func=ACT.Copy), reads, writes)
    else:
        P.op(eng, lambda e: e.tensor_copy(out=out, in_=in_), reads, writes)


def DMA(P, q, out, in_, dst, src, slow=False):
    if slow:
        P.dma(q, lambda e: e.dma_start(out=out, in_=in_, allow_slow_non_contiguous=True), dst, src)
    else:
        P.dma(q, lambda e: e.dma_start(out=out, in_=in_), dst, src)


def cmul(P, eng, o_re, o_im, a_re, a_im, b_re, b_im, t1, t2, reads, writes, tb):
    TT(P, eng, t1, a_re, b_re, ALU.mult, reads, [tb])
    TT(P, eng, t2, a_im, b_im, ALU.mult, reads, [tb])
    TT(P, eng, o_re, t1, t2, ALU.subtract, [tb], writes)
    TT(P, eng, t1, a_re, b_im, ALU.mult, reads, [tb])
    TT(P, eng, t2, a_im, b_re, ALU.mult, reads, [tb])
    TT(P, eng, o_im, t1, t2, ALU.add, [tb], writes)


T0 = 8
UNIT = 1024
NSC = UNIT // T0
I32 = mybir.dt.int32


def s5_stage(nc, P, dr, n_pre_units, n_own_units):
    from contextlib import ExitStack
    B = Buf
    n_units = n_pre_units + n_own_units
    ext = dr["a_re"][1]
    with ExitStack() as st0:
        sbp = lambda n, s, d: st0.enter_context(nc.sbuf_tensor(_uid(n), s, d))
        sc = sbp("sc", [128, 24, 32], F32); b_sc = B("sc")
        pw = sbp("pw", [128, 3, T0 + 1, 32], F32); b_pw = B("pw")
        WBT = sbp("WBT", [128, 8, T0, 2, 128], BF16); b_WBT = B("WBT")
        WCT = sbp("WCT", [128, 8, 2, 128], BF16); b_WCT = B("WCT")
        Rc = sbp("Rc", [128, 32, NSC], F32); Rs = sbp("Rs", [128, 32, NSC], F32); b_R = B("R")
        Dcol = sbp("Dcol", [128, 8], F32); b_D = B("Dcol")
        wglu = sbp("wglu", [128, 8, 1024], BF16); b_wglu = B("wglu")
        bglu = sbp("bglu", [128, 8], F32); b_bglu = B("bglu")
        cr = sbp("cr", [128, 32, 1], F32); ci = sbp("ci", [128, 32, 1], F32); b_c = B("carry")
        P.op("dve", lambda e: e.memset(cr[:], 0.0), [], [b_c])
        P.op("dve", lambda e: e.memset(ci[:], 0.0), [], [b_c])
        S = lambda j: sc[:, j, :]
        DT, MAG, PHI, SINP, COSP, ABR, ABI, CR, CI, T1, T2, T3, RHO, C8, S8, NUMR, NUMI, DEN = range(18)
        DMA(P, "sync", Dcol[:], dr["d_skip"][0].rearrange("(k gl) p -> (gl p) k", gl=8), b_D, ext, slow=True)
        DMA(P, "sync", wglu[:], dr["wglu_bf"][0], b_wglu, dr["wglu_bf"][1])
        DMA(P, "sync", bglu[:], dr["b_glu"][0].rearrange("(k p) -> p k", p=128), b_bglu, ext, slow=True)

        with ExitStack() as st:
            sb = lambda n, s, d: st.enter_context(nc.sbuf_tensor(_uid(n), s, d))
            ps = lambda n, s, d: st.enter_context(nc.psum_tensor(_uid(n), s, d))
            are = sb("are", [128, 32], F32); aim = sb("aim", [128, 32], F32); ldt = sb("ldt", [128, 32], F32)
            b_par = B("par")
            DMA(P, "sync", are[:], dr["a_re"][0].rearrange("(pi g) n -> (g n) pi", g=2), b_par, ext, slow=True)
            DMA(P, "sync", aim[:], dr["a_im"][0].rearrange("(pi g) n -> (g n) pi", g=2), b_par, ext, slow=True)
            for g2 in range(2):
                src = dr["log_dt"][0].rearrange("(pi g) -> g pi", g=2)[g2:g2 + 1, :].to_broadcast([64, 32])
                DMA(P, "sync", ldt[64 * g2:64 * g2 + 64, :], src, b_par, ext, slow=True)
            Bre = sb("Bre", [128, 32, 16], F32); Bim = sb("Bim", [128, 32, 16], F32); b_B = B("B")
            DMA(P, "sync", Bre[:], dr["b_re"][0].rearrange("(pi g) n q -> (g n) pi q", g=2), b_B, ext, slow=True)
            DMA(P, "sync", Bim[:], dr["b_im"][0].rearrange("(pi g) n q -> (g n) pi q", g=2), b_B, ext, slow=True)
            id32 = sb("id32", [128, 128], F32); b_id = B("id32")
            DMA(P, "sync", id32[:], dr["ident32"][0], b_id, ext)
            Cx = [sb("Cx%d" % i, [128, 8, 128], F32) for i in range(2)]; b_Cx = B("Cx")
            for i in range(2):
                P.op("pool", lambda e, i=i: e.memset(Cx[i][:], 0.0), [], [b_Cx])
            for i, nm in enumerate(("c_re", "c_im")):
                cv = dr[nm][0].rearrange("(pi8 pi4 g) p n -> pi4 g p pi8 n", pi4=4, g=2)
                for pi4 in range(4):
                    for g2 in range(2):
                        p0 = pi4 * 32 + g2 * 16
                        DMA(P, "sync", Cx[i][p0:p0 + 16, :, 64 * g2:64 * g2 + 64], cv[pi4, g2], b_Cx, ext, slow=True)
            ki = sb("ki", [128, 32], I32); b_ki = B("ki")

            def sin_of(out, ang):
                TS(P, "dve", S(T1), ang, 1.0 / TWO_PI, None, ALU.mult, None, [b_sc], [b_sc])
                CP(P, "dve", ki[:], S(T1), [b_sc], [b_ki])
                CP(P, "dve", S(T1), ki[:], [b_ki], [b_sc])
                STT(P, "dve", S(T2), S(T1), -TWO_PI, ang, ALU.mult, ALU.add, [b_sc], [b_sc])
                TS(P, "dve", S(T3), S(T2), math.pi, -TWO_PI, ALU.is_gt, ALU.mult, [b_sc], [b_sc])
                TT(P, "dve", S(T2), S(T2), S(T3), ALU.add, [b_sc], [b_sc])
                TS(P, "dve", S(T3), S(T2), -math.pi, TWO_PI, ALU.is_lt, ALU.mult, [b_sc], [b_sc])
                TT(P, "dve", S(T2), S(T2), S(T3), ALU.add, [b_sc], [b_sc])
                ACTF(P, out, S(T2), ACT.Sin, [b_sc], [b_sc])

            ACTF(P, S(DT), ldt[:], ACT.Exp, [b_par], [b_sc])
            TT(P, "dve", S(MAG), are[:], S(DT), ALU.mult, [b_par, b_sc], [b_sc])
            ACTF(P, S(RHO), S(MAG), ACT.Exp, [b_sc], [b_sc], scale=float(T0))
            ACTF(P, S(MAG), S(MAG), ACT.Exp, [b_sc], [b_sc])
            TT(P, "dve", S(PHI), aim[:], S(DT), ALU.mult, [b_par, b_sc], [b_sc])
            sin_of(S(SINP), S(PHI))
            TS(P, "dve", S(NUMR), S(PHI), math.pi / 2, None, ALU.add, None, [b_sc], [b_sc])
            sin_of(S(COSP), S(NUMR))
            TT(P, "dve", S(ABR), S(MAG), S(COSP), ALU.mult, [b_sc], [b_sc])
            TT(P, "dve", S(ABI), S(MAG), S(SINP), ALU.mult, [b_sc], [b_sc])
            TS(P, "dve", S(T1), S(ABR), -1.0, None, ALU.add, None, [b_sc], [b_sc])
            TT(P, "dve", S(NUMR), S(T1), are[:], ALU.mult, [b_sc, b_par], [b_sc])
            TT(P, "dve", S(T2), S(ABI), aim[:], ALU.mult, [b_sc, b_par], [b_sc])
            TT(P, "dve", S(NUMR), S(NUMR), S(T2), ALU.add, [b_sc], [b_sc])
            TT(P, "dve", S(NUMI), S(ABI), are[:], ALU.mult, [b_sc, b_par], [b_sc])
            TT(P, "dve", S(T2), S(T1), aim[:], ALU.mult, [b_sc, b_par], [b_sc])
            TT(P, "dve", S(NUMI), S(NUMI), S(T2), ALU.subtract, [b_sc], [b_sc])
            TT(P, "dve", S(DEN), are[:], are[:], ALU.mult, [b_par], [b_sc])
            TT(P, "dve", S(T2), aim[:], aim[:], ALU.mult, [b_par], [b_sc])
            TT(P, "dve", S(DEN), S(DEN), S(T2), ALU.add, [b_sc], [b_sc])
            P.op("dve", lambda e: e.reciprocal(out=S(DEN), in_=S(DEN)), [b_sc], [b_sc])
            TT(P, "dve", S(CR), S(NUMR), S(DEN), ALU.mult, [b_sc], [b_sc])
            TT(P, "dve", S(CI), S(NUMI), S(DEN), ALU.mult, [b_sc], [b_sc])
            CP(P, "dve", S(C8), S(COSP), [b_sc], [b_sc])
            CP(P, "dve", S(S8), S(SINP), [b_sc], [b_sc])
            for _ in range(3):
                TT(P, "dve", S(T1), S(C8), S(C8), ALU.mult, [b_sc], [b_sc])
                TT(P, "dve", S(T2), S(S8), S(S8), ALU.mult, [b_sc], [b_sc])
                TT(P, "dve", S(T3), S(C8), S(S8), ALU.mult, [b_sc], [b_sc])
                TT(P, "dve", S(C8), S(T1), S(T2), ALU.subtract, [b_sc], [b_sc])
                TS(P, "dve", S(S8), S(T3), 2.0, None, ALU.mult, None, [b_sc], [b_sc])
            P.op("dve", lambda e: e.memset(pw[:, 0, 0, :], 1.0), [], [b_pw])
            P.op("dve", lambda e: e.memset(pw[:, 1, 0, :], 0.0), [], [b_pw])
            for k in range(1, T0 + 1):
                pr, pi_ = pw[:, 0, k - 1, :], pw[:, 1, k - 1, :]
                TT(P, "dve", S(T1), pr, S(ABR), ALU.mult, [b_pw, b_sc], [b_sc])
                TT(P, "dve", S(T2), pi_, S(ABI), ALU.mult, [b_pw, b_sc], [b_sc])
                TT(P, "dve", pw[:, 0, k, :], S(T1), S(T2), ALU.subtract, [b_sc], [b_pw])
                TT(P, "dve", S(T1), pr, S(ABI), ALU.mult, [b_pw, b_sc], [b_sc])
                TT(P, "dve", S(T2), pi_, S(ABR), ALU.mult, [b_pw, b_sc], [b_sc])
                TT(P, "dve", pw[:, 1, k, :], S(T1), S(T2), ALU.add, [b_sc], [b_pw])
            TS(P, "dve", pw[:, 2, :, :], pw[:, 1, :, :], -1.0, None, ALU.mult, None, [b_pw], [b_pw])
            Bb = sb("Bb", [128, 2, 32, 16], F32); b_Bb = B("Bb")
            tA = sb("tA", [128, T0, 32, 16], F32); tB = sb("tB", [128, T0, 32, 16], F32); b_t = B("tAB")
            bc16 = lambda j: S(j).unsqueeze(2).to_broadcast([128, 32, 16])
            cmul(P, "dve", Bb[:, 0], Bb[:, 1], bc16(CR), bc16(CI), Bre[:], Bim[:], tA[:, 0], tB[:, 0],
                 [b_sc, b_B], [b_Bb], b_t)
            WBx = sb("WBx", [128, T0, 2, 32, 32], F32); b_WBx = B("WBx")
            P.op("pool", lambda e: e.memset(WBx[:], 0.0), [], [b_WBx])
            for g2 in range(2):
                ps_ = slice(64 * g2, 64 * g2 + 64)
                cs_ = slice(16 * g2, 16 * g2 + 16)
                pwb = lambda ri: pw[ps_, ri, 0:T0, :].unsqueeze(3).to_broadcast([64, T0, 32, 16])
                bbb = lambda ri: Bb[ps_, ri].unsqueeze(1).to_broadcast([64, T0, 32, 16])
                cmul(P, "dve", WBx[ps_, :, 0, :, cs_], WBx[ps_, :, 1, :, cs_], pwb(0), pwb(1), bbb(0), bbb(1),
                     tA[ps_], tB[ps_], [b_pw, b_Bb], [b_WBx], b_t)
            pTr = [ps("pTr%d" % i, [128, 4, 128], F32) for i in range(2)]
            b_pTr = [B("pTr0"), B("pTr1")]
            nt = 0
            for pi8 in range(8):
                for tau in range(T0):
                    s = nt % 2; nt += 1
                    fns = []
                    for ri in range(2):
                        fns.append(lambda e, s=s, ri=ri, pi8=pi8, tau=tau: e.transpose(
                            out=pTr[s][:, ri, :],
                            in_=WBx[:, tau, ri, 4 * pi8:4 * pi8 + 4, :].rearrange("p a b -> p (a b)"),
                            identity=id32[:]))
                    P.pe_group(fns, [b_WBx, b_id], [b_pTr[s]])
                    CP(P, "act" if nt % 2 else "dve", WBT[:, pi8, tau, :, :], pTr[s][:, 0:2, :], [b_pTr[s]], [b_WBT])
            for pi8 in range(0, 8, 2):
                s = nt % 2; nt += 1
                fns = []
                for j in range(2):
                    for ri in range(2):
                        fns.append(lambda e, s=s, ri=ri, j=j, pi8=pi8: e.transpose(
                            out=pTr[s][:, 2 * j + ri, :], in_=Cx[ri][:, pi8 + j, :], identity=id32[:]))
                P.pe_group(fns, [b_Cx, b_id], [b_pTr[s]])
                for j in range(2):
                    CP(P, "dve", WCT[:, pi8 + j, 0, :], pTr[s][:, 2 * j, :], [b_pTr[s]], [b_WCT])
                    TS(P, "dve", WCT[:, pi8 + j, 1, :], pTr[s][:, 2 * j + 1, :], -1.0, None, ALU.mult, None,
                       [b_pTr[s]], [b_WCT])
            CP(P, "dve", Rc[:, :, 0], S(C8), [b_sc], [b_R])
            CP(P, "dve", Rs[:, :, 0], S(S8), [b_sc], [b_R])
            m = 1
            tAv = tA[:].rearrange("p a b c -> p (a b c)")
            tBv = tB[:].rearrange("p a b c -> p (a b c)")
            while m < NSC:
                bc = lambda t: t[:, :, m - 1:m].to_broadcast([128, 32, m])
                t1v = tAv[:, 0:32 * m].rearrange("p (a b) -> p a b", a=32)
                t2v = tBv[:, 0:32 * m].rearrange("p (a b) -> p a b", a=32)
                cmul(P, "dve", Rc[:, :, m:2 * m], Rs[:, :, m:2 * m], Rc[:, :, 0:m], Rs[:, :, 0:m], bc(Rc), bc(Rs),
                     t1v, t2v, [b_R], [b_R], b_t)
                m *= 2
            P.flush(scope="s5pre")

        NQ = 8
        uT_d, b_uTd = dr["uT_d"]
        ssm_d, b_ssmd = dr["ssm_d"]

        def alloc_set(sb, ps, tag, rr=False):
            Sx = {}
            for nm in ("bA", "bB", "bC", "bD"):
                Sx[nm] = sb(nm + tag, [128, NQ, NSC], F32); Sx["b_" + nm] = B(nm + tag)
            for nm in ("bE", "bF"):
                Sx[nm] = sb(nm + tag, [128, NQ, NSC + 1], F32); Sx["b_" + nm] = B(nm + tag)
            Sx["rr"] = rr
            if rr:
                Sx["psS"] = [ps("psS%d%s" % (i, tag), [128, 512], F32) for i in range(4)]
            else:
                Sx["psS"] = [ps("psS%d%s" % (i, tag), [128, 4, NSC], F32) for i in range(2)]
            Sx["b_psS"] = B("psS" + tag)
            Sx["b_Ap"] = [B("Ap%d%s" % (i, tag)) for i in range(NQ)]; Sx["b_Bp"] = [B("Bp%d%s" % (i, tag)) for i in range(NQ)]
            return Sx

        def quarter(Sx, uTb, b_uTb, q4, own):
            bA, bB, bC, bD, bE, bF = Sx["bA"], Sx["bB"], Sx["bC"], Sx["bD"], Sx["bE"], Sx["bF"]
            b_A, b_Bq, b_C, b_Dq, b_E, b_F = Sx["b_bA"], Sx["b_bB"], Sx["b_bC"], Sx["b_bD"], Sx["b_bE"], Sx["b_bF"]
            psS, b_psS = Sx["psS"], Sx["b_psS"]
            u3 = uTb[:].rearrange("p k (c j) -> p k j c", j=T0)
            psl = slice(NQ * q4, NQ * q4 + NQ)
            for h8 in range(NQ // 4):
                pi8 = (NQ // 4) * q4 + h8
                fns = []
                if Sx["rr"]:
                    for ri in range(2):
                        for j in range(T0):
                            for pi4 in range(4):
                                r0 = 32 * pi4
                                fns.append(lambda e, r0=r0, ri=ri, j=j, pi8=pi8, pi4=pi4, u3=u3, psS=psS: e.matmul(
                                    psS[pi4][:, ri * NSC:(ri + 1) * NSC], lhsT=WBT[r0:r0 + 32, pi8, T0 - 1 - j, ri, :],
                                    rhs=u3[r0:r0 + 32, pi8, j, :], start=(j == 0), stop=(j == T0 - 1),
                                    tile_position=(r0, 0)))
                    P.pe_group(fns, [b_WBT, b_uTb], [b_psS])
                    for pi4 in range(4):
                        CP(P, "act", bA[:, 4 * h8 + pi4, :], psS[pi4][:, 0:NSC], [b_psS], [b_A] + Sx["b_Ap"])
                        CP(P, "act", bB[:, 4 * h8 + pi4, :], psS[pi4][:, NSC:2 * NSC], [b_psS], [b_Bq] + Sx["b_Bp"])
                    continue
                for pi4 in range(4):
                    r0 = 32 * pi4
                    for ri in range(2):
                        for j in range(T0):
                            fns.append(lambda e, r0=r0, ri=ri, j=j, pi8=pi8, pi4=pi4, u3=u3, psS=psS: e.matmul(
                                psS[ri][:, pi4, :], lhsT=WBT[r0:r0 + 32, pi8, T0 - 1 - j, ri, :],
                                rhs=u3[r0:r0 + 32, pi8, j, :], start=(j == 0), stop=(j == T0 - 1),
                                tile_position=(r0, 0)))
                P.pe_group(fns, [b_WBT, b_uTb], [b_psS])
                CP(P, "act", bA[:, 4 * h8:4 * h8 + 4, :], psS[0][:], [b_psS], [b_A] + Sx["b_Ap"])
                CP(P, "act", bB[:, 4 * h8:4 * h8 + 4, :], psS[1][:], [b_psS], [b_Bq] + Sx["b_Bp"])
            rc, rs = Rc[:, psl, :], Rs[:, psl, :]
            E1, F1 = bE[:, :, 1:], bF[:, :, 1:]
            b_Ap, b_Bp = Sx["b_Ap"], Sx["b_Bp"]
            TT(P, "dve", bC[:], bA[:], rc, ALU.mult, [b_A, b_R] + b_Ap, [b_C])
            TT(P, "pool", E1, bB[:], rs, ALU.mult, [b_Bq, b_R] + b_Bp, [b_E])
            TT(P, "dve", bD[:], bB[:], rc, ALU.mult, [b_Bq, b_R] + b_Bp, [b_Dq])
            TT(P, "pool", F1, bA[:], rs, ALU.mult, [b_A, b_R] + b_Ap, [b_F])
            TT(P, "dve", bC[:], bC[:], E1, ALU.add, [b_C, b_E], [b_C])
            TT(P, "dve", bD[:], bD[:], F1, ALU.subtract, [b_Dq, b_F], [b_Dq])
            for p_ in range(NQ):
                pi = NQ * q4 + p_
                P.op("dve", lambda e, pi=pi, p_=p_, bA=bA, bC=bC: e.tensor_tensor_scan(
                    out=bA[:, p_, :], data0=sc[:, RHO, pi:pi + 1].to_broadcast([128, NSC]), data1=bC[:, p_, :],
                    initial=cr[:, pi, :], op0=ALU.mult, op1=ALU.add), [b_C, b_sc, b_c, b_A], [b_Ap[p_]])
                P.op("dve", lambda e, pi=pi, p_=p_, bB=bB, bD=bD: e.tensor_tensor_scan(
                    out=bB[:, p_, :], data0=sc[:, RHO, pi:pi + 1].to_broadcast([128, NSC]), data1=bD[:, p_, :],
                    initial=ci[:, pi, :], op0=ALU.mult, op1=ALU.add), [b_Dq, b_sc, b_c, b_Bq], [b_Bp[p_]])
            co = slice(0, NSC) if own else slice(NSC - 1, NSC)
            rco, rso = rc[:, :, co], rs[:, :, co]
            TT(P, "dve", E1[:, :, co], bA[:, :, co], rco, ALU.mult, [b_R] + b_Ap, [b_E])
            TT(P, "pool", bC[:, :, co], bB[:, :, co], rso, ALU.mult, [b_R] + b_Bp, [b_C])
            TT(P, "dve", F1[:, :, co], bB[:, :, co], rco, ALU.mult, [b_R] + b_Bp, [b_F])
            TT(P, "pool", bD[:, :, co], bA[:, :, co], rso, ALU.mult, [b_R] + b_Ap, [b_Dq])
            TT(P, "dve", E1[:, :, co], E1[:, :, co], bC[:, :, co], ALU.subtract, [b_E, b_C], [b_E])
            TT(P, "dve", F1[:, :, co], F1[:, :, co], bD[:, :, co], ALU.add, [b_F, b_Dq], [b_F])
            if own:
                CP(P, "dve", bE[:, :, 0:1], cr[:, psl, :], [b_c], [b_E])
                CP(P, "dve", bF[:, :, 0:1], ci[:, psl, :], [b_c], [b_F])
            CP(P, "dve", cr[:, psl, :], bE[:, :, NSC:NSC + 1], [b_E], [b_c])
            CP(P, "dve", ci[:, psl, :], bF[:, :, NSC:NSC + 1], [b_F], [b_c])

        with ExitStack() as st:
            sb = lambda n, s, d: st.enter_context(nc.sbuf_tensor(_uid(n), s, d))
            ps = lambda n, s, d: st.enter_context(nc.psum_tensor(_uid(n), s, d))
            uTp = [sb("uTp%d" % i, [128, 8, UNIT], BF16) for i in range(2)]; b_uTp = [B("uTp0"), B("uTp1")]
            sets = [alloc_set(sb, ps, "p%d" % i, rr=True) for i in range(2)]
            nq = 0
            for u in range(n_pre_units):
                su = u % 2
                DMA(P, "sync", uTp[su][:], uT_d[:, :, u * UNIT:(u + 1) * UNIT], b_uTp[su], b_uTd)
                for q4 in range(32 // NQ):
                    quarter(sets[nq % 2], uTp[su], b_uTp[su], q4, False)
                    nq += 1
            P.flush(scope="s5prefix")

        with ExitStack() as st:
            sb = lambda n, s, d: st.enter_context(nc.sbuf_tensor(_uid(n), s, d))
            ps = lambda n, s, d: st.enter_context(nc.psum_tensor(_uid(n), s, d))
            uT = sb("uT", [128, 8, UNIT], BF16); b_uT = B("uT")
            Sx = alloc_set(sb, ps, "o")
            bE, bF, b_E, b_F = Sx["bE"], Sx["bF"], Sx["b_bE"], Sx["b_bF"]
            psH = [ps("psH%d" % i, [128, T0, NSC], F32) for i in range(2)]; b_psH = B("psH")
            psY = ps("psY", [128, UNIT], F32); b_psY = B("psY")
            hset = [[sb("h%s%d" % (c, i), [128, T0, NSC], F32) for c in "ABCD"] for i in range(2)]
            b_hset = [[B("h%s%d" % (c, i)) for c in "ABCD"] for i in range(2)]
            Hre = [sb("Hre%d" % i, [128, UNIT], BF16) for i in range(2)]
            Him = [sb("Him%d" % i, [128, UNIT], BF16) for i in range(2)]
            b_H = [B("H0"), B("H1")]
            ysb = sb("ysb", [128, UNIT], F32); b_y = B("ysb")
            gl1 = sb("gl1", [128, UNIT], F32); gl2 = sb("gl2", [128, UNIT], F32); b_g = B("g12")
            ygT = sb("ygT", [128, 8, UNIT], BF16); b_yg = B("ygT")
            gate = sb("gate", [128, 512], F32); b_gate = B("gate")
            sso = [sb("sso%d" % i, [128, UNIT], BF16) for i in range(2)]; b_sso = [B("sso0"), B("sso1")]
            for ou in range(n_own_units):
                u = n_pre_units + ou
                DMA(P, "sync", uT[:], uT_d[:, :, u * UNIT:(u + 1) * UNIT], b_uT, b_uTd)
                u3 = uT[:].rearrange("p k (c j) -> p k j c", j=T0)
                for q4 in range(32 // NQ):
                    quarter(Sx, uT, b_uT, q4, True)
                    for h8 in range(NQ // 4):
                        pi8 = (NQ // 4) * q4 + h8
                        for pi4 in range(4):
                            pi = 4 * pi8 + pi4
                            p_ = 4 * h8 + pi4
                            r0 = 32 * pi4
                            hs = pi % 2
                            fns = []
                            for ri in range(2):
                                for i in range(T0):
                                    for tau in range(i + 1):
                                        fns.append(lambda e, r0=r0, ri=ri, i=i, tau=tau, pi8=pi8, u3=u3: e.matmul(
                                            psH[ri][:, i, :], lhsT=WBT[r0:r0 + 32, pi8, tau, ri, :],
                                            rhs=u3[r0:r0 + 32, pi8, i - tau, :], start=(tau == 0), stop=(tau == i),
                                            tile_position=(r0, 0)))
                            P.pe_group(fns, [b_WBT, b_uT], [b_psH])
                            xr_b = bE[:, p_, 0:NSC].unsqueeze(1).to_broadcast([128, T0, NSC])
                            xi_b = bF[:, p_, 0:NSC].unsqueeze(1).to_broadcast([128, T0, NSC])
                            pwr_b = pw[:, 0, 1:T0 + 1, pi].unsqueeze(2).to_broadcast([128, T0, NSC])
                            pwi_b = pw[:, 1, 1:T0 + 1, pi].unsqueeze(2).to_broadcast([128, T0, NSC])
                            hre_v = Hre[hs][:].rearrange("p (c i) -> p i c", i=T0)
                            him_v = Him[hs][:].rearrange("p (c i) -> p i c", i=T0)
                            hp = pi % 2
                            hA_, hB_, hC_, hD_ = hset[hp]
                            bh = b_hset[hp]
                            TT(P, "pool", hA_[:], xr_b, pwr_b, ALU.mult, [b_E, b_pw], [bh[0]])
                            TT(P, "pool", hB_[:], xi_b, pwi_b, ALU.mult, [b_F, b_pw], [bh[1]])
                            TT(P, "pool", hC_[:], xi_b, pwr_b, ALU.mult, [b_F, b_pw], [bh[2]])
                            TT(P, "pool", hD_[:], xr_b, pwi_b, ALU.mult, [b_E, b_pw], [bh[3]])
                            TT(P, "pool", hA_[:], hA_[:], hB_[:], ALU.subtract, [bh[0], bh[1]], [bh[0]])
                            TT(P, "pool", hC_[:], hC_[:], hD_[:], ALU.add, [bh[2], bh[3]], [bh[2]])
                            TT(P, "dve", hre_v, psH[0][:], hA_[:], ALU.add, [b_psH, bh[0]], [b_H[hs]])
                            TT(P, "dve", him_v, psH[1][:], hC_[:], ALU.add, [b_psH, bh[2]], [b_H[hs]])
                            fns = []
                            for half in range(2):
                                hsl = slice(512 * half, 512 * half + 512)
                                fns.append(lambda e, hs=hs, hsl=hsl, r0=r0, pi8=pi8: e.matmul(
                                    psY[r0:r0 + 32, hsl], lhsT=WCT[:, pi8, 0, r0:r0 + 32], rhs=Hre[hs][:, hsl],
                                    start=True, stop=False, tile_position=(0, r0)))
                                fns.append(lambda e, hs=hs, hsl=hsl, r0=r0, pi8=pi8: e.matmul(
                                    psY[r0:r0 + 32, hsl], lhsT=WCT[:, pi8, 1, r0:r0 + 32], rhs=Him[hs][:, hsl],
                                    start=False, stop=True, tile_position=(0, r0)))
                            P.pe_group(fns, [b_WCT, b_H[hs]], [b_psY])
                        STT(P, "dve", ysb[:], uT[:, pi8, :], Dcol[:, pi8:pi8 + 1], psY[:], ALU.mult, ALU.add,
                            [b_uT, b_D, b_psY], [b_y])
                        ACTF(P, gl1[:], ysb[:], ACT.Square, [b_y], [b_g])
                        TS(P, "dve", gl1[:], gl1[:], 0.044715, 1.0, ALU.mult, ALU.add, [b_g], [b_g])
                        TT(P, "dve", gl1[:], gl1[:], ysb[:], ALU.mult, [b_g, b_y], [b_g])
                        ACTF(P, gl2[:], gl1[:], ACT.Sigmoid, [b_g], [b_g], scale=1.5957691216057308)
                        TT(P, "dve", ygT[:, pi8, :], gl2[:], ysb[:], ALU.mult, [b_g, b_y], [b_yg])
                for mt in range(8):
                    so = mt % 2
                    for half in range(2):
                        hsl = slice(512 * half, 512 * half + 512)
                        fns = [lambda e, kt=kt, mt=mt, hsl=hsl: e.matmul(
                            psY[:, hsl], lhsT=wglu[:, kt, 128 * mt:128 * mt + 128], rhs=ygT[:, kt, hsl],
                            start=(kt == 0), stop=(kt == 7)) for kt in range(8)]
                        P.pe_group(fns, [b_wglu, b_yg], [b_psY])
                        ACTF(P, gate[:], psY[:, hsl], ACT.Sigmoid, [b_psY, b_bglu], [b_gate], bias=bglu[:, mt:mt + 1])
                        TT(P, "dve", sso[so][:, hsl], gate[:], ygT[:, mt, hsl], ALU.mult, [b_gate, b_yg], [b_sso[so]])
                    DMA(P, "act", ssm_d[:, mt, ou * UNIT:(ou + 1) * UNIT], sso[so][:], b_ssmd, b_sso[so])
            P.flush(scope="s5own")


def cast_weights(nc, P, pairs):
    for dst, db, src, sbuf_ in pairs:
        rows, cols = dst.shape
        step = max(1, (1 << 20) // cols)
        for r in range(0, rows, step):
            r1 = min(rows, r + step)
            DMA(P, "pool", dst[r:r1, :], src[r:r1, :], db, sbuf_)


NCT = 30
CT_Q, CT_QSW, CT_K, CT_KSW, CT_U = 0, 8, 11, 19, 22


def norm_transpose(nc, P, T, x_rows_ap, b_src, hn_out3, b_hn, gain, b_gain, s):
    xt, junk, ss, rstd, xn, pT, idt, eps_t = T["xt"], T["junk"], T["ss"], T["rstd"], T["xn"], T["pT"], T["idt"], T["eps"]
    s2 = s % 2
    bx, bj, bs, br, bxn, bpT = T["b_xt"][s], T["b_junk"], T["b_ss"][s], T["b_rstd"][s], T["b_xn"][s2], T["b_pT"][s2]
    if x_rows_ap is not None:
        DMA(P, "sync", xt[s][:], x_rows_ap, bx, b_src)
    P.op("act", lambda e: e.activation(out=junk[:], in_=xt[s][:], func=ACT.Square, accum_out=ss[:, s:s + 1]),
         [bx], [bj, bs])
    P.op("act", lambda e: e.activation(out=rstd[:, s:s + 1], in_=ss[:, s:s + 1], func=ACT.Sqrt, scale=1.0 / D,
                                       bias=eps_t[:]), [bs, T["b_eps"]], [br])
    P.op("dve", lambda e: e.reciprocal(out=rstd[:, s:s + 1], in_=rstd[:, s:s + 1]), [br], [br])
    P.op("act", lambda e: e.activation(out=xn[s2][:], in_=xt[s][:], func=ACT.Copy, scale=rstd[:, s:s + 1]),
         [bx, br], [bxn])
    P.pe_group([(lambda e, k=k: e.transpose(out=pT[s2][:, k * 128:(k + 1) * 128], in_=xn[s2][:, k * 128:(k + 1) * 128],
                                            identity=idt[:])) for k in range(KT)], [bxn, T["b_idt"]], [bpT])
    TT(P, "dve", hn_out3, pT[s2][:].rearrange("p (k c) -> p k c", k=KT),
       gain[:].unsqueeze(2).to_broadcast([128, KT, 128]), ALU.mult, [bpT, b_gain], [b_hn])


def norm_tiles(nc, sb, ps, B, nx=2):
    T = {}
    T["xt"] = [sb("xt%d" % i, [128, D], F32) for i in range(nx)]
    T["junk"] = sb("junk", [128, D], BF16)
    T["ss"] = sb("ss", [128, nx], F32); T["rstd"] = sb("rstd", [128, nx], F32)
    T["xn"] = [sb("xn%d" % i, [128, D], BF16) for i in range(2)]
    T["pT"] = [ps("pT%d" % i, [128, D], BF16) for i in range(2)]
    T["idt"] = sb("idt", [128, 128], BF16); T["eps"] = sb("eps_t", [128, 1], F32)
    T["b_xt"] = [B("xt%d" % i) for i in range(nx)]; T["b_junk"] = B("junk"); T["b_ss"] = [B("ss%d" % i) for i in range(nx)]
    T["b_rstd"] = [B("r%d" % i) for i in range(nx)]; T["b_xn"] = [B("xn0"), B("xn1")]; T["b_pT"] = [B("pT0"), B("pT1")]
    T["b_idt"] = B("idt"); T["b_eps"] = B("eps")
    return T


def front_stage(nc, P, dr, n_blocks, first_kv, first_q):
    from contextlib import ExitStack
    B = Buf
    ext = dr["ext"]
    with ExitStack() as st:
        sb = lambda n, s, d: st.enter_context(nc.sbuf_tensor(_uid(n), s, d))
        ps = lambda n, s, d: st.enter_context(nc.psum_tensor(_uid(n), s, d))
        T = norm_tiles(nc, sb, ps, B, nx=4)
        DMA(P, "sync", T["idt"][:], dr["ident_bf"][0], T["b_idt"], ext)
        P.op("dve", lambda e: e.memset(T["eps"][:], 1e-6), [], [T["b_eps"]])
        g1t = sb("g1t", [128, KT], F32); b_g1 = B("g1t")
        DMA(P, "sync", g1t[:], dr["g1"][0], b_g1, ext)
        hnT = [sb("hnT%d" % i, [128, KT, BLK], BF16) for i in range(2)]; b_hn = [B("hn0"), B("hn1")]
        wu = sb("wu", [128, 8, KT, 128], BF16); b_wu = B("wu")
        wfm_bf, b_wfm = dr["wfm_bf"]
        DMA(P, "sync", wu[:], wfm_bf[CT_U:CT_U + 8].rearrange("c p k m -> p c k m"), b_wu, dr.get("wfmu_b", b_wfm))
        ublk = [sb("ublk%d" % i, [128, 8, BLK], BF16) for i in range(2)]; b_ub = [B("ub0"), B("ub1")]
        NWS = 2
        wst = [sb("wst%d" % i, [128, 4 * KT * 128], BF16) for i in range(NWS)]; b_wst = [B("wst%d" % i) for i in range(NWS)]
        pP = [ps("pP%d" % i, [128, BLK], F32) for i in range(3)]; b_pP = [B("pP%d" % i) for i in range(3)]
        cosb = sb("cosb", [128, 3, BLK], F32); sinb = sb("sinb", [128, 3, BLK], F32); b_cs = B("cossin")
        swsin = sb("swsin", [128, 3, BLK], F32); b_sw = B("swsin")
        rtmp = sb("rtmp", [128, BLK], F32); b_rt = B("rtmp")
        rtmp2 = sb("rtmp2", [128, BLK], F32); b_rt2 = B("rtmp2")
        qkblk = [sb("qkblk%d" % i, [128, 8, BLK], BF16) for i in range(2)]; b_qk = [B("qk0"), B("qk1")]
        vrow = sb("vrow", [128, 4, 1024], BF16); b_vr = B("vrow")
        xpad, b_x = dr["xpad"]
        uT_d, b_uTd = dr["uT_d"]
        wv_bf, b_wv = dr["wv_bf"]
        st_ = {"np": 0, "nw": 0, "nqk": 0}

        def proj_tile(lhs_w, hn, b_w, b_h):
            s = st_["np"] % 3; st_["np"] += 1
            P.pe_group([(lambda e, k=k, s=s: e.matmul(pP[s][:], lhsT=lhs_w[:, k, :], rhs=hn[:, k, :],
                                                      start=(k == 0), stop=(k == KT - 1))) for k in range(KT)],
                       [b_w, b_h], [b_pP[s]])
            return s

        def load_chunk(ct0, n):
            s = st_["nw"] % NWS; st_["nw"] += 1
            v4 = wst[s][:, 0:n * KT * 128].rearrange("p (c k m) -> p c k m", c=n, k=KT)
            DMA(P, "sync", v4, wfm_bf[ct0:ct0 + n].rearrange("c p k m -> p c k m"), b_wst[s], b_wfm)
            return s, v4

        def qk_proj(hs, ct_main, ct_sw, out_d, b_outd, col0):
            hn = hnT[hs]
            s, v4 = load_chunk(ct_sw, 3)
            for j in range(3):
                sp = proj_tile(v4[:, j], hn[:], b_wst[s], b_hn[hs])
                CP(P, "act", swsin[:, j, :], pP[sp][:], [b_pP[sp]], [b_sw])
            qs = st_["nqk"] % 2; st_["nqk"] += 1
            for c4 in range(2):
                s, v4 = load_chunk(ct_main + 4 * c4, 4)
                for j in range(4):
                    h = 4 * c4 + j
                    jb = h % 3
                    sp = proj_tile(v4[:, j], hn[:], b_wst[s], b_hn[hs])
                    TT(P, "dve", rtmp[:], pP[sp][:], cosb[:, jb, :], ALU.mult, [b_pP[sp], b_cs], [b_rt])
                    TT(P, "pool", rtmp2[:], swsin[:, h // 3, :], sinb[:, jb, :], ALU.mult, [b_sw, b_cs], [b_rt2])
                    TT(P, "dve", qkblk[qs][:, h, :], rtmp[:], rtmp2[:], ALU.add, [b_rt, b_rt2], [b_qk[qs]])
            DMA(P, "pool", out_d[:, :, col0:col0 + BLK], qkblk[qs][:], b_outd, b_qk[qs])

        nt = 0
        for b in range(n_blocks):
            hs = b % 2
            for t in range(4):
                s = nt % 4; nt += 1
                row = b * BLK + t * 128
                norm_transpose(nc, P, T, xpad[row:row + 128, :], b_x, hnT[hs][:, :, t * 128:(t + 1) * 128], b_hn[hs],
                               g1t, b_g1, s)
            us = b % 2
            for ct in range(8):
                sp = proj_tile(wu[:, ct], hnT[hs][:], b_wu, b_hn[hs])
                CP(P, "act", ublk[us][:, ct, :], pP[sp][:], [b_pP[sp]], [b_ub[us]])
            DMA(P, "pool", uT_d[:, :, b * BLK:(b + 1) * BLK], ublk[us][:], b_uTd, b_ub[us])
            if b < first_kv:
                continue
            wb = b - first_kv
            DMA(P, "sync", cosb[:], dr["cos_d"][0][:, :, wb * BLK:(wb + 1) * BLK], b_cs, ext)
            DMA(P, "sync", sinb[:], dr["sin_d"][0][:, :, wb * BLK:(wb + 1) * BLK], b_cs, ext)
            if b >= first_q:
                qk_proj(hs, CT_Q, CT_QSW, dr["qT_d"][0], dr["qT_d"][1], (b - first_q) * BLK)
            qk_proj(hs, CT_K, CT_KSW, dr["kT_d"][0], dr["kT_d"][1], wb * BLK)
            for half in range(2):
                if "v" in SKIP:
                    break
                s = st_["nw"] % NWS; st_["nw"] += 1
                vv = wst[s][:, 0:KT * 512].rearrange("p (k n) -> p k n", k=KT)
                DMA(P, "sync", vv, wv_bf[:, :, half * 512:(half + 1) * 512], b_wst[s], b_wv)
                for t in range(4):
                    sp = st_["np"] % 3; st_["np"] += 1
                    P.pe_group([(lambda e, k=k, sp=sp, t=t, vv=vv, hs=hs: e.matmul(
                        pP[sp][:], lhsT=hnT[hs][:, k, t * 128:(t + 1) * 128], rhs=vv[:, k, :],
                        start=(k == 0), stop=(k == KT - 1))) for k in range(KT)], [b_wst[s], b_hn[hs]], [b_pP[sp]])
                    CP(P, "act", vrow[:, t, half * 512:(half + 1) * 512], pP[sp][:], [b_pP[sp]], [b_vr])
            if "v" not in SKIP:
                DMA(P, "pool", dr["V_d"][0][wb * BLK:(wb + 1) * BLK, :].rearrange("(t p) c -> p t c", p=128), vrow[:],
                    dr["V_d"][1], b_vr)
        P.flush(scope="front")


def attn_stage(nc, P, dr):
    from contextlib import ExitStack
    B = Buf
    ext = dr["ext"]
    W = 2 * NTOK
    SC = 128 ** -0.5
    with ExitStack() as st:
        sb = lambda n, s, d: st.enter_context(nc.sbuf_tensor(_uid(n), s, d))
        ps = lambda n, s, d: st.enter_context(nc.psum_tensor(_uid(n), s, d))
        cs = sb("cs", [128, 4, 128], BF16); b_c = B("cs")
        DMA(P, "sync", cs[:], dr["aconsts"][0], b_c, ext)
        hm = sb("hm", [128, 1], F32); b_hm = B("hm")
        DMA(P, "sync", hm[:], dr["hmask"][0], b_hm, ext)
        qT = [sb("qTh%d" % i, [128, NTOK], BF16) for i in range(2)]; b_q = [B("q0"), B("q1")]
        kT = [sb("kTh%d" % i, [128, W], BF16) for i in range(2)]; b_k = [B("k0"), B("k1")]
        num = sb("num", [128, NTOK], F32); den = sb("den", [128, NTOK], F32); b_num, b_den = B("num"), B("den")
        outb = [sb("outb%d" % i, [128, NTOK], BF16) for i in range(2)]; b_ob = [B("ob0"), B("ob1")]
        NV = 6
        vt = [sb("vt%d" % i, [128, 128], BF16) for i in range(NV)]; b_vt = [B("vt%d" % i) for i in range(NV)]
        pt = [sb("pt%d" % i, [128, 128], BF16) for i in range(4)]; b_pt = [B("pt%d" % i) for i in range(4)]
        pS = [ps("pS%d" % i, [128, 128], F32) for i in range(4)]; b_pS = [B("pS%d" % i) for i in range(4)]
        pO = [ps("pO%d" % i, [128, 128], F32) for i in range(2)]; b_pO = [B("pO0"), B("pO1")]
        pL = [ps("pL%d" % i, [128, 128], F32) for i in range(2)]; b_pL = [B("pL0"), B("pL1")]
        qT_d, b_qd = dr["qT_d"]; kT_d, b_kd = dr["kT_d"]; V_d, b_vd = dr["V_d"]; mix_d, b_mix = dr["mix_d"]
        iv = 0; ip = 0; blk = 0
        for h in range(8):
            hs = h % 2
            DMA(P, "sync", qT[hs][:], qT_d[:, h, :], b_q[hs], b_qd)
            DMA(P, "sync", kT[hs][:], kT_d[:, h, :], b_k[hs], b_kd)
            for pi_, dil in enumerate((1, 4, 16)):
                nb_all = W // dil // 128
                nb0 = nb_all // 2
                for n in range(nb0, nb_all):
                    for r in range(dil):
                        def wpos(nn):
                            s0 = nn * 128 * dil + r
                            return slice(s0, s0 + 127 * dil + 1, dil)
                        pq_w = wpos(n)
                        pq = slice(pq_w.start - NTOK, pq_w.stop - NTOK, dil)
                        so = blk % 2; blk += 1
                        slots = []
                        for (kn, mi) in ((n, 1), (n - 1, 2)):
                            pk = wpos(kn)
                            halo = kn < nb0
                            sv = iv % NV; iv += 1
                            sp = ip % 4; ip += 1
                            slots.append((sv, sp))
                            DMA(P, "sync", vt[sv][:], V_d[pk, 128 * h:128 * h + 128], b_vt[sv], b_vd)
                            P.pe_group([lambda e, sp=sp, pk=pk, pq=pq, hs=hs: e.matmul(
                                            pS[sp][:], lhsT=kT[hs][:, pk], rhs=qT[hs][:, pq], start=True, stop=False),
                                        lambda e, sp=sp, mi=mi: e.matmul(
                                            pS[sp][:], lhsT=cs[:, 0, :], rhs=cs[:, mi, :], start=False, stop=True)],
                                       [b_k[hs], b_q[hs], b_c], [b_pS[sp]])
                            if halo:
                                ACTF(P, pt[sp][:], pS[sp][:], ACT.Exp, [b_pS[sp], b_hm], [b_pt[sp]], scale=SC, bias=hm[:])
                            else:
                                ACTF(P, pt[sp][:], pS[sp][:], ACT.Exp, [b_pS[sp]], [b_pt[sp]], scale=SC)
                        P.pe_group([lambda e, sv=sv, sp=sp, i=i, so=so: e.matmul(
                            pO[so][:], lhsT=vt[sv][:], rhs=pt[sp][:], start=(i == 0), stop=(i == 1))
                            for i, (sv, sp) in enumerate(slots)],
                            [b_vt[sv] for sv, _ in slots] + [b_pt[sp] for _, sp in slots], [b_pO[so]])
                        P.pe_group([lambda e, sp=sp, i=i, so=so: e.matmul(
                            pL[so][:], lhsT=cs[:, 3, :], rhs=pt[sp][:], start=(i == 0), stop=(i == 1))
                            for i, (_, sp) in enumerate(slots)], [b_c] + [b_pt[sp] for _, sp in slots], [b_pL[so]])
                        if pi_ == 0:
                            CP(P, "dve", num[:, pq], pO[so][:], [b_pO[so]], [b_num])
                            CP(P, "act", den[:, pq], pL[so][:], [b_pL[so]], [b_den])
                        else:
                            TT(P, "dve", num[:, pq], pO[so][:], num[:, pq], ALU.add, [b_pO[so], b_num], [b_num])
                            TT(P, "dve", den[:, pq], pL[so][:], den[:, pq], ALU.add, [b_pL[so], b_den], [b_den])
            P.op("dve", lambda e: e.reciprocal(out=den[:], in_=den[:]), [b_den], [b_den])
            TT(P, "dve", outb[hs][:], num[:], den[:], ALU.mult, [b_num, b_den], [b_ob[hs]])
            DMA(P, "pool", mix_d[:, h, :], outb[hs][:], b_mix, b_ob[hs])
        P.flush(scope="attn")


def tail1_stage(nc, P, dr, x_row0):
    from contextlib import ExitStack
    B = Buf
    ext = dr["ext"]
    with ExitStack() as st:
        sb = lambda n, s, d: st.enter_context(nc.sbuf_tensor(_uid(n), s, d))
        ps = lambda n, s, d: st.enter_context(nc.psum_tensor(_uid(n), s, d))
        T = norm_tiles(nc, sb, ps, B)
        DMA(P, "sync", T["idt"][:], dr["ident_bf"][0], T["b_idt"], ext)
        P.op("dve", lambda e: e.memset(T["eps"][:], 1e-6), [], [T["b_eps"]])
        g2t = sb("g2t_", [128, KT], F32); b_g2 = B("g2t")
        DMA(P, "sync", g2t[:], dr["g2"][0], b_g2, ext)
        wout = sb("wout", [128, KT, D], BF16); b_wo = B("wout")
        DMA(P, "sync", wout[:], dr["wout_bf"][0], b_wo, dr["wout_bf"][1])
        mixt = [sb("mixt%d" % i, [128, KT, 128], BF16) for i in range(2)]; b_mt = [B("mt0"), B("mt1")]
        xin = [sb("xin%d" % i, [128, D], F32) for i in range(2)]; b_xin = [B("xin0"), B("xin1")]
        hn2b = [sb("hn2b%d" % i, [128, KT, 128], BF16) for i in range(2)]; b_h2 = [B("h2b0"), B("h2b1")]
        pH = ps("pH", [128, D], F32); b_pH = B("pH")
        xpad, b_x = dr["xpad"]; mix_d, b_mix = dr["mix_d"]; h_d, b_hd = dr["h_d"]; hn2_d, b_hn2d = dr["hn2T_d"]
        for i in range(NTOK // 128):
            s = i % 2
            DMA(P, "sync", mixt[s][:], mix_d[:, :, i * 128:(i + 1) * 128], b_mt[s], b_mix)
            DMA(P, "sync", xin[s][:], xpad[x_row0 + i * 128:x_row0 + (i + 1) * 128, :], b_xin[s], b_x)
            fns = []
            for fb in range(4):
                for k in range(KT):
                    fns.append(lambda e, s=s, fb=fb, k=k: e.matmul(
                        pH[:, fb * 512:(fb + 1) * 512], lhsT=mixt[s][:, k, :], rhs=wout[:, k, fb * 512:(fb + 1) * 512],
                        start=(k == 0), stop=(k == KT - 1)))
            P.pe_group(fns, [b_mt[s], b_wo], [b_pH])
            TT(P, "dve", T["xt"][s][:], pH[:], xin[s][:], ALU.add, [b_pH, b_xin[s]], [T["b_xt"][s]])
            DMA(P, "pool", h_d[i * 128:(i + 1) * 128, :], T["xt"][s][:], b_hd, T["b_xt"][s])
            norm_transpose(nc, P, T, None, None, hn2b[s][:], b_h2[s], g2t, b_g2, s)
            DMA(P, "pool", hn2_d[:, :, i * 128:(i + 1) * 128], hn2b[s][:], b_hn2d, b_h2[s])
        P.flush(scope="tail1")


DFF = 5632
NF = DFF // 128


def tail2_stage(nc, P, dr):
    from contextlib import ExitStack
    B = Buf
    ext = dr["ext"]
    with ExitStack() as st:
        sb = lambda n, s, d: st.enter_context(nc.sbuf_tensor(_uid(n), s, d))
        ps = lambda n, s, d: st.enter_context(nc.psum_tensor(_uid(n), s, d))
        hn2 = [sb("hn2s%d" % i, [128, KT, BLK], BF16) for i in range(2)]; b_hn2 = [B("hn2s0"), B("hn2s1")]
        HT = sb("HT", [128, NF, BLK], BF16); b_HT = B("HT")
        wst = [sb("wst2_%d" % i, [128, KT * 512], BF16) for i in range(3)]; b_wst = [B("w2_%d" % i) for i in range(3)]
        gsig = sb("gsig", [128, BLK], F32); b_gs = B("gsig")
        hin = [sb("hin%d" % i, [128, D], F32) for i in range(4)]; b_hin = [B("hin%d" % i) for i in range(4)]
        junk = sb("junk2", [128, D], F32); b_junk = B("junk2")
        gF = sb("gF", [128, D], F32); b_gF = B("gF")
        DMA(P, "sync", gF[:], dr["final_g"][0].partition_broadcast(128), b_gF, ext)
        ss = sb("ss2", [128, 4], F32); rstd = sb("rstd2", [128, 4], F32); b_ss = [B("ss2_%d" % i) for i in range(4)]
        eps_t = sb("eps2", [128, 1], F32); b_eps = B("eps2")
        P.op("dve", lambda e: e.memset(eps_t[:], 1e-6), [], [b_eps])
        pG = [ps("pG%d" % i, [128, BLK], F32) for i in range(2)]; b_pG = [B("pG0"), B("pG1")]
        pU = [ps("pU%d" % i, [128, BLK], F32) for i in range(2)]; b_pU = [B("pU0"), B("pU1")]
        pD = [ps("pD%d" % i, [128, 1024], F32) for i in range(2)]; b_pD = [B("pD0"), B("pD1")]
        hn2_d, b_hn2d = dr["hn2T_d"]; h_d, b_hd = dr["h_d"]; y_d, b_yd = dr["y"]
        wg_bf, b_wg = dr["wgate_bf"]; wu_bf, b_wub = dr["wup_bf"]; wd_bf, b_wd = dr["wdown_bf"]
        nw = 0
        for sbk in range(NTOK // BLK):
            hs = sbk % 2
            DMA(P, "sync", hn2[hs][:], hn2_d[:, :, sbk * BLK:(sbk + 1) * BLK], b_hn2[hs], b_hn2d)
            for fc in range(NF // 4):
                sg = nw % 3; nw += 1
                vg = wst[sg][:].rearrange("p (k n) -> p k n", k=KT)
                DMA(P, "sync", vg, wg_bf[:, :, fc * 512:(fc + 1) * 512], b_wst[sg], b_wg)
                su = nw % 3; nw += 1
                vu = wst[su][:].rearrange("p (k n) -> p k n", k=KT)
                DMA(P, "sync", vu, wu_bf[:, :, fc * 512:(fc + 1) * 512], b_wst[su], b_wub)
                for j in range(4):
                    f = 4 * fc + j
                    s = f % 2
                    P.pe_group([(lambda e, k=k, s=s, j=j, vg=vg, hs=hs: e.matmul(
                        pG[s][:], lhsT=vg[:, k, j * 128:(j + 1) * 128], rhs=hn2[hs][:, k, :],
                        start=(k == 0), stop=(k == KT - 1))) for k in range(KT)], [b_wst[sg], b_hn2[hs]], [b_pG[s]])
                    P.pe_group([(lambda e, k=k, s=s, j=j, vu=vu, hs=hs: e.matmul(
                        pU[s][:], lhsT=vu[:, k, j * 128:(j + 1) * 128], rhs=hn2[hs][:, k, :],
                        start=(k == 0), stop=(k == KT - 1))) for k in range(KT)], [b_wst[su], b_hn2[hs]], [b_pU[s]])
                    ACTF(P, gsig[:], pG[s][:], ACT.Silu, [b_pG[s]], [b_gs])
                    TT(P, "dve", HT[:, f, :], pU[s][:], gsig[:], ALU.mult, [b_pU[s], b_gs], [b_HT])
            for t in range(4):
                i = sbk * 4 + t
                DMA(P, "sync", hin[t][:], h_d[i * 128:(i + 1) * 128, :], b_hin[t], b_hd)
            for fc in range(NF // 4):
                sd = nw % 3; nw += 1
                vd = wst[sd][:].rearrange("p (f n) -> p f n", f=4)
                DMA(P, "sync", vd, wd_bf[:, fc * 4:(fc + 1) * 4, :], b_wst[sd], b_wd)
                for t in range(4):
                    for hf in range(2):
                        fns = []
                        for j in range(4):
                            f = 4 * fc + j
                            for fb2 in range(2):
                                fb = 2 * hf + fb2
                                fns.append(lambda e, j=j, f=f, fb=fb, fb2=fb2, hf=hf, vd=vd, t=t: e.matmul(
                                    pD[hf][:, fb2 * 512:(fb2 + 1) * 512], lhsT=HT[:, f, t * 128:(t + 1) * 128],
                                    rhs=vd[:, j, fb * 512:(fb + 1) * 512], start=(j == 0), stop=(j == 3)))
                        P.pe_group(fns, [b_HT, b_wst[sd]], [b_pD[hf]])
                        hsl = slice(1024 * hf, 1024 * hf + 1024)
                        TT(P, "dve", hin[t][:, hsl], pD[hf][:], hin[t][:, hsl], ALU.add, [b_pD[hf], b_hin[t]], [b_hin[t]])
            for t in range(4):
                i = sbk * 4 + t
                s2 = t
                P.op("act", lambda e, s2=s2: e.activation(out=junk[:], in_=hin[s2][:], func=ACT.Square,
                                                          accum_out=ss[:, s2:s2 + 1]), [b_hin[s2]], [b_junk, b_ss[s2]])
                P.op("act", lambda e, s2=s2: e.activation(out=rstd[:, s2:s2 + 1], in_=ss[:, s2:s2 + 1], func=ACT.Sqrt,
                                                          scale=1.0 / D, bias=eps_t[:]), [b_ss[s2], b_eps], [b_ss[s2]])
                P.op("dve", lambda e, s2=s2: e.reciprocal(out=rstd[:, s2:s2 + 1], in_=rstd[:, s2:s2 + 1]),
                     [b_ss[s2]], [b_ss[s2]])
                STT(P, "dve", hin[s2][:], hin[s2][:], rstd[:, s2:s2 + 1], gF[:], ALU.mult, ALU.mult,
                    [b_hin[s2], b_ss[s2], b_gF], [b_hin[s2]])
                DMA(P, "pool", y_d[i * 128:(i + 1) * 128, :], hin[s2][:], b_yd, b_hin[s2])
        P.flush(final_bufs=[b_yd], scope="tail2")


N_PRE_UNITS = (SEQ - NTOK) // UNIT
N_OWN_UNITS = NTOK // UNIT
FIRST_KV_BLK = (SEQ - 2 * NTOK) // BLK
FIRST_Q_BLK = (SEQ - NTOK) // BLK

_IN_SPECS = [
    ("xpad", [SEQ, D], F32), ("wfm32", [NCT * 128, KT * 128], F32), ("wv32", [128, KT * 1024], F32),
    ("wglu32", [128, 8 * 1024], F32), ("wout32", [128, KT * D], F32), ("wgate32", [128, KT * DFF], F32),
    ("wup32", [128, KT * DFF], F32), ("wdown32", [128, NF * D], F32), ("g1", [128, KT], F32), ("g2", [128, KT], F32),
    ("final_g", [D], F32), ("a_re", [64, 64], F32), ("a_im", [64, 64], F32), ("log_dt", [64], F32),
    ("b_re", [64, 64, 16], F32), ("b_im", [64, 64, 16], F32), ("c_re", [64, 16, 64], F32), ("c_im", [64, 16, 64], F32),
    ("d_skip", [64, 16], F32), ("b_glu", [1024], F32), ("ident32", [128, 128], F32), ("ident_bf", [128, 128], BF16),
    ("aconsts", [128, 4, 128], BF16), ("hmask", [128, 1], F32), ("cos_d", [128, 3, 2 * NTOK], F32),
    ("sin_d", [128, 3, 2 * NTOK], F32),
]


def build_program():
    from contextlib import ExitStack
    nc = bass.Bass("TRN2", target_bir_lowering=False)
    ext = Buf("ext", False)
    dr = {"ext": ext}
    for name, shape, dt in _IN_SPECS:
        dr[name] = (nc.dram_tensor(name, shape, dt, kind="ExternalInput").ap(), ext)
    dr["y"] = (nc.dram_tensor("y", [NTOK, D], F32, kind="ExternalOutput").ap(), Buf("y", False))

    def scratch(name, shape, dt, keep=False):
        dr[name] = (nc.dram_tensor(name, shape, dt).ap(), Buf(name, False, keep))
    scratch("wfm_bf2", [NCT * 128, KT * 128], BF16); scratch("wv_bf2", [128, KT * 1024], BF16)
    scratch("wglu_bf2", [128, 8 * 1024], BF16); scratch("wout_bf2", [128, KT * D], BF16)
    scratch("wgate_bf2", [128, KT * DFF], BF16); scratch("wup_bf2", [128, KT * DFF], BF16)
    scratch("wdown_bf2", [128, NF * D], BF16)
    scratch("uT_d", [128, 8, SEQ], BF16); scratch("qT_d", [128, 8, NTOK], BF16); scratch("kT_d", [128, 8, 2 * NTOK], BF16)
    scratch("V_d", [2 * NTOK, 1024], BF16); scratch("mix_d", [128, 16, NTOK], BF16)
    scratch("h_d", [NTOK, D], F32); scratch("hn2T_d", [128, KT, NTOK], BF16)
    dr["wfm_bf"] = (dr["wfm_bf2"][0].rearrange("(c p) (k m) -> c p k m", p=128, k=KT), dr["wfm_bf2"][1])
    dr["wv_bf"] = (dr["wv_bf2"][0].rearrange("p (k n) -> p k n", k=KT), dr["wv_bf2"][1])
    dr["wglu_bf"] = (dr["wglu_bf2"][0].rearrange("p (k n) -> p k n", k=8), dr["wglu_bf2"][1])
    dr["wout_bf"] = (dr["wout_bf2"][0].rearrange("p (k n) -> p k n", k=KT), dr["wout_bf2"][1])
    dr["wgate_bf"] = (dr["wgate_bf2"][0].rearrange("p (k n) -> p k n", k=KT), dr["wgate_bf2"][1])
    dr["wup_bf"] = (dr["wup_bf2"][0].rearrange("p (k n) -> p k n", k=KT), dr["wup_bf2"][1])
    dr["wdown_bf"] = (dr["wdown_bf2"][0].rearrange("p (f n) -> p f n", f=NF), dr["wdown_bf2"][1])
    dr["ssm_d"] = (dr["mix_d"][0][:, 8:16, :], dr["mix_d"][1])
    with ExitStack() as st:
        P = Prog(nc, st)
        pairs = []
        r_u = CT_U * 128
        dr["wfmu_b"] = Buf("wfmu_bf", False)
        pairs.append((dr["wfm_bf2"][0][r_u:, :], dr["wfmu_b"], dr["wfm32"][0][r_u:, :], Buf("wfmu32", False, keep=True)))
        pairs.append((dr["wfm_bf2"][0][:r_u, :], dr["wfm_bf2"][1], dr["wfm32"][0][:r_u, :], Buf("wfm32", False, keep=True)))
        for nm in ("wv", "wglu", "wout", "wgate", "wup", "wdown"):
            pairs.append((dr[nm + "_bf2"][0], dr[nm + "_bf2"][1], dr[nm + "32"][0], Buf(nm + "32", False, keep=True)))
        cast_weights(nc, P, pairs)
        front_stage(nc, P, dr, NBLK_ALL, FIRST_KV_BLK, FIRST_Q_BLK)
        s5_stage(nc, P, dr, N_PRE_UNITS, N_OWN_UNITS)
        attn_stage(nc, P, dr)
        tail1_stage(nc, P, dr, SEQ - NTOK)
        tail2_stage(nc, P, dr)
    return nc


def _tile_rows(w, kt):
    n = w.shape[1]
    return np.ascontiguousarray(w.reshape(kt, 128, n).transpose(1, 0, 2)).reshape(128, kt * n)


def _head_perm(h):
    j = h % 3
    perm = np.zeros(128, np.int64)
    for m in range(128):
        if 32 * j <= m < 32 * j + 32:
            perm[m] = m - 32 * j
        elif m < 32 * j:
            perm[m] = 32 + m
        else:
            perm[m] = m
    return perm


def _prep_shared(inp):
    f32 = np.float32
    w_in = np.asarray(inp["w_in"], f32)[0]
    cols = np.full((NCT, 128), -1, np.int64)
    for base, ct0, ctsw in ((0, CT_Q, CT_QSW), (1024, CT_K, CT_KSW)):
        for h in range(8):
            cols[ct0 + h] = base + h * 128 + _head_perm(h)
            tt, j = h // 3, h % 3
            for i in range(32):
                cols[ctsw + tt, 32 * j + i] = base + h * 128 + (i + 16) % 32
    for k in range(8):
        cols[CT_U + k] = 3072 + k * 128 + np.arange(128)
    flat = cols.reshape(-1)
    wsel = np.where(flat[None, :] >= 0, w_in[:, np.maximum(flat, 0)], 0.0).astype(f32)
    wfm = wsel.reshape(KT, 128, NCT, 128).transpose(2, 1, 0, 3)
    sh = {}
    sh["wfm32"] = np.ascontiguousarray(wfm).reshape(NCT * 128, KT * 128)
    sh["wv32"] = _tile_rows(np.ascontiguousarray(w_in[:, 2048:3072]), KT)
    sh["wglu32"] = _tile_rows(np.asarray(inp["w_glu"], f32)[0], 8)
    sh["wout32"] = _tile_rows(np.asarray(inp["w_out"], f32)[0], KT)
    sh["wgate32"] = _tile_rows(np.asarray(inp["w_gate"], f32)[0], KT)
    sh["wup32"] = _tile_rows(np.asarray(inp["w_up"], f32)[0], KT)
    sh["wdown32"] = _tile_rows(np.asarray(inp["w_down"], f32)[0], NF)
    sh["g1"] = np.ascontiguousarray(np.asarray(inp["norm1_g"], f32)[0].reshape(KT, 128).T)
    sh["g2"] = np.ascontiguousarray(np.asarray(inp["norm2_g"], f32)[0].reshape(KT, 128).T)
    sh["final_g"] = np.ascontiguousarray(np.asarray(inp["final_g"], f32))
    for nm in ("a_re", "a_im", "log_dt", "b_re", "b_im", "c_re", "c_im", "d_skip", "b_glu"):
        sh[nm] = np.ascontiguousarray(np.asarray(inp[nm], f32)[0])
    sh["ident32"] = np.eye(128, dtype=f32)
    sh["ident_bf"] = np.eye(128, dtype=f32).astype(ml_dtypes.bfloat16)
    kk = np.arange(128)[:, None]; qq = np.arange(128)[None, :]
    mcur = np.where(kk <= qq, 0.0, -30000.0); mprev = np.where(kk >= qq, 0.0, -30000.0)
    sh["aconsts"] = np.ascontiguousarray(
        np.stack([np.eye(128), mcur, mprev, np.ones((128, 128))], 1).astype(f32).astype(ml_dtypes.bfloat16))
    return sh


def _rope_tables(t0):
    f32 = np.float32
    pos = (np.arange(2 * NTOK) + (t0 - NTOK)).astype(f32)
    pos = np.maximum(pos, f32(0))
    inv_freq = (f32(500000.0) ** (-(np.arange(0, 32, 2).astype(f32)) / f32(32))).astype(f32)
    i = np.arange(32)
    ang = (pos[None, :] * inv_freq[i % 16][:, None]).astype(f32)
    c32 = np.cos(ang).astype(f32)
    s32 = np.sin(ang).astype(f32) * np.where(i < 16, -1.0, 1.0).astype(f32)[:, None]
    cosT = np.ones((128, 3, 2 * NTOK), f32)
    sinT = np.zeros((128, 3, 2 * NTOK), f32)
    for j in range(3):
        cosT[32 * j:32 * j + 32, j, :] = c32
        sinT[32 * j:32 * j + 32, j, :] = s32
    return cosT, sinT


def kernel(**inputs):
    x = np.asarray(inputs["x"], np.float32)[0]
    sh = _prep_shared(inputs)
    nc = build_program()
    in_maps = []
    for c in range(NCORES):
        t0 = c * NTOK
        xpad = np.zeros((SEQ, D), np.float32)
        n_real = t0 + NTOK
        xpad[SEQ - n_real:] = x[:n_real]
        cosT, sinT = _rope_tables(t0)
        m = dict(sh)
        m["xpad"] = xpad
        m["cos_d"] = cosT
        m["sin_d"] = sinT
        m["hmask"] = np.full((128, 1), -30000.0 if c == 0 else 0.0, np.float32)
        in_maps.append(m)
    res = run_bass_kernel_spmd(nc, in_maps, core_ids=list(range(NCORES)))
    y = np.concatenate([np.asarray(res.results[c]["y"], np.float32) for c in range(NCORES)], axis=0)
    return y.reshape(1, SEQ, D)
```
